# Optimizing a Trainium2 kernel written in Bass

```python
import jax, jax.numpy as jnp
from jax import lax
import numpy as np

D_MODEL = 2048
BATCH = 4
SEQ = 8192
DEPTH = 4
DEC_BATCH = 8
DEC_SEQ = 4096
PAST_LEN = 128

N_META = 16
D_FF = 4 * D_MODEL
NORM_EPS = 1e-6
ROPE_BASE = 10000.0
N_RET_LAYERS = (DEPTH + 1) // 2
N_MLA_LAYERS = DEPTH // 2
RET_HEADS = 8
RET_DK = 256
RET_DV = 512
RET_CHUNK = 128
MLA_HEADS = 16
MLA_Q_LORA = 512
MLA_KV_LORA = 512
MLA_NOPE = 128
MLA_ROPE = 64
MLA_V = 128
MLA_QBLOCK = 128

kernel_name = "hybrid_retention_mla_encoder"


def rms_norm(x, g):
    xf = x.astype(jnp.float32)
    y = xf * lax.rsqrt(jnp.mean(xf * xf, axis=-1, keepdims=True) + NORM_EPS)
    return (y * g.astype(jnp.float32)).astype(x.dtype)


def rope_tables(n, dim):
    inv = 1.0 / (ROPE_BASE ** (jnp.arange(0, dim, 2, dtype=jnp.float32) / dim))
    ang = jnp.arange(n, dtype=jnp.float32)[:, None] * inv[None, :]
    return jnp.cos(ang), jnp.sin(ang)


def apply_rope(x, cos, sin):
    half = x.shape[-1] // 2
    x1, x2 = x[..., :half], x[..., half:]
    c = cos[None, :, None, :].astype(x.dtype)
    s = sin[None, :, None, :].astype(x.dtype)
    return jnp.concatenate([x1 * c - x2 * s, x1 * s + x2 * c], axis=-1)


def retention_decays(log_g, backward):
    i = jnp.arange(RET_CHUNK, dtype=jnp.float32)
    diff = i[:, None] - i[None, :]
    lg = log_g[:, None, None]
    pw = jnp.exp(lg * jnp.abs(diff)[None])
    if backward:
        intra = jnp.where(diff[None] < 0, pw, 0.0)
        q_dec = jnp.exp(log_g[:, None] * (RET_CHUNK - i)[None])
        k_dec = jnp.exp(log_g[:, None] * i[None])
    else:
        intra = jnp.where(diff[None] >= 0, pw, 0.0)
        q_dec = jnp.exp(log_g[:, None] * (i + 1.0)[None])
        k_dec = jnp.exp(log_g[:, None] * (RET_CHUNK - 1.0 - i)[None])
    c_dec = jnp.exp(log_g * RET_CHUNK)
    return intra, q_dec, k_dec, c_dec


def retention(h, cos, sin, wq, wk, wv, wg, wo):
    B, L, _ = h.shape
    q = apply_rope((h @ wq).reshape(B, L, RET_HEADS, RET_DK), cos, sin)
    k = apply_rope((h @ wk).reshape(B, L, RET_HEADS, RET_DK), cos, sin) * (RET_DK ** -0.5)
    v = (h @ wv).reshape(B, L, RET_HEADS, RET_DV)
    pad = RET_CHUNK - N_META

    def to_chunks(t):
        t = jnp.pad(t, ((0, 0), (pad, 0), (0, 0), (0, 0)))
        n_c = t.shape[1] // RET_CHUNK
        t = t.reshape(B, n_c, RET_CHUNK, RET_HEADS, t.shape[-1])
        return jnp.transpose(t, (1, 0, 3, 2, 4))

    qc, kc, vc = to_chunks(q), to_chunks(k), to_chunks(v)
    hh = jnp.arange(RET_HEADS, dtype=jnp.float32)
    log_g_fwd = jnp.log(1.0 - 2.0 ** (-5.0 - hh))
    log_g_bwd = jnp.log(1.0 - 2.0 ** (-5.5 - hh))
    D_f, qd_f, kd_f, cd_f = retention_decays(log_g_fwd, backward=False)
    D_b, qd_b, kd_b, cd_b = retention_decays(log_g_bwd, backward=True)

    def make_step(D, qd, kd, cd):
        def step(R, xs):
            q_, k_, v_ = xs
            s = jnp.einsum('bhid,bhjd->bhij', q_, k_) * D[None]
            o = (jnp.einsum('bhij,bhjv->bhiv', s, v_)
                 + jnp.einsum('bhid,bhdv->bhiv', q_ * qd[None, :, :, None], R))
            R = cd[None, :, None, None] * R + jnp.einsum('bhjd,bhjv->bhdv', k_ * kd[None, :, :, None], v_)
            return R, o
        return step

    R0 = jnp.zeros((B, RET_HEADS, RET_DK, RET_DV), jnp.float32)
    _, o_f = lax.scan(make_step(D_f, qd_f, kd_f, cd_f), R0, (qc, kc, vc))
    _, o_b = lax.scan(make_step(D_b, qd_b, kd_b, cd_b), R0, (qc, kc, vc), reverse=True)
    o = jnp.transpose(o_f + o_b, (1, 0, 3, 2, 4))
    o = o.reshape(B, -1, RET_HEADS, RET_DV)[:, pad:].astype(jnp.float32)
    mu = jnp.mean(o, axis=-1, keepdims=True)
    var = jnp.mean(jnp.square(o - mu), axis=-1, keepdims=True)
    o = ((o - mu) * lax.rsqrt(var + NORM_EPS)).astype(h.dtype).reshape(B, L, RET_HEADS * RET_DV)
    return (jax.nn.silu(h @ wg) * o) @ wo


def mla(h, cos, sin, wq_a, q_norm, wq_b, wkv_a, kv_norm, wkv_b, wo):
    B, L, _ = h.shape
    cq = rms_norm(h @ wq_a, q_norm)
    q = (cq @ wq_b).reshape(B, L, MLA_HEADS, MLA_NOPE + MLA_ROPE)
    q_nope = q[..., :MLA_NOPE]
    q_rope = apply_rope(q[..., MLA_NOPE:], cos, sin)
    kv_a = h @ wkv_a
    ckv = rms_norm(kv_a[..., :MLA_KV_LORA], kv_norm)
    k_rope = apply_rope(kv_a[..., None, MLA_KV_LORA:], cos, sin)[:, :, 0]
    kv = (ckv @ wkv_b).reshape(B, L, MLA_HEADS, MLA_NOPE + MLA_V)
    k_nope, v = kv[..., :MLA_NOPE], kv[..., MLA_NOPE:]
    scale = (MLA_NOPE + MLA_ROPE) ** -0.5

    def attend(qn, qr):
        s = (jnp.einsum('bqhd,bkhd->bhqk', qn, k_nope)
             + jnp.einsum('bqhd,bkd->bhqk', qr, k_rope))
        p = jax.nn.softmax(s.astype(jnp.float32) * scale, axis=-1)
        return jnp.einsum('bhqk,bkhd->bqhd', p.astype(v.dtype), v)

    o_meta = attend(q_nope[:, :N_META], q_rope[:, :N_META])
    n_real = L - N_META
    n_blk = n_real // MLA_QBLOCK

    def blocks(t):
        return jnp.moveaxis(t[:, N_META:].reshape(B, n_blk, MLA_QBLOCK, MLA_HEADS, t.shape[-1]), 1, 0)

    o_real = lax.map(lambda a: attend(a[0], a[1]), (blocks(q_nope), blocks(q_rope)))
    o_real = jnp.moveaxis(o_real, 0, 1).reshape(B, n_real, MLA_HEADS, MLA_V)
    o = jnp.concatenate([o_meta, o_real], axis=1).reshape(B, L, MLA_HEADS * MLA_V)
    return o @ wo


def sq_relu_mlp(h, w1, w2):
    return jnp.square(jax.nn.relu(h @ w1)) @ w2


def trunk(x, meta_tokens, norm1_g, norm2_g, mlp_w1, mlp_w2,
          ret_wq, ret_wk, ret_wv, ret_wg, ret_wo,
          mla_wq_a, mla_q_norm, mla_wq_b, mla_wkv_a, mla_kv_norm, mla_wkv_b, mla_wo,
          final_norm):
    B, S, D = x.shape
    meta = jnp.broadcast_to(meta_tokens.astype(x.dtype)[None], (B, N_META, D))
    h = jnp.concatenate([meta, x], axis=1)
    L = S + N_META
    cos_r, sin_r = rope_tables(L, RET_DK)
    cos_m, sin_m = rope_tables(L, MLA_ROPE)
    for i in range(DEPTH):
        a = rms_norm(h, norm1_g[i])
        j = i // 2
        if i % 2 == 0:
            h = h + retention(a, cos_r, sin_r, ret_wq[j], ret_wk[j], ret_wv[j], ret_wg[j], ret_wo[j])
        else:
            h = h + mla(a, cos_m, sin_m, mla_wq_a[j], mla_q_norm[j], mla_wq_b[j],
                        mla_wkv_a[j], mla_kv_norm[j], mla_wkv_b[j], mla_wo[j])
        a = rms_norm(h, norm2_g[i])
        h = h + sq_relu_mlp(a, mlp_w1[i], mlp_w2[i])
    return rms_norm(h, final_norm)[:, N_META:]


def setup_inputs(seed: int = 0) -> dict:
    key = jax.random.key(seed)
    ks = jax.random.split(key, 24)
    f32 = jnp.float32

    def w(k, shape, fan_in):
        return jax.random.normal(k, shape, f32) * (fan_in ** -0.5)

    def gain(k, shape):
        return 1.0 + 0.02 * jax.random.normal(k, shape, f32)

    D = D_MODEL
    NR, NM = N_RET_LAYERS, N_MLA_LAYERS
    return {
        "x_prompt": jax.random.normal(ks[0], (BATCH, SEQ, D), f32),
        "x_sample": jax.random.normal(ks[1], (DEC_BATCH, DEC_SEQ, D), f32),
        "meta_tokens": jax.random.normal(ks[2], (N_META, D), f32),
        "norm1_g": gain(ks[3], (DEPTH, D)),
        "norm2_g": gain(ks[4], (DEPTH, D)),
        "mlp_w1": w(ks[5], (DEPTH, D, D_FF), D),
        "mlp_w2": w(ks[6], (DEPTH, D_FF, D), D_FF),
        "ret_wq": w(ks[7], (NR, D, RET_HEADS * RET_DK), D),
        "ret_wk": w(ks[8], (NR, D, RET_HEADS * RET_DK), D),
        "ret_wv": w(ks[9], (NR, D, RET_HEADS * RET_DV), D),
        "ret_wg": w(ks[10], (NR, D, RET_HEADS * RET_DV), D),
        "ret_wo": w(ks[11], (NR, RET_HEADS * RET_DV, D), RET_HEADS * RET_DV),
        "mla_wq_a": w(ks[12], (NM, D, MLA_Q_LORA), D),
        "mla_q_norm": gain(ks[13], (NM, MLA_Q_LORA)),
        "mla_wq_b": w(ks[14], (NM, MLA_Q_LORA, MLA_HEADS * (MLA_NOPE + MLA_ROPE)), MLA_Q_LORA),
        "mla_wkv_a": w(ks[15], (NM, D, MLA_KV_LORA + MLA_ROPE), D),
        "mla_kv_norm": gain(ks[16], (NM, MLA_KV_LORA)),
        "mla_wkv_b": w(ks[17], (NM, MLA_KV_LORA, MLA_HEADS * (MLA_NOPE + MLA_V)), MLA_KV_LORA),
        "mla_wo": w(ks[18], (NM, MLA_HEADS * MLA_V, D), MLA_HEADS * MLA_V),
        "final_norm": gain(ks[19], (D,)),
    }


def reference(x_prompt, x_sample, meta_tokens, norm1_g, norm2_g, mlp_w1, mlp_w2,
              ret_wq, ret_wk, ret_wv, ret_wg, ret_wo,
              mla_wq_a, mla_q_norm, mla_wq_b, mla_wkv_a, mla_kv_norm, mla_wkv_b, mla_wo,
              final_norm):
    y_prompt = trunk(x_prompt, meta_tokens, norm1_g, norm2_g, mlp_w1, mlp_w2,
                     ret_wq, ret_wk, ret_wv, ret_wg, ret_wo,
                     mla_wq_a, mla_q_norm, mla_wq_b, mla_wkv_a, mla_kv_norm, mla_wkv_b, mla_wo,
                     final_norm)
    y_sample = trunk(x_sample, meta_tokens, norm1_g, norm2_g, mlp_w1, mlp_w2,
                     ret_wq, ret_wk, ret_wv, ret_wg, ret_wo,
                     mla_wq_a, mla_q_norm, mla_wq_b, mla_wkv_a, mla_kv_norm, mla_wkv_b, mla_wo,
                     final_norm)
    return (y_prompt, y_sample)
```

```python
import numpy as np
import ml_dtypes
from contextlib import ExitStack
import concourse.bass as bass
import concourse.mybir as mybir
from concourse.bass_utils import run_bass_kernel_spmd

F32 = mybir.dt.float32
BF16 = mybir.dt.bfloat16
AF = mybir.ActivationFunctionType
ALU = mybir.AluOpType

D = 2048
DFF = 8192
NMETA = 16
EPS = 1e-6
RH, RDK, RDV = 8, 256, 512
MH, MQL, MKL, MNOPE, MROPE, MV = 16, 512, 512, 128, 64, 128
NEG = -30000.0


class DSem:
    def __init__(self, handle, key):
        self.handle, self.key, self.cnt = handle, key, 0


class Buf:
    def __init__(self, t, name):
        self.t, self.name = t, name
        self.w, self.r = {}, {}
        self.dsem = None

    def __getitem__(self, k):
        return self.t[k]


class KB:
    def __init__(self, nc, es, n_dsem=48):
        self.nc, self.es = nc, es
        self.E = dict(pe=nc.tensor, act=nc.scalar, dve=nc.vector, pool=nc.gpsimd, sp=nc.sync)
        self.sem = {e: es.enter_context(nc.semaphore("s_" + e)) for e in self.E}
        self.semobj = dict(self.sem)
        self.cnt = {e: 0 for e in self.E}
        self.seen = {e: {} for e in self.E}
        self.dsems = []
        for i in range(n_dsem):
            d = DSem(es.enter_context(nc.semaphore("d%d" % i)), "d%d" % i)
            self.dsems.append(d)
            self.semobj[d.key] = d.handle
        self.dfree = list(self.dsems)
        self.dmap = {d.key: d for d in self.dsems}
        self.nwait = 0

    def scope(self):
        return Scope(self)

    def _wait(self, e, key, v):
        d = self.dmap.get(key)
        if d is not None:
            v = max(v, d.cnt)
        if self.seen[e].get(key, 0) >= v:
            return
        self.E[e].wait_ge(self.semobj[key], v)
        self.seen[e][key] = v
        self.nwait += 1

    def _pre(self, e, reads, writes):
        for b in reads:
            for key, v in b.w.items():
                self._wait(e, key, v)
        for b in writes:
            for key, v in b.w.items():
                if key != e:
                    self._wait(e, key, v)
            for key, v in b.r.items():
                if key != e:
                    self._wait(e, key, v)

    def op(self, e, fn, reads=(), writes=()):
        self._pre(e, reads, writes)
        ins = fn(self.E[e])
        self.cnt[e] += 1
        v = self.cnt[e]
        ins.then_inc(self.sem[e], 1)
        for b in reads:
            b.r[e] = v
        for b in writes:
            b.w[e] = v
            b.r = {}
        return ins

    def dma(self, q, out, in_, sbuf, reads=(), writes=()):
        self._pre(q, reads, writes)
        ds = sbuf.dsem
        ins = self.E[q].dma_start(out=out, in_=in_)
        ds.cnt += 16
        ins.then_inc(ds.handle, 16)
        for b in reads:
            b.r[ds.key] = ds.cnt
        for b in writes:
            b.w[ds.key] = ds.cnt
            b.r = {}

    def barrier(self):
        for e in self.E:
            for e2 in self.E:
                if e2 != e and self.cnt[e2]:
                    self._wait(e, e2, self.cnt[e2])
            for d in self.dsems:
                if d.cnt:
                    self._wait(e, d.key, d.cnt)


class Scope:
    _n = 0

    def __init__(self, kb):
        self.kb = kb
        self.es = ExitStack()
        self.held = []
        Scope._n += 1
        self.sfx = "_s%d" % Scope._n

    def __enter__(self):
        self.es.__enter__()
        return self

    def __exit__(self, *a):
        self.kb.barrier()
        for d in self.held:
            self.kb.dfree.append(d)
        return self.es.__exit__(*a)

    def sb(self, name, shape, dt, dma=False):
        b = Buf(self.es.enter_context(self.kb.nc.sbuf_tensor(name + self.sfx, list(shape), dt)), name)
        if dma:
            b.dsem = self.kb.dfree.pop()
            self.held.append(b.dsem)
        return b

    def ps(self, name, shape, dt):
        return Buf(self.es.enter_context(self.kb.nc.psum_tensor(name + self.sfx, list(shape), dt)), name)

    def ring(self, name, n, shape, dt, dma=False):
        return [self.sb("%s%d" % (name, i), shape, dt, dma) for i in range(n)]

    def psring(self, name, n, shape, dt):
        return [self.ps("%s%d" % (name, i), shape, dt) for i in range(n)]


class Prog:
    def __init__(self, NT, depth, debug=False):
        self.NT, self.depth = NT, depth
        self.TT = 2 * NT + 2
        self.TOK = self.TT * 128
        self.debug = debug
        self.groups = [(t, min(4, 2 * NT - t)) for t in range(0, 2 * NT, 4)] + [(2 * NT, 2)]
        self.qgroups = []
        for s in range(2):
            for t in range(s * NT, (s + 1) * NT, 4):
                self.qgroups.append((t, min(4, (s + 1) * NT - t), s))
        self.qgroups += [(2 * NT, 1, 0), (2 * NT + 1, 1, 1)]
        self.fwd = [2 * NT] + list(range(NT)) + [2 * NT + 1] + list(range(NT, 2 * NT))

    def seg_of_tile(self, t):
        NT = self.NT
        if t < NT:
            return 0
        if t < 2 * NT:
            return 1
        return t - 2 * NT

    def build(self):
        nc = bass.Bass("TRN2", target_bir_lowering=False)
        self.nc = nc
        TOK, TT = self.TOK, self.TT
        NR, NM = (self.depth + 1) // 2, self.depth // 2
        self.NR, self.NM = NR, NM

        def inp(name, shape, dt=F32):
            return nc.dram_tensor(name, list(shape), dt, kind="ExternalInput").ap()

        def scr(name, shape, dt):
            kind = "ExternalOutput" if (self.debug and name in self.debug) else "Internal"
            return nc.dram_tensor(name, list(shape), dt, kind=kind).ap()

        dp = self.depth
        I = self.I = {}
        I["x_in"] = inp("x_in", [TOK, D])
        I["norm1_g"] = inp("norm1_g", [dp, D])
        I["norm2_g"] = inp("norm2_g", [dp, D])
        I["final_norm"] = inp("final_norm", [1, D])
        wshapes = dict(
            mlp_w1=[dp, D, DFF], mlp_w2=[dp, DFF, D],
            ret_wq=[NR, D, RH * RDK], ret_wk=[NR, D, RH * RDK], ret_wv=[NR, D, RH * RDV],
            ret_wg=[NR, D, RH * RDV], ret_wo=[NR, RH * RDV, D],
            mla_wq_a=[NM, D, MQL], mla_wq_b=[NM, MQL, MH * (MNOPE + MROPE)],
            mla_wkv_a=[NM, D, MKL + MROPE], mla_wkv_b=[NM, MKL, MH * (MNOPE + MV)],
            mla_wo=[NM, MH * MV, D])
        self.W = {}
        for n, s in wshapes.items():
            if s[0] == 0:
                continue
            I[n] = inp(n, s)
            self.W[n] = scr("wb_" + n, s, BF16)
        if NM:
            I["mla_q_norm"] = inp("mla_q_norm", [NM, MQL])
            I["mla_kv_norm"] = inp("mla_kv_norm", [NM, MKL])
        for n, s in dict(ident=[128, 128], cosRT=[128, TOK], sinRT=[128, TOK],
                         cosM=[TOK, 256], sinM=[TOK, 256], DT=[128, RH * 128],
                         qdf=[128, RH * 256], qdb=[128, RH * 256],
                         kdF=[128, TT * RH], kdB=[128, TT * RH],
                         cdf=[128, TT * RH], cdb=[128, TT * RH],
                         kbias=[128, TT * 2], valid=[128, TT]).items():
            I[n] = inp(n, s)
        self.out = nc.dram_tensor("out", [2 * self.NT * 128, D], F32, kind="ExternalOutput").ap()
        S = self.S = {}
        S["h"] = scr("h", [TOK, D], F32)
        S["qT"] = scr("qT", [RH, 128, 2, TOK], BF16)
        S["kT"] = scr("kT", [RH, 128, 2, TOK], BF16)
        S["v"] = scr("v", [TOK, RH * RDV], BF16)
        S["g"] = scr("g", [TOK, RH * RDV], BF16)
        S["ob"] = scr("ob", [TOK, RH * RDV], F32)
        S["ogT"] = scr("ogT", [32, 128, TOK], BF16)
        S["qnT"] = scr("qnT", [MH, 128, TOK], BF16)
        S["qrT"] = scr("qrT", [MH // 2, 128, TOK], BF16)
        S["knT"] = scr("knT", [MH, 128, TOK], BF16)
        S["krT"] = scr("krT", [64, TOK], BF16)
        S["vm"] = scr("vm", [TOK, MH * MV], BF16)

        with ExitStack() as es:
            kb = self.kb = KB(nc, es)
            self.prologue()
            for l in range(self.depth):
                src = I["x_in"] if l == 0 else S["h"]
                last = (l == self.depth - 1)
                if l % 2 == 0:
                    self.phase_R1(l, src)
                    self.phase_R2(l)
                    self.phase_out(l, src, "ret", last)
                else:
                    self.phase_M1(l, src)
                    self.phase_M2(l)
                    self.phase_out(l, src, "mla", last)
            kb.barrier()
        return nc

    def prologue(self):
        kb, nc = self.kb, self.nc

        def convert(names_layers, tracker):
            for n, l in names_layers:
                dst, src = self.W[n], self.I[n]
                L, R, C = src.shape
                step = max(1, (1 << 21) // C)
                for r0 in range(0, R, step):
                    r1 = min(R, r0 + step)
                    kb.dma("pool", dst[l, r0:r1, :], src[l, r0:r1, :], tracker)

        first = [(n, 0) for n in ("ret_wq", "ret_wk", "ret_wv", "ret_wg")]
        with kb.scope() as sc:
            dummy = sc.sb("cvt_dummy", [128, 8], F32, dma=True)
            convert(first, dummy)
        bg = Buf(None, "cvt_bg")
        bg.dsem = kb.dfree.pop()
        rest = [(n, l) for n in self.W for l in range(self.I[n].shape[0]) if (n, l) not in first]
        order = ["ret_wo", "mlp_w1", "mlp_w2", "mla_wq_a", "mla_wkv_a", "mla_wq_b", "mla_wkv_b", "mla_wo",
                 "ret_wq", "ret_wk", "ret_wv", "ret_wg"]
        rest.sort(key=lambda x: (x[1], order.index(x[0])))
        convert(rest, bg)

    def load_ident(self, sc):
        kb = self.kb
        idf = sc.sb("idf", [128, 128], F32, dma=True)
        idb = sc.sb("idb", [128, 128], BF16)
        kb.dma("sp", idf[:], self.I["ident"][:, :], idf, writes=[idf])
        kb.op("dve", lambda e: e.tensor_copy(out=idb[:], in_=idf[:]), reads=[idf], writes=[idb])
        return idb

    def load_bcast(self, sc, name, row_ap, n):
        kb = self.kb
        b = sc.sb(name, [128, n], F32, dma=True)
        kb.dma("sp", b[:], row_ap.partition_broadcast(128), b, writes=[b])
        return b

    def load_const(self, sc, name, ap, shape, dt=F32):
        kb = self.kb
        b = sc.sb(name, shape, dt, dma=True)
        kb.dma("sp", b[:], ap, b, writes=[b])
        return b

    def rmsnorm_tile(self, sc_bufs, x, xap, gbc, out, outap, n):
        kb = self.kb
        junk, ssq, rstd = sc_bufs
        kb.op("act", lambda e: e.activation(out=junk[:, 0:n], in_=xap, func=AF.Square,
                                             accum_out=ssq[:, 0:1]),
              reads=[x], writes=[junk, ssq])
        kb.op("dve", lambda e: e.tensor_scalar(out=rstd[:, 0:1], in0=ssq[:, 0:1], scalar1=1.0 / n,
                                               scalar2=EPS, op0=ALU.mult, op1=ALU.add),
              reads=[ssq], writes=[rstd])
        kb.op("act", lambda e: e.sqrt(out=rstd[:, 0:1], in_=rstd[:, 0:1]), reads=[rstd], writes=[rstd])
        kb.op("dve", lambda e: e.reciprocal(out=rstd[:, 0:1], in_=rstd[:, 0:1]), reads=[rstd], writes=[rstd])
        kb.op("dve", lambda e: e.scalar_tensor_tensor(out=outap, in0=xap, scalar=rstd[:, 0:1],
                                                      in1=gbc[:, 0:n], op0=ALU.mult, op1=ALU.mult),
              reads=[x, rstd, gbc], writes=[out])

    def transpose_blocks(self, src, src_ap_fn, nblk, idb, pst_ring, dst, dst_ap_fn, ctr, evac=None):
        kb = self.kb
        i = 0
        while i < nblk:
            n = min(8, nblk - i)
            ps = pst_ring[ctr[0] % len(pst_ring)]
            for j in range(n):
                kb.op("pe", lambda e, j=j: e.transpose(ps[:, j * 128:(j + 1) * 128], src_ap_fn(i + j), idb[:]),
                      reads=[src, idb], writes=[ps])
            eng = ("act", "dve")[ctr[0] % 2] if evac is None else evac
            oap = dst_ap_fn(i, n)
            iap = ps[:, 0:n * 128].rearrange("p (a b) -> p a b", b=128)
            if eng == "act":
                kb.op("act", lambda e: e.copy(out=oap, in_=iap), reads=[ps], writes=[dst])
            else:
                kb.op("dve", lambda e: e.tensor_copy(out=oap, in_=iap), reads=[ps], writes=[dst])
            ctr[0] += 1
            i += n

    def phase_R1(self, l, src):
        kb, I, S = self.kb, self.I, self.S
        j = l // 2
        Wq, Wk, Wv, Wg = (self.W[n][j] for n in ("ret_wq", "ret_wk", "ret_wv", "ret_wg"))
        with kb.scope() as sc:
            idb = self.load_ident(sc)
            gbc = self.load_bcast(sc, "gbc", I["norm1_g"][l, :], D)
            xr = sc.ring("xr", 2, [128, D], F32, dma=True)
            atm = sc.ring("atm", 2, [128, D], BF16)
            junk = sc.sb("junk", [128, D], F32)
            ssq = sc.sb("ssq", [128, 1], F32)
            rstd = sc.sb("rstd", [128, 1], F32)
            aT = sc.sb("aT", [128, 16, 512], BF16)
            cs = sc.ring("cs", 2, [128, 2, 512], F32, dma=True)
            wqk = sc.ring("wqk", 2, [128, 16, 256], BF16, dma=True)
            wvg = sc.ring("wvg", 2, [128, 16, 512], BF16, dma=True)
            tmp = sc.ring("rt", 2, [128, 4, 512], F32)
            qko = sc.ring("qko", 2, [128, 2, 512], BF16, dma=True)
            vgo = sc.ring("vgo", 2, [128, 4, 512], BF16, dma=True)
            pst = sc.psring("pst", 2, [128, 1024], BF16)
            psqk = sc.psring("psqk", 4, [128, 512], F32)
            psvg = sc.psring("psvg", 2, [128, 512], F32)
            ctr = [0]
            nx = nqk = nvg = nps = 0
            for gi, (t0, nt) in enumerate(self.groups):
                N = nt * 128
                c0 = t0 * 128
                csb = cs[gi % 2]
                kb.dma("sp", csb[:, 0, 0:N], I["cosRT"][:, c0:c0 + N], csb, writes=[csb])
                kb.dma("sp", csb[:, 1, 0:N], I["sinRT"][:, c0:c0 + N], csb, writes=[csb])
                for t in range(nt):
                    x = xr[nx % 2]
                    a = atm[nx % 2]
                    nx += 1
                    r0 = (t0 + t) * 128
                    kb.dma("sp", x[:], src[r0:r0 + 128, :], x, writes=[x])
                    self.rmsnorm_tile((junk, ssq, rstd), x, x[:], gbc, a, a[:], D)
                    self.transpose_blocks(a, lambda i: a[:, i * 128:(i + 1) * 128], 16, idb, pst, aT,
                                          lambda i0, n: aT[:, i0:i0 + n, t * 128:(t + 1) * 128], ctr)
                for which, Wm, dst in (("q", Wq, S["qT"]), ("k", Wk, S["kT"])):
                    for h in range(RH):
                        w = wqk[nqk % 2]
                        kb.dma("sp", w[:], Wm[:, h * 256:(h + 1) * 256].rearrange("(k p) c -> p k c", p=128),
                               w, writes=[w])
                        p1 = psqk[(nqk % 2) * 2]
                        p2 = psqk[(nqk % 2) * 2 + 1]
                        for c, pp in ((0, p1), (1, p2)):
                            for kc in range(16):
                                kb.op("pe", lambda e, c=c, pp=pp, kc=kc: e.matmul(
                                    pp[:, 0:N], w[:, kc, c * 128:(c + 1) * 128], aT[:, kc, 0:N],
                                    start=(kc == 0), stop=(kc == 15)), reads=[w, aT], writes=[pp])
                        tb = tmp[nqk % 2]
                        ob = qko[nqk % 2]
                        nqk += 1
                        for ti, (pp, ci) in enumerate(((p1, 0), (p2, 1), (p1, 1), (p2, 0))):
                            kb.op("dve", lambda e, ti=ti, pp=pp, ci=ci: e.tensor_tensor(
                                out=tb[:, ti, 0:N], in0=pp[:, 0:N], in1=csb[:, ci, 0:N], op=ALU.mult),
                                reads=[pp, csb], writes=[tb])
                        kb.op("pool", lambda e: e.tensor_tensor(out=ob[:, 0, 0:N], in0=tb[:, 0, 0:N],
                                                                in1=tb[:, 1, 0:N], op=ALU.subtract),
                              reads=[tb], writes=[ob])
                        kb.op("pool", lambda e: e.tensor_tensor(out=ob[:, 1, 0:N], in0=tb[:, 2, 0:N],
                                                                in1=tb[:, 3, 0:N], op=ALU.add),
                              reads=[tb], writes=[ob])
                        kb.dma("act", dst[h, :, :, c0:c0 + N], ob[:, :, 0:N], ob, reads=[ob])
                for which, Wm, dst in (("v", Wv, S["v"]), ("g", Wg, S["g"])):
                    for cg in range(8):
                        w = wvg[nvg % 2]
                        ob = vgo[nvg % 2]
                        nvg += 1
                        kb.dma("sp", w[:], Wm[:, cg * 512:(cg + 1) * 512].rearrange("(k p) c -> p k c", p=128),
                               w, writes=[w])
                        for t in range(nt):
                            pp = psvg[nps % 2]
                            nps += 1
                            for kc in range(16):
                                kb.op("pe", lambda e, pp=pp, kc=kc, t=t: e.matmul(
                                    pp[:], aT[:, kc, t * 128:(t + 1) * 128], w[:, kc, :],
                                    start=(kc == 0), stop=(kc == 15)), reads=[w, aT], writes=[pp])
                            fn = AF.Copy if which == "v" else AF.Silu
                            kb.op("act", lambda e, pp=pp, t=t, fn=fn: e.activation(out=ob[:, t, :], in_=pp[:], func=fn),
                                  reads=[pp], writes=[ob])
                        kb.dma("act", dst[c0:c0 + N, cg * 512:(cg + 1) * 512].rearrange("(t p) c -> p t c", p=128),
                               ob[:, 0:nt, :], ob, reads=[ob])

    def phase_R2(self, l):
        kb, I, S = self.kb, self.I, self.S
        TT = self.TT
        NH = 2
        with kb.scope() as sc:
            idb = self.load_ident(sc)
            DT = self.load_const(sc, "DTc", I["DT"][:, :], [128, RH * 128])
            qdf = self.load_const(sc, "qdfc", I["qdf"][:, :], [128, RH * 256])
            qdb = self.load_const(sc, "qdbc", I["qdb"][:, :], [128, RH * 256])
            kdF = self.load_const(sc, "kdFc", I["kdF"][:, :], [128, TT * RH])
            kdB = self.load_const(sc, "kdBc", I["kdB"][:, :], [128, TT * RH])
            cdf = self.load_const(sc, "cdfc", I["cdf"][:, :], [128, TT * RH])
            cdb = self.load_const(sc, "cdbc", I["cdb"][:, :], [128, TT * RH])
            NRG = 6
            qc = sc.ring("qc", NRG, [128, 2, 128], BF16, dma=True)
            kc_ = sc.ring("kc", NRG, [128, 2, 128], BF16, dma=True)
            vc = sc.ring("vc", NRG, [128, 512], BF16, dma=True)
            gc = sc.ring("gc", NRG, [128, 512], BF16, dma=True)
            obl = sc.ring("obl", NRG, [128, 512], F32, dma=True)
            ks = sc.ring("ks", 4, [128, 256], BF16)
            qs = sc.ring("qs", 4, [128, 2, 128], BF16)
            obs = sc.ring("obs", 4, [128, 512], F32, dma=True)
            Rst = sc.ring("Rst", NH, [128, 2, 512], F32)
            R16 = sc.ring("R16", NH, [128, 2, 512], BF16)
            sm = sc.ring("sm", 4, [128, 128], BF16)
            ot = sc.ring("ot", 4, [128, 512], F32)
            junk = sc.sb("junk2", [128, 512], F32)
            st = sc.ring("st", 4, [128, 8], F32)
            on = sc.ring("on", 4, [128, 512], F32)
            ogm = sc.ring("ogm", 4, [128, 512], BF16)
            ogT = sc.ring("ogTs", 4, [128, 4, 128], BF16, dma=True)
            psT = sc.ps("psT", [128, 1024], BF16)
            psG = Buf(psT.t, "psG")
            psO = sc.psring("psO", 2, [128, 512], F32)
            psKV = sc.psring("psKV", 2 * NH, [128, 512], F32)
            psS = sc.ps("psS", [128, 512], F32)
            cnt = [0]

            def body(direction, h, slot, c):
                kd, cd, qd = (kdB, cdb, qdb) if direction == "b" else (kdF, cdf, qdf)
                Rs, Rb = Rst[slot], R16[slot]
                i = cnt[0]
                cnt[0] += 1
                r0 = c * 128
                q_, k_, v_ = qc[i % NRG], kc_[i % NRG], vc[i % NRG]
                kb.dma("sp", q_[:], S["qT"][h, :, :, r0:r0 + 128], q_, writes=[q_])
                kb.dma("sp", k_[:], S["kT"][h, :, :, r0:r0 + 128], k_, writes=[k_])
                kb.dma("sp", v_[:], S["v"][r0:r0 + 128, h * 512:(h + 1) * 512], v_, writes=[v_])
                col = c * RH + h
                qq = qs[i % 4]
                kb.op("pool", lambda e: e.tensor_tensor(
                    out=qq[:], in0=q_[:], in1=qd[:, h * 256:(h + 1) * 256].rearrange("p (a b) -> p a b", b=128),
                    op=ALU.mult), reads=[q_, qd], writes=[qq])
                for dc in range(2):
                    kb.op("pe", lambda e, dc=dc: e.transpose(psT[:, dc * 128:(dc + 1) * 128], k_[:, dc, :], idb[:]),
                          reads=[k_, idb], writes=[psT])
                ks_ = ks[i % 4]
                kb.op("act", lambda e: e.activation(out=ks_[:], in_=psT[:, 0:256], func=AF.Copy,
                                                     scale=kd[:, col:col + 1]),
                      reads=[psT, kd], writes=[ks_])
                po = psO[i % 2]
                if direction == "b":
                    for dc in range(2):
                        kb.op("pe", lambda e, dc=dc: e.matmul(po[:], qq[:, dc, :], Rb[:, dc, :],
                                                              start=(dc == 0), stop=(dc == 1)),
                              reads=[qq, Rb], writes=[po])
                    o_ = obs[i % 4]
                    kb.op("act", lambda e: e.copy(out=o_[:], in_=po[:]), reads=[po], writes=[o_])
                    kb.dma("act", S["ob"][r0:r0 + 128, h * 512:(h + 1) * 512], o_[:], o_, reads=[o_])
                else:
                    g_, ol = gc[i % NRG], obl[i % NRG]
                    kb.dma("sp", g_[:], S["g"][r0:r0 + 128, h * 512:(h + 1) * 512], g_, writes=[g_])
                    kb.dma("sp", ol[:], S["ob"][r0:r0 + 128, h * 512:(h + 1) * 512], ol, writes=[ol])
                    for dc in range(2):
                        kb.op("pe", lambda e, dc=dc: e.matmul(psS[:, 0:128], k_[:, dc, :], q_[:, dc, :],
                                                              start=(dc == 0), stop=(dc == 1)),
                              reads=[k_, q_], writes=[psS])
                    s_ = sm[i % 4]
                    kb.op("dve", lambda e: e.tensor_tensor(out=s_[:], in0=psS[:, 0:128],
                                                           in1=DT[:, h * 128:(h + 1) * 128], op=ALU.mult),
                          reads=[psS, DT], writes=[s_])
                    kb.op("pe", lambda e: e.matmul(po[:], s_[:], v_[:], start=True, stop=False),
                          reads=[s_, v_], writes=[po])
                    for dc in range(2):
                        kb.op("pe", lambda e, dc=dc: e.matmul(po[:], qq[:, dc, :], Rb[:, dc, :],
                                                              start=False, stop=(dc == 1)),
                              reads=[qq, Rb], writes=[po])
                for dc in range(2):
                    pk = psKV[slot * 2 + dc]
                    kb.op("pe", lambda e, dc=dc, pk=pk: e.matmul(pk[:], ks_[:, dc * 128:(dc + 1) * 128], v_[:],
                                                                 start=True, stop=True),
                          reads=[ks_, v_], writes=[pk])
                    kb.op("dve", lambda e, dc=dc, pk=pk: e.scalar_tensor_tensor(
                        out=Rs[:, dc, :], in0=Rs[:, dc, :], scalar=cd[:, col:col + 1], in1=pk[:],
                        op0=ALU.mult, op1=ALU.add), reads=[Rs, cd, pk], writes=[Rs])
                kb.op("act", lambda e: e.copy(out=Rb[:], in_=Rs[:]), reads=[Rs], writes=[Rb])
                if direction == "f":
                    o_ = ot[i % 4]
                    s4 = st[i % 4]
                    kb.op("dve", lambda e: e.tensor_tensor(out=o_[:], in0=po[:], in1=ol[:], op=ALU.add),
                          reads=[po, ol], writes=[o_])
                    kb.op("act", lambda e: e.activation(out=junk[:], in_=o_[:], func=AF.Copy,
                                                         accum_out=s4[:, 0:1]),
                          reads=[o_], writes=[junk, s4])
                    kb.op("act", lambda e: e.activation(out=junk[:], in_=o_[:], func=AF.Square,
                                                         accum_out=s4[:, 1:2]),
                          reads=[o_], writes=[junk, s4])
                    kb.op("dve", lambda e: e.tensor_scalar(out=s4[:, 2:3], in0=s4[:, 0:1], scalar1=1.0 / RDV,
                                                           scalar2=None, op0=ALU.mult),
                          reads=[s4], writes=[s4])
                    kb.op("dve", lambda e: e.tensor_tensor(out=s4[:, 3:4], in0=s4[:, 2:3], in1=s4[:, 2:3],
                                                           op=ALU.mult), reads=[s4], writes=[s4])
                    kb.op("dve", lambda e: e.scalar_tensor_tensor(out=s4[:, 4:5], in0=s4[:, 1:2],
                                                                  scalar=1.0 / RDV, in1=s4[:, 3:4],
                                                                  op0=ALU.mult, op1=ALU.subtract),
                          reads=[s4], writes=[s4])
                    kb.op("dve", lambda e: e.tensor_scalar(out=s4[:, 5:6], in0=s4[:, 4:5], scalar1=EPS,
                                                           scalar2=None, op0=ALU.add),
                          reads=[s4], writes=[s4])
                    kb.op("act", lambda e: e.sqrt(out=s4[:, 6:7], in_=s4[:, 5:6]), reads=[s4], writes=[s4])
                    kb.op("dve", lambda e: e.reciprocal(out=s4[:, 5:6], in_=s4[:, 6:7]), reads=[s4], writes=[s4])
                    n_ = on[i % 4]
                    kb.op("dve", lambda e: e.tensor_scalar(out=n_[:], in0=o_[:], scalar1=s4[:, 2:3],
                                                           scalar2=s4[:, 5:6], op0=ALU.subtract, op1=ALU.mult),
                          reads=[o_, s4], writes=[n_])
                    m_ = ogm[i % 4]
                    kb.op("pool", lambda e: e.tensor_tensor(out=m_[:], in0=n_[:], in1=g_[:], op=ALU.mult),
                          reads=[n_, g_], writes=[m_])
                    return (m_, ogT[i % 4], h, r0)
                return None

            def epilogue(p):
                m_, gT, h, r0 = p
                for b4 in range(4):
                    kb.op("pe", lambda e, b4=b4: e.transpose(psG[:, 512 + b4 * 128:512 + (b4 + 1) * 128],
                                                             m_[:, b4 * 128:(b4 + 1) * 128], idb[:]),
                          reads=[m_, idb], writes=[psG])
                kb.op("act", lambda e: e.copy(out=gT[:], in_=psG[:, 512:1024].rearrange("p (a b) -> p a b", b=128)),
                      reads=[psG], writes=[gT])
                kb.dma("act", S["ogT"][h * 4:(h + 1) * 4, :, r0:r0 + 128].rearrange("k p t -> p k t"),
                       gT[:], gT, reads=[gT])

            for direction in ("b", "f"):
                if direction == "f":
                    kb.barrier()
                order = self.fwd[::-1] if direction == "b" else self.fwd
                for h0 in range(0, RH, NH):
                    for slot in range(NH):
                        kb.op("pool", lambda e, slot=slot: e.memset(Rst[slot][:], 0.0), writes=[Rst[slot]])
                        kb.op("pool", lambda e, slot=slot: e.memset(R16[slot][:], 0.0), writes=[R16[slot]])
                    pend = []
                    for c in order:
                        for slot in range(NH):
                            p = body(direction, h0 + slot, slot, c)
                            if p is not None:
                                pend.append(p)
                            if len(pend) > 2:
                                epilogue(pend.pop(0))
                    while pend:
                        epilogue(pend.pop(0))

    def phase_out(self, l, src, kind, last):
        kb, I, S = self.kb, self.I, self.S
        j = l // 2
        Wo = self.W["ret_wo" if kind == "ret" else "mla_wo"][j]
        KC = 32 if kind == "ret" else 16
        W1, W2 = self.W["mlp_w1"][l], self.W["mlp_w2"][l]
        NT2 = 2 * self.NT
        with kb.scope() as sc:
            idb = self.load_ident(sc)
            g2 = self.load_bcast(sc, "g2", I["norm2_g"][l, :], D)
            gf = self.load_bcast(sc, "gf", I["final_norm"][0, :], D) if last else None
            valid = self.load_const(sc, "validc", I["valid"][:, :], [128, self.TT])
            hb = sc.sb("hb", [128, 4, D], F32, dma=True)
            big = sc.sb("big", [128, 64, 512], BF16, dma=True)
            aT = sc.sb("aT2", [128, 16, 512], BF16)
            atm = sc.ring("atm2", 2, [128, D], BF16)
            junk = sc.sb("junk3", [128, D], F32)
            ssq = sc.sb("ssq3", [128, 1], F32)
            rtmp = sc.sb("rtmp", [128, 512], F32)
            rstd = sc.sb("rstd3", [128, 1], F32)
            w1s = sc.ring("w1s", 2, [128, 16, 256], BF16, dma=True)
            w2s = sc.ring("w2s", 2, [128, 16, 512], BF16, dma=True)
            yo = sc.ring("yo", 1, [128, D], F32, dma=True) if last else None
            pst = sc.psring("pst2", 2, [128, 1024], BF16)
            ps1 = sc.psring("ps1", 2, [128, 512], F32)
            psA = sc.psring("psA", 4, [128, 512], F32)
            ctr = [0]
            n1 = n2 = ny = 0
            for gi, (t0, nt) in enumerate(self.groups):
                N = nt * 128
                c0 = t0 * 128
                kb.dma("sp", hb[:, 0:nt, :], src[c0:c0 + N, :].rearrange("(t p) c -> p t c", p=128), hb, writes=[hb])
                kb.dma("sp", big[:, 0:KC, 0:N], S["ogT"][0:KC, :, c0:c0 + N].rearrange("k p t -> p k t"),
                       big, writes=[big])
                for cg in range(4):
                    for half in range(KC // 16):
                        w = w2s[n2 % 2]
                        n2 += 1
                        kb.dma("sp", w[:], Wo[half * 2048:(half + 1) * 2048, cg * 512:(cg + 1) * 512]
                               .rearrange("(k p) c -> p k c", p=128), w, writes=[w])
                        for t in range(nt):
                            for kc in range(16):
                                kk = half * 16 + kc
                                kb.op("pe", lambda e, t=t, kc=kc, kk=kk: e.matmul(
                                    psA[t][:], big[:, kk, t * 128:(t + 1) * 128], w[:, kc, :],
                                    start=(kk == 0), stop=(kk == KC - 1)), reads=[big, w], writes=[psA[t]])
                    for t in range(nt):
                        kb.op("dve", lambda e, t=t: e.tensor_tensor(
                            out=hb[:, t, cg * 512:(cg + 1) * 512], in0=psA[t][:],
                            in1=hb[:, t, cg * 512:(cg + 1) * 512], op=ALU.add),
                            reads=[psA[t], hb], writes=[hb])
                for t in range(nt):
                    a = atm[(gi * 4 + t) % 2]
                    self.rmsnorm_tile((junk, ssq, rstd), hb, hb[:, t, :], g2, a, a[:], D)
                    self.transpose_blocks(a, lambda i: a[:, i * 128:(i + 1) * 128], 16, idb, pst, aT,
                                          lambda i0, n: aT[:, i0:i0 + n, t * 128:(t + 1) * 128], ctr)
                for fs in range(32):
                    w = w1s[n1 % 2]
                    n1 += 1
                    kb.dma("sp", w[:], W1[:, fs * 256:(fs + 1) * 256].rearrange("(k p) c -> p k c", p=128),
                           w, writes=[w])
                    for f2 in range(2):
                        f = fs * 2 + f2
                        pp = ps1[f % 2]
                        for kc in range(16):
                            kb.op("pe", lambda e, kc=kc, pp=pp, f2=f2: e.matmul(
                                pp[:, 0:N], w[:, kc, f2 * 128:(f2 + 1) * 128], aT[:, kc, 0:N],
                                start=(kc == 0), stop=(kc == 15)), reads=[w, aT], writes=[pp])
                        if f % 2 == 0:
                            kb.op("dve", lambda e, pp=pp, f=f: e.tensor_scalar(
                                out=rtmp[:, 0:N], in0=pp[:, 0:N], scalar1=0.0, scalar2=None,
                                op0=ALU.max), reads=[pp], writes=[rtmp])
                            kb.op("pool", lambda e, f=f: e.tensor_tensor(
                                out=big[:, f, 0:N], in0=rtmp[:, 0:N], in1=rtmp[:, 0:N], op=ALU.mult),
                                reads=[rtmp], writes=[big])
                        else:
                            kb.op("act", lambda e, pp=pp, f=f: e.activation(
                                out=junk[:, 0:N], in_=pp[:, 0:N], func=AF.Relu), reads=[pp], writes=[junk])
                            kb.op("pool", lambda e, f=f: e.tensor_tensor(
                                out=big[:, f, 0:N], in0=junk[:, 0:N], in1=junk[:, 0:N], op=ALU.mult),
                                reads=[junk], writes=[big])
                for cg in range(4):
                    for qk in range(4):
                        w = w2s[n2 % 2]
                        n2 += 1
                        kb.dma("sp", w[:], W2[qk * 2048:(qk + 1) * 2048, cg * 512:(cg + 1) * 512]
                               .rearrange("(k p) c -> p k c", p=128), w, writes=[w])
                        for t in range(nt):
                            for kc in range(16):
                                kk = qk * 16 + kc
                                kb.op("pe", lambda e, t=t, kc=kc, kk=kk: e.matmul(
                                    psA[t][:], big[:, kk, t * 128:(t + 1) * 128], w[:, kc, :],
                                    start=(kk == 0), stop=(kk == 63)), reads=[big, w], writes=[psA[t]])
                    for t in range(nt):
                        kb.op("dve", lambda e, t=t: e.tensor_tensor(
                            out=hb[:, t, cg * 512:(cg + 1) * 512], in0=psA[t][:],
                            in1=hb[:, t, cg * 512:(cg + 1) * 512], op=ALU.add),
                            reads=[psA[t], hb], writes=[hb])
                for t in range(nt):
                    tile = t0 + t
                    r0 = tile * 128
                    if tile >= NT2 and not last:
                        kb.op("dve", lambda e, t=t, tile=tile: e.tensor_scalar(
                            out=hb[:, t, :], in0=hb[:, t, :], scalar1=valid[:, tile:tile + 1], scalar2=None,
                            op0=ALU.mult), reads=[hb, valid], writes=[hb])
                    if not last:
                        kb.dma("act", S["h"][r0:r0 + 128, :], hb[:, t, :], hb, reads=[hb])
                    elif tile < NT2:
                        y = yo[0]
                        ny += 1
                        self.rmsnorm_tile((junk, ssq, rstd), hb, hb[:, t, :], gf, y, y[:], D)
                        kb.dma("act", self.out[r0:r0 + 128, :], y[:], y, reads=[y])

    def phase_M1(self, l, src):
        kb, I, S = self.kb, self.I, self.S
        j = l // 2
        Wqa, Wqb, Wkva, Wkvb = (self.W[n][j] for n in ("mla_wq_a", "mla_wq_b", "mla_wkv_a", "mla_wkv_b"))
        with kb.scope() as sc:
            idb = self.load_ident(sc)
            gbc = self.load_bcast(sc, "gbcm", I["norm1_g"][l, :], D)
            gq = self.load_bcast(sc, "gq", I["mla_q_norm"][j, :], MQL)
            gkv = self.load_bcast(sc, "gkv", I["mla_kv_norm"][j, :], MKL)
            wqa = self.load_const(sc, "wqa", Wqa.rearrange("(k p) c -> p k c", p=128), [128, 16, MQL], BF16)
            wkva = self.load_const(sc, "wkva", Wkva.rearrange("(k p) c -> p k c", p=128), [128, 16, MKL + MROPE], BF16)
            wqb = self.load_const(sc, "wqb", Wqb.rearrange("(k p) c -> p k c", p=128), [128, 4, MH * 192], BF16)
            wkvb = self.load_const(sc, "wkvb", Wkvb.rearrange("(k p) c -> p k c", p=128), [128, 4, MH * 256], BF16)
            xr = sc.ring("xrm", 2, [128, D], F32, dma=True)
            atm = sc.ring("atmm", 1, [128, D], BF16)
            junk = sc.sb("junkm", [128, D], F32)
            ssq = sc.sb("ssqm", [128, 1], F32)
            rstd = sc.sb("rstdm", [128, 1], F32)
            aT = sc.sb("aTm", [128, 16, 512], BF16)
            cqT = sc.sb("cqT", [128, 4, 512], BF16)
            ckvT = sc.sb("ckvT", [128, 4, 512], BF16)
            lat = sc.ring("lat", 1, [128, 512], F32)
            latn = sc.ring("latn", 2, [128, 512], BF16)
            csm = sc.ring("csm", 2, [128, 2, 256], F32, dma=True)
            krr = sc.ring("krr", 2, [128, 64], F32)
            krt = sc.ring("krt", 2, [128, 4, 32], F32)
            krb = sc.ring("krb", 2, [128, 128], BF16)
            krTs = sc.sb("krTs", [128, 512], BF16, dma=True)
            qrr = sc.ring("qrr", 2, [128, 512], F32)
            qrt = sc.ring("qrt", 1, [128, 4, 256], F32)
            qrb = sc.ring("qrb", 2, [128, 1024], BF16)
            qrTs = sc.ring("qrTs", 2, [128, 8, 128], BF16, dma=True)
            fo = sc.ring("fo", 2, [128, 512], BF16, dma=True)
            vo = sc.ring("vo", 1, [128, 2048], BF16, dma=True)
            pst = sc.psring("pstm", 2, [128, 1024], BF16)
            psB = sc.psring("psB", 2, [128, 512], F32)
            psC = sc.psring("psC", 2, [128, 512], F32)
            psD = sc.ps("psD", [128, 512], F32)
            ctr = [0]
            nx = nb = nc_ = nf = 0
            for gi, (t0, nt) in enumerate(self.groups):
                N = nt * 128
                c0 = t0 * 128
                for t in range(nt):
                    x, a = xr[nx % 2], atm[0]
                    r0 = (t0 + t) * 128
                    cm = csm[nx % 2]
                    nx += 1
                    kb.dma("sp", x[:], src[r0:r0 + 128, :], x, writes=[x])
                    kb.dma("sp", cm[:, 0, :], I["cosM"][r0:r0 + 128, :], cm, writes=[cm])
                    kb.dma("sp", cm[:, 1, :], I["sinM"][r0:r0 + 128, :], cm, writes=[cm])
                    self.rmsnorm_tile((junk, ssq, rstd), x, x[:], gbc, a, a[:], D)
                    self.transpose_blocks(a, lambda i: a[:, i * 128:(i + 1) * 128], 16, idb, pst, aT,
                                          lambda i0, n: aT[:, i0:i0 + n, t * 128:(t + 1) * 128], ctr)
                    for which in ("q", "kv"):
                        wa = wqa if which == "q" else wkva
                        pp = psB[nb % 2]
                        nb += 1
                        for kc in range(16):
                            kb.op("pe", lambda e, kc=kc, pp=pp, wa=wa: e.matmul(
                                pp[:], aT[:, kc, t * 128:(t + 1) * 128], wa[:, kc, 0:512],
                                start=(kc == 0), stop=(kc == 15)), reads=[aT, wa], writes=[pp])
                        la, ln_ = lat[0], latn[nb % 2]
                        kb.op("act", lambda e, pp=pp, la=la: e.copy(out=la[:], in_=pp[:]), reads=[pp], writes=[la])
                        self.rmsnorm_tile((junk, ssq, rstd), la, la[:], gq if which == "q" else gkv, ln_, ln_[:], 512)
                        dstT = cqT if which == "q" else ckvT
                        self.transpose_blocks(ln_, lambda i, ln_=ln_: ln_[:, i * 128:(i + 1) * 128], 4, idb, pst, dstT,
                                              lambda i0, n, dstT=dstT: dstT[:, i0:i0 + n, t * 128:(t + 1) * 128], ctr)
                    for kc in range(16):
                        kb.op("pe", lambda e, kc=kc: e.matmul(
                            psD[:, 0:64], aT[:, kc, t * 128:(t + 1) * 128], wkva[:, kc, 512:576],
                            start=(kc == 0), stop=(kc == 15)), reads=[aT, wkva], writes=[psD])
                    kr, kt, kbf = krr[nx % 2], krt[nx % 2], krb[nx % 2]
                    kb.op("act", lambda e: e.copy(out=kr[:], in_=psD[:, 0:64]), reads=[psD], writes=[kr])
                    cosv, sinv = cm[:, 0, 0:32], cm[:, 1, 0:32]
                    for ti, (xa, tb) in enumerate(((kr[:, 0:32], cosv), (kr[:, 32:64], sinv),
                                                   (kr[:, 0:32], sinv), (kr[:, 32:64], cosv))):
                        kb.op("dve", lambda e, ti=ti, xa=xa, tb=tb: e.tensor_tensor(
                            out=kt[:, ti, :], in0=xa, in1=tb, op=ALU.mult), reads=[kr, cm], writes=[kt])
                    for dup in range(2):
                        kb.op("dve", lambda e, dup=dup: e.tensor_tensor(
                            out=kbf[:, dup * 64:dup * 64 + 32], in0=kt[:, 0, :], in1=kt[:, 1, :], op=ALU.subtract),
                            reads=[kt], writes=[kbf])
                        kb.op("dve", lambda e, dup=dup: e.tensor_tensor(
                            out=kbf[:, dup * 64 + 32:dup * 64 + 64], in0=kt[:, 2, :], in1=kt[:, 3, :], op=ALU.add),
                            reads=[kt], writes=[kbf])
                    self.transpose_blocks(kbf, lambda i: kbf[:, :], 1, idb, pst, krTs,
                                          lambda i0, n: krTs[:, t * 128:(t + 1) * 128].rearrange("p (a b) -> p a b", a=1), ctr)
                    qb_ = qrb[nx % 2]
                    for hf in range(2):
                        pp = psC[nc_ % 2]
                        nc_ += 1
                        rhs_w = lambda kc: wqb[:, kc, hf * 8 * 192:(hf + 1) * 8 * 192].rearrange(
                            "p (h d) -> p h d", d=192)[:, :, 128:192]
                        for kc in range(4):
                            kb.op("pe", lambda e, kc=kc, pp=pp, rhs_w=rhs_w: e.matmul(
                                pp[:], cqT[:, kc, t * 128:(t + 1) * 128], rhs_w(kc),
                                start=(kc == 0), stop=(kc == 3)), reads=[cqT, wqb], writes=[pp])
                        qr, qt = qrr[nc_ % 2], qrt[0]
                        kb.op("act", lambda e, pp=pp, qr=qr: e.copy(out=qr[:], in_=pp[:]), reads=[pp], writes=[qr])
                        q3 = qr[:].rearrange("p (h d) -> p h d", d=64)
                        c3 = cm[:, 0, :].rearrange("p (h d) -> p h d", d=32)
                        s3 = cm[:, 1, :].rearrange("p (h d) -> p h d", d=32)
                        for ti, (xa, tb) in enumerate(((q3[:, :, 0:32], c3), (q3[:, :, 32:64], s3),
                                                       (q3[:, :, 0:32], s3), (q3[:, :, 32:64], c3))):
                            kb.op("dve", lambda e, ti=ti, xa=xa, tb=tb, qt=qt: e.tensor_tensor(
                                out=qt[:, ti, :].rearrange("p (h d) -> p h d", d=32), in0=xa, in1=tb, op=ALU.mult),
                                reads=[qr, cm], writes=[qt])
                        qo3 = qb_[:, hf * 512:(hf + 1) * 512].rearrange("p (h d) -> p h d", d=64)
                        kb.op("pool", lambda e, qt=qt, qo3=qo3: e.tensor_tensor(
                            out=qo3[:, :, 0:32], in0=qt[:, 0, :].rearrange("p (h d) -> p h d", d=32),
                            in1=qt[:, 1, :].rearrange("p (h d) -> p h d", d=32), op=ALU.subtract),
                            reads=[qt], writes=[qb_])
                        kb.op("pool", lambda e, qt=qt, qo3=qo3: e.tensor_tensor(
                            out=qo3[:, :, 32:64], in0=qt[:, 2, :].rearrange("p (h d) -> p h d", d=32),
                            in1=qt[:, 3, :].rearrange("p (h d) -> p h d", d=32), op=ALU.add),
                            reads=[qt], writes=[qb_])
                    qT_ = qrTs[nx % 2]
                    self.transpose_blocks(qb_, lambda i: qb_[:, i * 128:(i + 1) * 128], 8, idb, pst, qT_,
                                          lambda i0, n: qT_[:, i0:i0 + n, :], ctr)
                    kb.dma("act", S["qrT"][:, :, r0:r0 + 128].rearrange("k p t -> p k t"), qT_[:], qT_, reads=[qT_])
                    v_ = vo[0]
                    for q4 in range(4):
                        pp = psC[nc_ % 2]
                        nc_ += 1
                        rhs_w = lambda kc: wkvb[:, kc, q4 * 4 * 256:(q4 + 1) * 4 * 256].rearrange(
                            "p (h d) -> p h d", d=256)[:, :, 128:256]
                        for kc in range(4):
                            kb.op("pe", lambda e, kc=kc, pp=pp, rhs_w=rhs_w: e.matmul(
                                pp[:], ckvT[:, kc, t * 128:(t + 1) * 128], rhs_w(kc),
                                start=(kc == 0), stop=(kc == 3)), reads=[ckvT, wkvb], writes=[pp])
                        kb.op("act", lambda e, pp=pp, q4=q4: e.copy(out=v_[:, q4 * 512:(q4 + 1) * 512], in_=pp[:]),
                              reads=[pp], writes=[v_])
                    kb.dma("act", S["vm"][r0:r0 + 128, :], v_[:], v_, reads=[v_])
                kb.dma("act", S["krT"][:, c0:c0 + N], krTs[0:64, 0:N], krTs, reads=[krTs])
                for which in ("q", "k"):
                    for h in range(MH):
                        pp = psB[nb % 2]
                        nb += 1
                        if which == "q":
                            wsl = lambda kc: wqb[:, kc, h * 192:h * 192 + 128]
                            srcT, wb_, dst = cqT, wqb, S["qnT"]
                        else:
                            wsl = lambda kc: wkvb[:, kc, h * 256:h * 256 + 128]
                            srcT, wb_, dst = ckvT, wkvb, S["knT"]
                        for kc in range(4):
                            kb.op("pe", lambda e, kc=kc, pp=pp, wsl=wsl, srcT=srcT: e.matmul(
                                pp[:, 0:N], wsl(kc), srcT[:, kc, 0:N], start=(kc == 0), stop=(kc == 3)),
                                reads=[srcT, wb_], writes=[pp])
                        f_ = fo[nf % 2]
                        if nf % 2 == 0:
                            kb.op("act", lambda e, pp=pp, f_=f_: e.copy(out=f_[:, 0:N], in_=pp[:, 0:N]),
                                  reads=[pp], writes=[f_])
                        else:
                            kb.op("dve", lambda e, pp=pp, f_=f_: e.tensor_copy(out=f_[:, 0:N], in_=pp[:, 0:N]),
                                  reads=[pp], writes=[f_])
                        nf += 1
                        kb.dma("act", dst[h, :, c0:c0 + N], f_[:, 0:N], f_, reads=[f_])

    def phase_M2(self, l):
        kb, I, S = self.kb, self.I, self.S
        TT, TOK, NT = self.TT, self.TOK, self.NT
        scale = float((MNOPE + MROPE) ** -0.5)
        NP = TT // 2
        with kb.scope() as sc:
            kbias = self.load_const(sc, "kbiasc", I["kbias"][:, :], [128, TT * 2])
            ones = sc.sb("ones", [128, 128], F32)
            kb.op("pool", lambda e: e.memset(ones[:], 1.0), writes=[ones])
            accD = sc.ring("accD", 2, [128, 2, 512], F32)
            accP = sc.ring("accP", 2, [128, 2, 512], F32)
            krT = sc.sb("krT2", [128, TOK], BF16, dma=True)
            kb.dma("sp", krT[0:64, :], S["krT"][:, :], krT, writes=[krT])
            kb.dma("sp", krT[64:128, :], S["krT"][:, :], krT, writes=[krT])
            knT = sc.ring("knT", 2, [128, TOK], BF16, dma=True)
            vh = sc.ring("vh", 2, [128, TT, 128], BF16, dma=True)
            qn = sc.ring("qn", 2, [128, 512], BF16, dma=True)
            qrE = sc.ring("qrE", 2, [128, 512], BF16, dma=True)
            qrO = sc.ring("qrO", 2, [128, 512], BF16, dma=True)
            for b_ in qrE + qrO:
                kb.op("pool", lambda e, b_=b_: e.memset(b_[:], 0.0), writes=[b_])
            NPT = 4
            pT = sc.ring("pT", NPT, [128, 2, 512], BF16)
            rinv = sc.ring("rinv", 2, [128, 512], F32)
            oo = sc.ring("oo", 2, [128, 512], BF16, dma=True)
            psS = sc.psring("psSm", 3, [128, 2, 512], F32)
            psO = sc.ps("psOm", [128, 512], F32)
            psL = sc.ps("psLm", [128, 512], F32)
            it = 0
            ng = 0
            for h in range(MH):
                kn, v_ = knT[h % 2], vh[h % 2]
                kb.dma("sp", kn[:], S["knT"][h, :, :], kn, writes=[kn])
                for tb in range(0, TT, 8):
                    te = min(TT, tb + 8)
                    kb.dma("sp", v_[:, tb:te, :],
                           S["vm"][tb * 128:te * 128, h * 128:(h + 1) * 128].rearrange("(t p) c -> p t c", p=128),
                           v_, writes=[v_])
                hp = (h % 2) * 64
                for (t0, nt, seg) in self.qgroups:
                    N = nt * 128
                    c0 = t0 * 128
                    qn_, qr_ = qn[ng % 2], (qrE if h % 2 == 0 else qrO)[ng % 2]
                    aD, aP = accD[ng % 2], accP[ng % 2]
                    po, pl = psO, psL
                    kb.dma("sp", qn_[:, 0:N], S["qnT"][h, :, c0:c0 + N], qn_, writes=[qn_])
                    kb.dma("sp", qr_[hp:hp + 64, 0:N], S["qrT"][h // 2, hp:hp + 64, c0:c0 + N], qr_, writes=[qr_])

                    def score(j, slot):
                        ps = psS[slot % 3]
                        for b2 in range(2):
                            kt = 2 * j + b2
                            kb.op("pe", lambda e: e.matmul(ps[:, b2, 0:N], kn[:, kt * 128:(kt + 1) * 128], qn_[:, 0:N],
                                                           start=True, stop=False), reads=[kn, qn_], writes=[ps])
                            kb.op("pe", lambda e: e.matmul(ps[:, b2, 0:N], krT[:, kt * 128:(kt + 1) * 128],
                                                           qr_[:, 0:N], start=False, stop=True),
                                  reads=[krT, qr_], writes=[ps])

                    score(0, it)
                    score(1, it + 1)
                    for j in range(NP):
                        ps, p_ = psS[it % 3], pT[it % NPT]
                        kt0 = 2 * j
                        if kt0 + 1 < 2 * NT:
                            bcol = kt0 * 2 + seg
                            kb.op("act", lambda e: e.activation(
                                out=p_[:, :, 0:N], in_=ps[:, :, 0:N], func=AF.Exp,
                                bias=kbias[:, bcol:bcol + 1], scale=scale), reads=[ps, kbias], writes=[p_])
                        else:
                            for b2 in range(2):
                                bcol = (kt0 + b2) * 2 + seg
                                kb.op("act", lambda e: e.activation(
                                    out=p_[:, b2, 0:N], in_=ps[:, b2, 0:N], func=AF.Exp,
                                    bias=kbias[:, bcol:bcol + 1], scale=scale), reads=[ps, kbias], writes=[p_])
                        for b2 in range(2):
                            kt = kt0 + b2
                            kb.op("pe", lambda e: e.matmul(po[:, 0:N], v_[:, kt, :], p_[:, b2, 0:N],
                                                           start=(kt == 0), stop=(kt == TT - 1)),
                                  reads=[v_, p_], writes=[po])
                        if j + 2 < NP:
                            score(j + 2, it + 2)
                        aeng, acc = ("dve", aD) if j % 2 == 0 else ("pool", aP)
                        if j < 2:
                            kb.op(aeng, lambda e: e.tensor_copy(out=acc[:, :, 0:N], in_=p_[:, :, 0:N]),
                                  reads=[p_], writes=[acc])
                        else:
                            kb.op(aeng, lambda e: e.tensor_tensor(
                                out=acc[:, :, 0:N], in0=acc[:, :, 0:N], in1=p_[:, :, 0:N], op=ALU.add),
                                reads=[p_, acc], writes=[acc])
                        it += 1
                    kb.op("dve", lambda e: e.tensor_tensor(out=aD[:, :, 0:N], in0=aD[:, :, 0:N],
                                                           in1=aP[:, :, 0:N], op=ALU.add),
                          reads=[aD, aP], writes=[aD])
                    kb.op("dve", lambda e: e.tensor_tensor(out=aD[:, 0, 0:N], in0=aD[:, 0, 0:N],
                                                           in1=aD[:, 1, 0:N], op=ALU.add),
                          reads=[aD], writes=[aD])
                    kb.op("pe", lambda e: e.matmul(pl[:, 0:N], ones[:], aD[:, 0, 0:N], start=True, stop=True),
                          reads=[ones, aD], writes=[pl])
                    ri, o_ = rinv[ng % 2], oo[ng % 2]
                    kb.op("dve", lambda e: e.reciprocal(out=ri[:, 0:N], in_=pl[:, 0:N]), reads=[pl], writes=[ri])
                    kb.op("dve", lambda e: e.tensor_tensor(out=o_[:, 0:N], in0=po[:, 0:N], in1=ri[:, 0:N], op=ALU.mult),
                          reads=[po, ri], writes=[o_])
                    kb.dma("act", S["ogT"][h, :, c0:c0 + N], o_[:, 0:N], o_, reads=[o_])
                    ng += 1


def _tables(NT, kind):
    TT = 2 * NT + 2
    TOK = TT * 128
    S = NT * 128
    pos = np.zeros(TOK, np.float64)
    valid = np.zeros(TOK, np.float64)
    for seg in range(2):
        base = seg * S
        off = NMETA + (base if kind == "prompt" else 0)
        pos[base:base + S] = off + np.arange(S)
        valid[base:base + S] = 1
    for seg in range(2):
        r = (2 * NT + seg) * 128 + 112
        if seg == 0 or kind == "sample":
            pos[r:r + 16] = np.arange(16)
            valid[r:r + 16] = 1
    f32 = np.float32
    inv_r = 1.0 / (10000.0 ** (np.arange(0, RDK, 2, dtype=np.float64) / RDK))
    ang = (pos[None, :] * inv_r[:, None])
    T = dict(cosRT=np.cos(ang).astype(f32), sinRT=np.sin(ang).astype(f32))
    inv_m = 1.0 / (10000.0 ** (np.arange(0, MROPE, 2, dtype=np.float64) / MROPE))
    angm = pos[:, None] * inv_m[None, :]
    T["cosM"] = np.tile(np.cos(angm), (1, 8)).astype(f32)
    T["sinM"] = np.tile(np.sin(angm), (1, 8)).astype(f32)
    T["ident"] = np.eye(128, dtype=f32)
    hh = np.arange(RH, dtype=np.float64)
    lgf = np.log(1.0 - 2.0 ** (-5.0 - hh))
    lgb = np.log(1.0 - 2.0 ** (-5.5 - hh))
    i = np.arange(128, dtype=np.float64)
    diff = i[:, None] - i[None, :]
    kscale = RDK ** -0.5
    DT = np.zeros((128, RH, 128))
    qdf = np.zeros((128, RH, 2, 128))
    qdb = np.zeros((128, RH, 2, 128))
    for h in range(RH):
        Df = np.where(diff >= 0, np.exp(lgf[h] * np.abs(diff)), 0.0)
        Db = np.where(diff < 0, np.exp(lgb[h] * np.abs(diff)), 0.0)
        DT[:, h, :] = (Df + Db).T * kscale
        qdf[:, h, :, :] = np.exp(lgf[h] * (i + 1.0))[None, None, :]
        qdb[:, h, :, :] = np.exp(lgb[h] * (128 - i))[None, None, :]
    T["DT"] = DT.reshape(128, RH * 128).astype(f32)
    T["qdf"] = qdf.reshape(128, RH * 256).astype(f32)
    T["qdb"] = qdb.reshape(128, RH * 256).astype(f32)
    kdF = np.zeros((128, TT, RH)); kdB = np.zeros((128, TT, RH))
    cdf = np.zeros((128, TT, RH)); cdb = np.zeros((128, TT, RH))
    for h in range(RH):
        kdF[:, :, h] = (np.exp(lgf[h] * (127.0 - i)) * kscale)[:, None]
        kdB[:, :, h] = (np.exp(lgb[h] * i) * kscale)[:, None]
        cdf[:, :, h] = np.exp(lgf[h] * 128)
        cdb[:, :, h] = np.exp(lgb[h] * 128)
    m1 = 2 * NT + 1
    if kind == "prompt":
        kdF[:, m1, :] = 0; kdB[:, m1, :] = 0; cdf[:, m1, :] = 1; cdb[:, m1, :] = 1
    else:
        kdF[:, NT - 1, :] = 0; cdf[:, NT - 1, :] = 0
        kdB[:, m1, :] = 0; cdb[:, m1, :] = 0
    for n, a in (("kdF", kdF), ("kdB", kdB), ("cdf", cdf), ("cdb", cdb)):
        T[n] = a.reshape(128, TT * RH).astype(f32)
    kbias = np.zeros((128, TT, 2))
    v2 = valid.reshape(TT, 128)
    for kt in range(TT):
        sk = 0 if kt < NT else (1 if kt < 2 * NT else kt - 2 * NT)
        for sq in range(2):
            b = np.where(v2[kt] > 0, 0.0, NEG)
            if kind == "sample" and sk != sq:
                b = np.full(128, NEG)
            kbias[:, kt, sq] = b
    T["kbias"] = kbias.reshape(128, TT * 2).astype(f32)
    T["valid"] = np.ascontiguousarray(v2.T).astype(f32)
    return T


def _core_x(NT, kind, seqs, meta):
    TT = 2 * NT + 2
    S = NT * 128
    x = np.zeros((TT * 128, D), np.float32)
    if kind == "prompt":
        x[0:2 * S] = seqs[0]
    else:
        x[0:S] = seqs[0]
        x[S:2 * S] = seqs[1]
    r = 2 * NT * 128 + 112
    x[r:r + 16] = meta
    if kind == "sample":
        r = (2 * NT + 1) * 128 + 112
        x[r:r + 16] = meta
    return x


_CACHE = {}


def run_cores(NT, depth, core_specs, weights, debug=None, trace=False):
    key = (NT, depth, tuple(sorted(debug)) if debug else None)
    if key not in _CACHE:
        _CACHE[key] = Prog(NT, depth, debug=debug).build()
    nc = _CACHE[key]
    NR, NM = (depth + 1) // 2, depth // 2
    common = {}
    f = lambda a: np.ascontiguousarray(np.asarray(a, dtype=np.float32))
    common["norm1_g"] = f(weights["norm1_g"][:depth])
    common["norm2_g"] = f(weights["norm2_g"][:depth])
    common["final_norm"] = f(weights["final_norm"]).reshape(1, D)
    common["mlp_w1"] = f(weights["mlp_w1"][:depth])
    common["mlp_w2"] = f(weights["mlp_w2"][:depth])
    for n in ("ret_wq", "ret_wk", "ret_wv", "ret_wg", "ret_wo"):
        common[n] = f(weights[n][:NR])
    if NM:
        for n in ("mla_wq_a", "mla_wq_b", "mla_wkv_a", "mla_wkv_b", "mla_wo", "mla_q_norm", "mla_kv_norm"):
            common[n] = f(weights[n][:NM])
    tabs = {k: _tables(NT, k) for k in set(s[0] for s in core_specs)}
    meta = f(weights["meta_tokens"])
    in_maps = []
    for kind, seqs in core_specs:
        m = dict(common)
        m.update(tabs[kind])
        m["x_in"] = _core_x(NT, kind, seqs, meta)
        in_maps.append(m)
    res = run_bass_kernel_spmd(nc, in_maps, core_ids=list(range(len(core_specs))), trace=trace)
    return res


def kernel(**inputs):
    NT = 32
    xp = np.asarray(inputs["x_prompt"], dtype=np.float32)
    xs = np.asarray(inputs["x_sample"], dtype=np.float32)
    specs = [("prompt", [xp[b]]) for b in range(4)] + [("sample", [xs[2 * c], xs[2 * c + 1]]) for c in range(4)]
    res = run_cores(NT, 4, specs, inputs)
    outs = [np.asarray(r["out"]) for r in res.results]
    yp = np.stack([outs[b].reshape(8192, D) for b in range(4)], axis=0).astype(np.float32)
    ys = np.concatenate([outs[4 + c].reshape(2, 4096, D) for c in range(4)], axis=0).astype(np.float32)
    return (yp, ys)
```

```python
import numpy as np
import ml_dtypes
from contextlib import ExitStack
import concourse.bass as bass
import concourse.mybir as mybir
from concourse.bass_utils import run_bass_kernel_spmd

F32 = mybir.dt.float32
BF16 = mybir.dt.bfloat16
AF = mybir.ActivationFunctionType
ALU = mybir.AluOpType

D = 2048
DFF = 8192
NMETA = 16
EPS = 1e-6
RH, RDK, RDV = 8, 256, 512
MH, MQL, MKL, MNOPE, MROPE, MV = 16, 512, 512, 128, 64, 128
NEG = -30000.0


class DSem:
    def __init__(self, handle, key):
        self.handle, self.key, self.cnt = handle, key, 0


class Buf:
    def __init__(self, t, name):
        self.t, self.name = t, name
        self.w, self.r = {}, {}
        self.dsem = None

    def __getitem__(self, k):
        return self.t[k]


class KB:
    def __init__(self, nc, es, n_dsem=48):
        self.nc, self.es = nc, es
        self.E = dict(pe=nc.tensor, act=nc.scalar, dve=nc.vector, pool=nc.gpsimd, sp=nc.sync)
        self.sem = {e: es.enter_context(nc.semaphore("s_" + e)) for e in self.E}
        self.semobj = dict(self.sem)
        self.cnt = {e: 0 for e in self.E}
        self.seen = {e: {} for e in self.E}
        self.dsems = []
        for i in range(n_dsem):
            d = DSem(es.enter_context(nc.semaphore("d%d" % i)), "d%d" % i)
            self.dsems.append(d)
            self.semobj[d.key] = d.handle
        self.dfree = list(self.dsems)
        self.dmap = {d.key: d for d in self.dsems}
        self.nwait = 0

    def scope(self):
        return Scope(self)

    def _wait(self, e, key, v):
        d = self.dmap.get(key)
        if d is not None:
            v = max(v, d.cnt)
        if self.seen[e].get(key, 0) >= v:
            return
        self.E[e].wait_ge(self.semobj[key], v)
        self.seen[e][key] = v
        self.nwait += 1

    def _pre(self, e, reads, writes):
        for b in reads:
            for key, v in b.w.items():
                self._wait(e, key, v)
        for b in writes:
            for key, v in b.w.items():
                if key != e:
                    self._wait(e, key, v)
            for key, v in b.r.items():
                if key != e:
                    self._wait(e, key, v)

    def op(self, e, fn, reads=(), writes=()):
        self._pre(e, reads, writes)
        ins = fn(self.E[e])
        self.cnt[e] += 1
        v = self.cnt[e]
        ins.then_inc(self.sem[e], 1)
        for b in reads:
            b.r[e] = v
        for b in writes:
            b.w[e] = v
            b.r = {}
        return ins

    def dma(self, q, out, in_, sbuf, reads=(), writes=()):
        self._pre(q, reads, writes)
        ds = sbuf.dsem
        ins = self.E[q].dma_start(out=out, in_=in_)
        ds.cnt += 16
        ins.then_inc(ds.handle, 16)
        for b in reads:
            b.r[ds.key] = ds.cnt
        for b in writes:
            b.w[ds.key] = ds.cnt
            b.r = {}

    def barrier(self):
        for e in self.E:
            for e2 in self.E:
                if e2 != e and self.cnt[e2]:
                    self._wait(e, e2, self.cnt[e2])
            for d in self.dsems:
                if d.cnt:
                    self._wait(e, d.key, d.cnt)


class Scope:
    _n = 0

    def __init__(self, kb):
        self.kb = kb
        self.es = ExitStack()
        self.held = []
        Scope._n += 1
        self.sfx = "_s%d" % Scope._n

    def __enter__(self):
        self.es.__enter__()
        return self

    def __exit__(self, *a):
        self.kb.barrier()
        for d in self.held:
            self.kb.dfree.append(d)
        return self.es.__exit__(*a)

    def sb(self, name, shape, dt, dma=False):
        b = Buf(self.es.enter_context(self.kb.nc.sbuf_tensor(name + self.sfx, list(shape), dt)), name)
        if dma:
            b.dsem = self.kb.dfree.pop()
            self.held.append(b.dsem)
        return b

    def ps(self, name, shape, dt):
        return Buf(self.es.enter_context(self.kb.nc.psum_tensor(name + self.sfx, list(shape), dt)), name)

    def ring(self, name, n, shape, dt, dma=False):
        return [self.sb("%s%d" % (name, i), shape, dt, dma) for i in range(n)]

    def psring(self, name, n, shape, dt):
        return [self.ps("%s%d" % (name, i), shape, dt) for i in range(n)]


class Prog:
    def __init__(self, NT, depth, debug=False):
        self.NT, self.depth = NT, depth
        self.TT = 2 * NT + 2
        self.TOK = self.TT * 128
        self.debug = debug
        self.groups = [(t, min(4, 2 * NT - t)) for t in range(0, 2 * NT, 4)] + [(2 * NT, 2)]
        self.qgroups = []
        for s in range(2):
            for t in range(s * NT, (s + 1) * NT, 4):
                self.qgroups.append((t, min(4, (s + 1) * NT - t), s))
        self.qgroups += [(2 * NT, 1, 0), (2 * NT + 1, 1, 1)]
        self.fwd = [2 * NT] + list(range(NT)) + [2 * NT + 1] + list(range(NT, 2 * NT))

    def seg_of_tile(self, t):
        NT = self.NT
        if t < NT:
            return 0
        if t < 2 * NT:
            return 1
        return t - 2 * NT

    def build(self):
        nc = bass.Bass("TRN2", target_bir_lowering=False)
        self.nc = nc
        TOK, TT = self.TOK, self.TT
        NR, NM = (self.depth + 1) // 2, self.depth // 2
        self.NR, self.NM = NR, NM

        def inp(name, shape, dt=F32):
            return nc.dram_tensor(name, list(shape), dt, kind="ExternalInput").ap()

        def scr(name, shape, dt):
            kind = "ExternalOutput" if (self.debug and name in self.debug) else "Internal"
            return nc.dram_tensor(name, list(shape), dt, kind=kind).ap()

        dp = self.depth
        I = self.I = {}
        I["x_in"] = inp("x_in", [TOK, D])
        I["norm1_g"] = inp("norm1_g", [dp, D])
        I["norm2_g"] = inp("norm2_g", [dp, D])
        I["final_norm"] = inp("final_norm", [1, D])
        wshapes = dict(
            mlp_w1=[dp, D, DFF], mlp_w2=[dp, DFF, D],
            ret_wq=[NR, D, RH * RDK], ret_wk=[NR, D, RH * RDK], ret_wv=[NR, D, RH * RDV],
            ret_wg=[NR, D, RH * RDV], ret_wo=[NR, RH * RDV, D],
            mla_wq_a=[NM, D, MQL], mla_wq_b=[NM, MQL, MH * (MNOPE + MROPE)],
            mla_wkv_a=[NM, D, MKL + MROPE], mla_wkv_b=[NM, MKL, MH * (MNOPE + MV)],
            mla_wo=[NM, MH * MV, D])
        self.W = {}
        for n, s in wshapes.items():
            if s[0] == 0:
                continue
            I[n] = inp(n, s)
            self.W[n] = scr("wb_" + n, s, BF16)
        if NM:
            I["mla_q_norm"] = inp("mla_q_norm", [NM, MQL])
            I["mla_kv_norm"] = inp("mla_kv_norm", [NM, MKL])
        for n, s in dict(ident=[128, 128], cosRT=[128, TOK], sinRT=[128, TOK],
                         cosM=[TOK, 256], sinM=[TOK, 256], DT=[128, RH * 128],
                         qdf=[128, RH * 256], qdb=[128, RH * 256],
                         kdF=[128, TT * RH], kdB=[128, TT * RH],
                         cdf=[128, TT * RH], cdb=[128, TT * RH],
                         kbias=[128, TT * 2], valid=[128, TT]).items():
            I[n] = inp(n, s)
        self.out = nc.dram_tensor("out", [2 * self.NT * 128, D], F32, kind="ExternalOutput").ap()
        S = self.S = {}
        S["h"] = scr("h", [TOK, D], F32)
        S["qT"] = scr("qT", [RH, 128, 2, TOK], BF16)
        S["kT"] = scr("kT", [RH, 128, 2, TOK], BF16)
        S["v"] = scr("v", [TOK, RH * RDV], BF16)
        S["g"] = scr("g", [TOK, RH * RDV], BF16)
        S["ob"] = scr("ob", [TOK, RH * RDV], F32)
        S["ogT"] = scr("ogT", [32, 128, TOK], BF16)
        S["qnT"] = scr("qnT", [MH, 128, TOK], BF16)
        S["qrT"] = scr("qrT", [MH // 2, 128, TOK], BF16)
        S["knT"] = scr("knT", [MH, 128, TOK], BF16)
        S["krT"] = scr("krT", [64, TOK], BF16)
        S["vm"] = scr("vm", [TOK, MH * MV], BF16)

        with ExitStack() as es:
            kb = self.kb = KB(nc, es)
            self.prologue()
            for l in range(self.depth):
                src = I["x_in"] if l == 0 else S["h"]
                last = (l == self.depth - 1)
                if l % 2 == 0:
                    self.phase_R1(l, src)
                    self.phase_R2(l)
                    self.phase_out(l, src, "ret", last)
                else:
                    self.phase_M1(l, src)
                    self.phase_M2(l)
                    self.phase_out(l, src, "mla", last)
            kb.barrier()
        return nc

    def prologue(self):
        kb, nc = self.kb, self.nc

        def convert(names_layers, tracker):
            for n, l in names_layers:
                dst, src = self.W[n], self.I[n]
                L, R, C = src.shape
                step = max(1, (1 << 21) // C)
                for r0 in range(0, R, step):
                    r1 = min(R, r0 + step)
                    kb.dma("pool", dst[l, r0:r1, :], src[l, r0:r1, :], tracker)

        first = [(n, 0) for n in ("ret_wq", "ret_wk", "ret_wv", "ret_wg")]
        with kb.scope() as sc:
            dummy = sc.sb("cvt_dummy", [128, 8], F32, dma=True)
            convert(first, dummy)
        bg = Buf(None, "cvt_bg")
        bg.dsem = kb.dfree.pop()
        rest = [(n, l) for n in self.W for l in range(self.I[n].shape[0]) if (n, l) not in first]
        order = ["ret_wo", "mlp_w1", "mlp_w2", "mla_wq_a", "mla_wkv_a", "mla_wq_b", "mla_wkv_b", "mla_wo",
                 "ret_wq", "ret_wk", "ret_wv", "ret_wg"]
        rest.sort(key=lambda x: (x[1], order.index(x[0])))
        convert(rest, bg)

    def load_ident(self, sc):
        kb = self.kb
        idf = sc.sb("idf", [128, 128], F32, dma=True)
        idb = sc.sb("idb", [128, 128], BF16)
        kb.dma("sp", idf[:], self.I["ident"][:, :], idf, writes=[idf])
        kb.op("dve", lambda e: e.tensor_copy(out=idb[:], in_=idf[:]), reads=[idf], writes=[idb])
        return idb

    def load_bcast(self, sc, name, row_ap, n):
        kb = self.kb
        b = sc.sb(name, [128, n], F32, dma=True)
        kb.dma("sp", b[:], row_ap.partition_broadcast(128), b, writes=[b])
        return b

    def load_const(self, sc, name, ap, shape, dt=F32):
        kb = self.kb
        b = sc.sb(name, shape, dt, dma=True)
        kb.dma("sp", b[:], ap, b, writes=[b])
        return b

    def rmsnorm_tile(self, sc_bufs, x, xap, gbc, out, outap, n):
        kb = self.kb
        junk, ssq, rstd = sc_bufs
        kb.op("act", lambda e: e.activation(out=junk[:, 0:n], in_=xap, func=AF.Square,
                                             accum_out=ssq[:, 0:1]),
              reads=[x], writes=[junk, ssq])
        kb.op("dve", lambda e: e.tensor_scalar(out=rstd[:, 0:1], in0=ssq[:, 0:1], scalar1=1.0 / n,
                                               scalar2=EPS, op0=ALU.mult, op1=ALU.add),
              reads=[ssq], writes=[rstd])
        kb.op("act", lambda e: e.sqrt(out=rstd[:, 0:1], in_=rstd[:, 0:1]), reads=[rstd], writes=[rstd])
        kb.op("dve", lambda e: e.reciprocal(out=rstd[:, 0:1], in_=rstd[:, 0:1]), reads=[rstd], writes=[rstd])
        kb.op("dve", lambda e: e.scalar_tensor_tensor(out=outap, in0=xap, scalar=rstd[:, 0:1],
                                                      in1=gbc[:, 0:n], op0=ALU.mult, op1=ALU.mult),
              reads=[x, rstd, gbc], writes=[out])

    def transpose_blocks(self, src, src_ap_fn, nblk, idb, pst_ring, dst, dst_ap_fn, ctr, evac=None):
        kb = self.kb
        i = 0
        while i < nblk:
            n = min(8, nblk - i)
            ps = pst_ring[ctr[0] % len(pst_ring)]
            for j in range(n):
                kb.op("pe", lambda e, j=j: e.transpose(ps[:, j * 128:(j + 1) * 128], src_ap_fn(i + j), idb[:]),
                      reads=[src, idb], writes=[ps])
            eng = ("act", "dve")[ctr[0] % 2] if evac is None else evac
            oap = dst_ap_fn(i, n)
            iap = ps[:, 0:n * 128].rearrange("p (a b) -> p a b", b=128)
            if eng == "act":
                kb.op("act", lambda e: e.copy(out=oap, in_=iap), reads=[ps], writes=[dst])
            else:
                kb.op("dve", lambda e: e.tensor_copy(out=oap, in_=iap), reads=[ps], writes=[dst])
            ctr[0] += 1
            i += n

    def phase_R1(self, l, src):
        kb, I, S = self.kb, self.I, self.S
        j = l // 2
        Wq, Wk, Wv, Wg = (self.W[n][j] for n in ("ret_wq", "ret_wk", "ret_wv", "ret_wg"))
        with kb.scope() as sc:
            idb = self.load_ident(sc)
            gbc = self.load_bcast(sc, "gbc", I["norm1_g"][l, :], D)
            xr = sc.ring("xr", 2, [128, D], F32, dma=True)
            atm = sc.ring("atm", 2, [128, D], BF16)
            junk = sc.sb("junk", [128, D], F32)
            ssq = sc.sb("ssq", [128, 1], F32)
            rstd = sc.sb("rstd", [128, 1], F32)
            aT = sc.sb("aT", [128, 16, 512], BF16)
            cs = sc.ring("cs", 2, [128, 2, 512], F32, dma=True)
            wqk = sc.ring("wqk", 2, [128, 16, 256], BF16, dma=True)
            wvg = sc.ring("wvg", 2, [128, 16, 512], BF16, dma=True)
            tmp = sc.ring("rt", 2, [128, 4, 512], F32)
            qko = sc.ring("qko", 2, [128, 2, 512], BF16, dma=True)
            vgo = sc.ring("vgo", 2, [128, 4, 512], BF16, dma=True)
            pst = sc.psring("pst", 2, [128, 1024], BF16)
            psqk = sc.psring("psqk", 4, [128, 512], F32)
            psvg = sc.psring("psvg", 2, [128, 512], F32)
            ctr = [0]
            nx = nqk = nvg = nps = 0
            for gi, (t0, nt) in enumerate(self.groups):
                N = nt * 128
                c0 = t0 * 128
                csb = cs[gi % 2]
                kb.dma("sp", csb[:, 0, 0:N], I["cosRT"][:, c0:c0 + N], csb, writes=[csb])
                kb.dma("sp", csb[:, 1, 0:N], I["sinRT"][:, c0:c0 + N], csb, writes=[csb])
                for t in range(nt):
                    x = xr[nx % 2]
                    a = atm[nx % 2]
                    nx += 1
                    r0 = (t0 + t) * 128
                    kb.dma("sp", x[:], src[r0:r0 + 128, :], x, writes=[x])
                    self.rmsnorm_tile((junk, ssq, rstd), x, x[:], gbc, a, a[:], D)
                    self.transpose_blocks(a, lambda i: a[:, i * 128:(i + 1) * 128], 16, idb, pst, aT,
                                          lambda i0, n: aT[:, i0:i0 + n, t * 128:(t + 1) * 128], ctr)
                for which, Wm, dst in (("q", Wq, S["qT"]), ("k", Wk, S["kT"])):
                    for h in range(RH):
                        w = wqk[nqk % 2]
                        kb.dma("sp", w[:], Wm[:, h * 256:(h + 1) * 256].rearrange("(k p) c -> p k c", p=128),
                               w, writes=[w])
                        p1 = psqk[(nqk % 2) * 2]
                        p2 = psqk[(nqk % 2) * 2 + 1]
                        for c, pp in ((0, p1), (1, p2)):
                            for kc in range(16):
                                kb.op("pe", lambda e, c=c, pp=pp, kc=kc: e.matmul(
                                    pp[:, 0:N], w[:, kc, c * 128:(c + 1) * 128], aT[:, kc, 0:N],
                                    start=(kc == 0), stop=(kc == 15)), reads=[w, aT], writes=[pp])
                        tb = tmp[nqk % 2]
                        ob = qko[nqk % 2]
                        nqk += 1
                        for ti, (pp, ci) in enumerate(((p1, 0), (p2, 1), (p1, 1), (p2, 0))):
                            kb.op("dve", lambda e, ti=ti, pp=pp, ci=ci: e.tensor_tensor(
                                out=tb[:, ti, 0:N], in0=pp[:, 0:N], in1=csb[:, ci, 0:N], op=ALU.mult),
                                reads=[pp, csb], writes=[tb])
                        kb.op("pool", lambda e: e.tensor_tensor(out=ob[:, 0, 0:N], in0=tb[:, 0, 0:N],
                                                                in1=tb[:, 1, 0:N], op=ALU.subtract),
                              reads=[tb], writes=[ob])
                        kb.op("pool", lambda e: e.tensor_tensor(out=ob[:, 1, 0:N], in0=tb[:, 2, 0:N],
                                                                in1=tb[:, 3, 0:N], op=ALU.add),
                              reads=[tb], writes=[ob])
                        kb.dma("act", dst[h, :, :, c0:c0 + N], ob[:, :, 0:N], ob, reads=[ob])
                for which, Wm, dst in (("v", Wv, S["v"]), ("g", Wg, S["g"])):
                    for cg in range(8):
                        w = wvg[nvg % 2]
                        ob = vgo[nvg % 2]
                        nvg += 1
                        kb.dma("sp", w[:], Wm[:, cg * 512:(cg + 1) * 512].rearrange("(k p) c -> p k c", p=128),
                               w, writes=[w])
                        for t in range(nt):
                            pp = psvg[nps % 2]
                            nps += 1
                            for kc in range(16):
                                kb.op("pe", lambda e, pp=pp, kc=kc, t=t: e.matmul(
                                    pp[:], aT[:, kc, t * 128:(t + 1) * 128], w[:, kc, :],
                                    start=(kc == 0), stop=(kc == 15)), reads=[w, aT], writes=[pp])
                            fn = AF.Copy if which == "v" else AF.Silu
                            kb.op("act", lambda e, pp=pp, t=t, fn=fn: e.activation(out=ob[:, t, :], in_=pp[:], func=fn),
                                  reads=[pp], writes=[ob])
                        kb.dma("act", dst[c0:c0 + N, cg * 512:(cg + 1) * 512].rearrange("(t p) c -> p t c", p=128),
                               ob[:, 0:nt, :], ob, reads=[ob])

    def phase_R2(self, l):
        kb, I, S = self.kb, self.I, self.S
        TT = self.TT
        NH = 2
        with kb.scope() as sc:
            idb = self.load_ident(sc)
            DT = self.load_const(sc, "DTc", I["DT"][:, :], [128, RH * 128])
            qdf = self.load_const(sc, "qdfc", I["qdf"][:, :], [128, RH * 256])
            qdb = self.load_const(sc, "qdbc", I["qdb"][:, :], [128, RH * 256])
            kdF = self.load_const(sc, "kdFc", I["kdF"][:, :], [128, TT * RH])
            kdB = self.load_const(sc, "kdBc", I["kdB"][:, :], [128, TT * RH])
            cdf = self.load_const(sc, "cdfc", I["cdf"][:, :], [128, TT * RH])
            cdb = self.load_const(sc, "cdbc", I["cdb"][:, :], [128, TT * RH])
            NRG = 6
            qc = sc.ring("qc", NRG, [128, 2, 128], BF16, dma=True)
            kc_ = sc.ring("kc", NRG, [128, 2, 128], BF16, dma=True)
            vc = sc.ring("vc", NRG, [128, 512], BF16, dma=True)
            gc = sc.ring("gc", NRG, [128, 512], BF16, dma=True)
            obl = sc.ring("obl", NRG, [128, 512], F32, dma=True)
            ks = sc.ring("ks", 4, [128, 256], BF16)
            qs = sc.ring("qs", 4, [128, 2, 128], BF16)
            obs = sc.ring("obs", 4, [128, 512], F32, dma=True)
            Rst = sc.ring("Rst", NH, [128, 2, 512], F32)
            R16 = sc.ring("R16", NH, [128, 2, 512], BF16)
            sm = sc.ring("sm", 4, [128, 128], BF16)
            ot = sc.ring("ot", 4, [128, 512], F32)
            junk = sc.sb("junk2", [128, 512], F32)
            st = sc.ring("st", 4, [128, 8], F32)
            on = sc.ring("on", 4, [128, 512], F32)
            ogm = sc.ring("ogm", 4, [128, 512], BF16)
            ogT = sc.ring("ogTs", 4, [128, 4, 128], BF16, dma=True)
            psT = sc.ps("psT", [128, 1024], BF16)
            psG = Buf(psT.t, "psG")
            psO = sc.psring("psO", 2, [128, 512], F32)
            psKV = sc.psring("psKV", 2 * NH, [128, 512], F32)
            psS = sc.ps("psS", [128, 512], F32)
            cnt = [0]

            def body(direction, h, slot, c):
                kd, cd, qd = (kdB, cdb, qdb) if direction == "b" else (kdF, cdf, qdf)
                Rs, Rb = Rst[slot], R16[slot]
                i = cnt[0]
                cnt[0] += 1
                r0 = c * 128
                q_, k_, v_ = qc[i % NRG], kc_[i % NRG], vc[i % NRG]
                kb.dma("sp", q_[:], S["qT"][h, :, :, r0:r0 + 128], q_, writes=[q_])
                kb.dma("sp", k_[:], S["kT"][h, :, :, r0:r0 + 128], k_, writes=[k_])
                kb.dma("sp", v_[:], S["v"][r0:r0 + 128, h * 512:(h + 1) * 512], v_, writes=[v_])
                col = c * RH + h
                qq = qs[i % 4]
                kb.op("pool", lambda e: e.tensor_tensor(
                    out=qq[:], in0=q_[:], in1=qd[:, h * 256:(h + 1) * 256].rearrange("p (a b) -> p a b", b=128),
                    op=ALU.mult), reads=[q_, qd], writes=[qq])
                for dc in range(2):
                    kb.op("pe", lambda e, dc=dc: e.transpose(psT[:, dc * 128:(dc + 1) * 128], k_[:, dc, :], idb[:]),
                          reads=[k_, idb], writes=[psT])
                ks_ = ks[i % 4]
                kb.op("act", lambda e: e.activation(out=ks_[:], in_=psT[:, 0:256], func=AF.Copy,
                                                     scale=kd[:, col:col + 1]),
                      reads=[psT, kd], writes=[ks_])
                po = psO[i % 2]
                if direction == "b":
                    for dc in range(2):
                        kb.op("pe", lambda e, dc=dc: e.matmul(po[:], qq[:, dc, :], Rb[:, dc, :],
                                                              start=(dc == 0), stop=(dc == 1)),
                              reads=[qq, Rb], writes=[po])
                    o_ = obs[i % 4]
                    kb.op("act", lambda e: e.copy(out=o_[:], in_=po[:]), reads=[po], writes=[o_])
                    kb.dma("act", S["ob"][r0:r0 + 128, h * 512:(h + 1) * 512], o_[:], o_, reads=[o_])
                else:
                    g_, ol = gc[i % NRG], obl[i % NRG]
                    kb.dma("sp", g_[:], S["g"][r0:r0 + 128, h * 512:(h + 1) * 512], g_, writes=[g_])
                    kb.dma("sp", ol[:], S["ob"][r0:r0 + 128, h * 512:(h + 1) * 512], ol, writes=[ol])
                    for dc in range(2):
                        kb.op("pe", lambda e, dc=dc: e.matmul(psS[:, 0:128], k_[:, dc, :], q_[:, dc, :],
                                                              start=(dc == 0), stop=(dc == 1)),
                              reads=[k_, q_], writes=[psS])
                    s_ = sm[i % 4]
                    kb.op("dve", lambda e: e.tensor_tensor(out=s_[:], in0=psS[:, 0:128],
                                                           in1=DT[:, h * 128:(h + 1) * 128], op=ALU.mult),
                          reads=[psS, DT], writes=[s_])
                    kb.op("pe", lambda e: e.matmul(po[:], s_[:], v_[:], start=True, stop=False),
                          reads=[s_, v_], writes=[po])
                    for dc in range(2):
                        kb.op("pe", lambda e, dc=dc: e.matmul(po[:], qq[:, dc, :], Rb[:, dc, :],
                                                              start=False, stop=(dc == 1)),
                              reads=[qq, Rb], writes=[po])
                for dc in range(2):
                    pk = psKV[slot * 2 + dc]
                    kb.op("pe", lambda e, dc=dc, pk=pk: e.matmul(pk[:], ks_[:, dc * 128:(dc + 1) * 128], v_[:],
                                                                 start=True, stop=True),
                          reads=[ks_, v_], writes=[pk])
                    kb.op("dve", lambda e, dc=dc, pk=pk: e.scalar_tensor_tensor(
                        out=Rs[:, dc, :], in0=Rs[:, dc, :], scalar=cd[:, col:col + 1], in1=pk[:],
                        op0=ALU.mult, op1=ALU.add), reads=[Rs, cd, pk], writes=[Rs])
                kb.op("act", lambda e: e.copy(out=Rb[:], in_=Rs[:]), reads=[Rs], writes=[Rb])
                if direction == "f":
                    o_ = ot[i % 4]
                    s4 = st[i % 4]
                    kb.op("dve", lambda e: e.tensor_tensor(out=o_[:], in0=po[:], in1=ol[:], op=ALU.add),
                          reads=[po, ol], writes=[o_])
                    kb.op("act", lambda e: e.activation(out=junk[:], in_=o_[:], func=AF.Copy,
                                                         accum_out=s4[:, 0:1]),
                          reads=[o_], writes=[junk, s4])
                    kb.op("act", lambda e: e.activation(out=junk[:], in_=o_[:], func=AF.Square,
                                                         accum_out=s4[:, 1:2]),
                          reads=[o_], writes=[junk, s4])
                    kb.op("dve", lambda e: e.tensor_scalar(out=s4[:, 2:3], in0=s4[:, 0:1], scalar1=1.0 / RDV,
                                                           scalar2=None, op0=ALU.mult),
                          reads=[s4], writes=[s4])
                    kb.op("dve", lambda e: e.tensor_tensor(out=s4[:, 3:4], in0=s4[:, 2:3], in1=s4[:, 2:3],
                                                           op=ALU.mult), reads=[s4], writes=[s4])
                    kb.op("dve", lambda e: e.scalar_tensor_tensor(out=s4[:, 4:5], in0=s4[:, 1:2],
                                                                  scalar=1.0 / RDV, in1=s4[:, 3:4],
                                                                  op0=ALU.mult, op1=ALU.subtract),
                          reads=[s4], writes=[s4])
                    kb.op("dve", lambda e: e.tensor_scalar(out=s4[:, 5:6], in0=s4[:, 4:5], scalar1=EPS,
                                                           scalar2=None, op0=ALU.add),
                          reads=[s4], writes=[s4])
                    kb.op("act", lambda e: e.sqrt(out=s4[:, 6:7], in_=s4[:, 5:6]), reads=[s4], writes=[s4])
                    kb.op("dve", lambda e: e.reciprocal(out=s4[:, 5:6], in_=s4[:, 6:7]), reads=[s4], writes=[s4])
                    n_ = on[i % 4]
                    kb.op("dve", lambda e: e.tensor_scalar(out=n_[:], in0=o_[:], scalar1=s4[:, 2:3],
                                                           scalar2=s4[:, 5:6], op0=ALU.subtract, op1=ALU.mult),
                          reads=[o_, s4], writes=[n_])
                    m_ = ogm[i % 4]
                    kb.op("pool", lambda e: e.tensor_tensor(out=m_[:], in0=n_[:], in1=g_[:], op=ALU.mult),
                          reads=[n_, g_], writes=[m_])
                    return (m_, ogT[i % 4], h, r0)
                return None

            def epilogue(p):
                m_, gT, h, r0 = p
                for b4 in range(4):
                    kb.op("pe", lambda e, b4=b4: e.transpose(psG[:, 512 + b4 * 128:512 + (b4 + 1) * 128],
                                                             m_[:, b4 * 128:(b4 + 1) * 128], idb[:]),
                          reads=[m_, idb], writes=[psG])
                kb.op("act", lambda e: e.copy(out=gT[:], in_=psG[:, 512:1024].rearrange("p (a b) -> p a b", b=128)),
                      reads=[psG], writes=[gT])
                kb.dma("act", S["ogT"][h * 4:(h + 1) * 4, :, r0:r0 + 128].rearrange("k p t -> p k t"),
                       gT[:], gT, reads=[gT])

            for direction in ("b", "f"):
                if direction == "f":
                    kb.barrier()
                order = self.fwd[::-1] if direction == "b" else self.fwd
                for h0 in range(0, RH, NH):
                    for slot in range(NH):
                        kb.op("pool", lambda e, slot=slot: e.memset(Rst[slot][:], 0.0), writes=[Rst[slot]])
                        kb.op("pool", lambda e, slot=slot: e.memset(R16[slot][:], 0.0), writes=[R16[slot]])
                    pend = []
                    for c in order:
                        for slot in range(NH):
                            p = body(direction, h0 + slot, slot, c)
                            if p is not None:
                                pend.append(p)
                            if len(pend) > 2:
                                epilogue(pend.pop(0))
                    while pend:
                        epilogue(pend.pop(0))

    def phase_out(self, l, src, kind, last):
        kb, I, S = self.kb, self.I, self.S
        j = l // 2
        Wo = self.W["ret_wo" if kind == "ret" else "mla_wo"][j]
        KC = 32 if kind == "ret" else 16
        W1, W2 = self.W["mlp_w1"][l], self.W["mlp_w2"][l]
        NT2 = 2 * self.NT
        with kb.scope() as sc:
            idb = self.load_ident(sc)
            g2 = self.load_bcast(sc, "g2", I["norm2_g"][l, :], D)
            gf = self.load_bcast(sc, "gf", I["final_norm"][0, :], D) if last else None
            valid = self.load_const(sc, "validc", I["valid"][:, :], [128, self.TT])
            hb = sc.sb("hb", [128, 4, D], F32, dma=True)
            big = sc.sb("big", [128, 64, 512], BF16, dma=True)
            aT = sc.sb("aT2", [128, 16, 512], BF16)
            atm = sc.ring("atm2", 2, [128, D], BF16)
            junk = sc.sb("junk3", [128, D], F32)
            ssq = sc.sb("ssq3", [128, 1], F32)
            rtmp = sc.sb("rtmp", [128, 512], F32)
            rstd = sc.sb("rstd3", [128, 1], F32)
            w1s = sc.ring("w1s", 2, [128, 16, 256], BF16, dma=True)
            w2s = sc.ring("w2s", 2, [128, 16, 512], BF16, dma=True)
            yo = sc.ring("yo", 1, [128, D], F32, dma=True) if last else None
            pst = sc.psring("pst2", 2, [128, 1024], BF16)
            ps1 = sc.psring("ps1", 2, [128, 512], F32)
            psA = sc.psring("psA", 4, [128, 512], F32)
            ctr = [0]
            n1 = n2 = ny = 0
            for gi, (t0, nt) in enumerate(self.groups):
                N = nt * 128
                c0 = t0 * 128
                kb.dma("sp", hb[:, 0:nt, :], src[c0:c0 + N, :].rearrange("(t p) c -> p t c", p=128), hb, writes=[hb])
                kb.dma("sp", big[:, 0:KC, 0:N], S["ogT"][0:KC, :, c0:c0 + N].rearrange("k p t -> p k t"),
                       big, writes=[big])
                for cg in range(4):
                    for half in range(KC // 16):
                        w = w2s[n2 % 2]
                        n2 += 1
                        kb.dma("sp", w[:], Wo[half * 2048:(half + 1) * 2048, cg * 512:(cg + 1) * 512]
                               .rearrange("(k p) c -> p k c", p=128), w, writes=[w])
                        for t in range(nt):
                            for kc in range(16):
                                kk = half * 16 + kc
                                kb.op("pe", lambda e, t=t, kc=kc, kk=kk: e.matmul(
                                    psA[t][:], big[:, kk, t * 128:(t + 1) * 128], w[:, kc, :],
                                    start=(kk == 0), stop=(kk == KC - 1)), reads=[big, w], writes=[psA[t]])
                    for t in range(nt):
                        kb.op("dve", lambda e, t=t: e.tensor_tensor(
                            out=hb[:, t, cg * 512:(cg + 1) * 512], in0=psA[t][:],
                            in1=hb[:, t, cg * 512:(cg + 1) * 512], op=ALU.add),
                            reads=[psA[t], hb], writes=[hb])
                for t in range(nt):
                    a = atm[(gi * 4 + t) % 2]
                    self.rmsnorm_tile((junk, ssq, rstd), hb, hb[:, t, :], g2, a, a[:], D)
                    self.transpose_blocks(a, lambda i: a[:, i * 128:(i + 1) * 128], 16, idb, pst, aT,
                                          lambda i0, n: aT[:, i0:i0 + n, t * 128:(t + 1) * 128], ctr)
                for fs in range(32):
                    w = w1s[n1 % 2]
                    n1 += 1
                    kb.dma("sp", w[:], W1[:, fs * 256:(fs + 1) * 256].rearrange("(k p) c -> p k c", p=128),
                           w, writes=[w])
                    for f2 in range(2):
                        f = fs * 2 + f2
                        pp = ps1[f % 2]
                        for kc in range(16):
                            kb.op("pe", lambda e, kc=kc, pp=pp, f2=f2: e.matmul(
                                pp[:, 0:N], w[:, kc, f2 * 128:(f2 + 1) * 128], aT[:, kc, 0:N],
                                start=(kc == 0), stop=(kc == 15)), reads=[w, aT], writes=[pp])
                        if f % 2 == 0:
                            kb.op("dve", lambda e, pp=pp, f=f: e.tensor_scalar(
                                out=rtmp[:, 0:N], in0=pp[:, 0:N], scalar1=0.0, scalar2=None,
                                op0=ALU.max), reads=[pp], writes=[rtmp])
                            kb.op("pool", lambda e, f=f: e.tensor_tensor(
                                out=big[:, f, 0:N], in0=rtmp[:, 0:N], in1=rtmp[:, 0:N], op=ALU.mult),
                                reads=[rtmp], writes=[big])
                        else:
                            kb.op("act", lambda e, pp=pp, f=f: e.activation(
                                out=junk[:, 0:N], in_=pp[:, 0:N], func=AF.Relu), reads=[pp], writes=[junk])
                            kb.op("pool", lambda e, f=f: e.tensor_tensor(
                                out=big[:, f, 0:N], in0=junk[:, 0:N], in1=junk[:, 0:N], op=ALU.mult),
                                reads=[junk], writes=[big])
                for cg in range(4):
                    for qk in range(4):
                        w = w2s[n2 % 2]
                        n2 += 1
                        kb.dma("sp", w[:], W2[qk * 2048:(qk + 1) * 2048, cg * 512:(cg + 1) * 512]
                               .rearrange("(k p) c -> p k c", p=128), w, writes=[w])
                        for t in range(nt):
                            for kc in range(16):
                                kk = qk * 16 + kc
                                kb.op("pe", lambda e, t=t, kc=kc, kk=kk: e.matmul(
                                    psA[t][:], big[:, kk, t * 128:(t + 1) * 128], w[:, kc, :],
                                    start=(kk == 0), stop=(kk == 63)), reads=[big, w], writes=[psA[t]])
                    for t in range(nt):
                        kb.op("dve", lambda e, t=t: e.tensor_tensor(
                            out=hb[:, t, cg * 512:(cg + 1) * 512], in0=psA[t][:],
                            in1=hb[:, t, cg * 512:(cg + 1) * 512], op=ALU.add),
                            reads=[psA[t], hb], writes=[hb])
                for t in range(nt):
                    tile = t0 + t
                    r0 = tile * 128
                    if tile >= NT2 and not last:
                        kb.op("dve", lambda e, t=t, tile=tile: e.tensor_scalar(
                            out=hb[:, t, :], in0=hb[:, t, :], scalar1=valid[:, tile:tile + 1], scalar2=None,
                            op0=ALU.mult), reads=[hb, valid], writes=[hb])
                    if not last:
                        kb.dma("act", S["h"][r0:r0 + 128, :], hb[:, t, :], hb, reads=[hb])
                    elif tile < NT2:
                        y = yo[0]
                        ny += 1
                        self.rmsnorm_tile((junk, ssq, rstd), hb, hb[:, t, :], gf, y, y[:], D)
                        kb.dma("act", self.out[r0:r0 + 128, :], y[:], y, reads=[y])

    def phase_M1(self, l, src):
        kb, I, S = self.kb, self.I, self.S
        j = l // 2
        Wqa, Wqb, Wkva, Wkvb = (self.W[n][j] for n in ("mla_wq_a", "mla_wq_b", "mla_wkv_a", "mla_wkv_b"))
        with kb.scope() as sc:
            idb = self.load_ident(sc)
            gbc = self.load_bcast(sc, "gbcm", I["norm1_g"][l, :], D)
            gq = self.load_bcast(sc, "gq", I["mla_q_norm"][j, :], MQL)
            gkv = self.load_bcast(sc, "gkv", I["mla_kv_norm"][j, :], MKL)
            wqa = self.load_const(sc, "wqa", Wqa.rearrange("(k p) c -> p k c", p=128), [128, 16, MQL], BF16)
            wkva = self.load_const(sc, "wkva", Wkva.rearrange("(k p) c -> p k c", p=128), [128, 16, MKL + MROPE], BF16)
            wqb = self.load_const(sc, "wqb", Wqb.rearrange("(k p) c -> p k c", p=128), [128, 4, MH * 192], BF16)
            wkvb = self.load_const(sc, "wkvb", Wkvb.rearrange("(k p) c -> p k c", p=128), [128, 4, MH * 256], BF16)
            xr = sc.ring("xrm", 2, [128, D], F32, dma=True)
            atm = sc.ring("atmm", 1, [128, D], BF16)
            junk = sc.sb("junkm", [128, D], F32)
            ssq = sc.sb("ssqm", [128, 1], F32)
            rstd = sc.sb("rstdm", [128, 1], F32)
            aT = sc.sb("aTm", [128, 16, 512], BF16)
            cqT = sc.sb("cqT", [128, 4, 512], BF16)
            ckvT = sc.sb("ckvT", [128, 4, 512], BF16)
            lat = sc.ring("lat", 1, [128, 512], F32)
            latn = sc.ring("latn", 2, [128, 512], BF16)
            csm = sc.ring("csm", 2, [128, 2, 256], F32, dma=True)
            krr = sc.ring("krr", 2, [128, 64], F32)
            krt = sc.ring("krt", 2, [128, 4, 32], F32)
            krb = sc.ring("krb", 2, [128, 128], BF16)
            krTs = sc.sb("krTs", [128, 512], BF16, dma=True)
            qrr = sc.ring("qrr", 2, [128, 512], F32)
            qrt = sc.ring("qrt", 1, [128, 4, 256], F32)
            qrb = sc.ring("qrb", 2, [128, 1024], BF16)
            qrTs = sc.ring("qrTs", 2, [128, 8, 128], BF16, dma=True)
            fo = sc.ring("fo", 2, [128, 512], BF16, dma=True)
            vo = sc.ring("vo", 1, [128, 2048], BF16, dma=True)
            pst = sc.psring("pstm", 2, [128, 1024], BF16)
            psB = sc.psring("psB", 2, [128, 512], F32)
            psC = sc.psring("psC", 2, [128, 512], F32)
            psD = sc.ps("psD", [128, 512], F32)
            ctr = [0]
            nx = nb = nc_ = nf = 0
            for gi, (t0, nt) in enumerate(self.groups):
                N = nt * 128
                c0 = t0 * 128
                for t in range(nt):
                    x, a = xr[nx % 2], atm[0]
                    r0 = (t0 + t) * 128
                    cm = csm[nx % 2]
                    nx += 1
                    kb.dma("sp", x[:], src[r0:r0 + 128, :], x, writes=[x])
                    kb.dma("sp", cm[:, 0, :], I["cosM"][r0:r0 + 128, :], cm, writes=[cm])
                    kb.dma("sp", cm[:, 1, :], I["sinM"][r0:r0 + 128, :], cm, writes=[cm])
                    self.rmsnorm_tile((junk, ssq, rstd), x, x[:], gbc, a, a[:], D)
                    self.transpose_blocks(a, lambda i: a[:, i * 128:(i + 1) * 128], 16, idb, pst, aT,
                                          lambda i0, n: aT[:, i0:i0 + n, t * 128:(t + 1) * 128], ctr)
                    for which in ("q", "kv"):
                        wa = wqa if which == "q" else wkva
                        pp = psB[nb % 2]
                        nb += 1
                        for kc in range(16):
                            kb.op("pe", lambda e, kc=kc, pp=pp, wa=wa: e.matmul(
                                pp[:], aT[:, kc, t * 128:(t + 1) * 128], wa[:, kc, 0:512],
                                start=(kc == 0), stop=(kc == 15)), reads=[aT, wa], writes=[pp])
                        la, ln_ = lat[0], latn[nb % 2]
                        kb.op("act", lambda e, pp=pp, la=la: e.copy(out=la[:], in_=pp[:]), reads=[pp], writes=[la])
                        self.rmsnorm_tile((junk, ssq, rstd), la, la[:], gq if which == "q" else gkv, ln_, ln_[:], 512)
                        dstT = cqT if which == "q" else ckvT
                        self.transpose_blocks(ln_, lambda i, ln_=ln_: ln_[:, i * 128:(i + 1) * 128], 4, idb, pst, dstT,
                                              lambda i0, n, dstT=dstT: dstT[:, i0:i0 + n, t * 128:(t + 1) * 128], ctr)
                    for kc in range(16):
                        kb.op("pe", lambda e, kc=kc: e.matmul(
                            psD[:, 0:64], aT[:, kc, t * 128:(t + 1) * 128], wkva[:, kc, 512:576],
                            start=(kc == 0), stop=(kc == 15)), reads=[aT, wkva], writes=[psD])
                    kr, kt, kbf = krr[nx % 2], krt[nx % 2], krb[nx % 2]
                    kb.op("act", lambda e: e.copy(out=kr[:], in_=psD[:, 0:64]), reads=[psD], writes=[kr])
                    cosv, sinv = cm[:, 0, 0:32], cm[:, 1, 0:32]
                    for ti, (xa, tb) in enumerate(((kr[:, 0:32], cosv), (kr[:, 32:64], sinv),
                                                   (kr[:, 0:32], sinv), (kr[:, 32:64], cosv))):
                        kb.op("dve", lambda e, ti=ti, xa=xa, tb=tb: e.tensor_tensor(
                            out=kt[:, ti, :], in0=xa, in1=tb, op=ALU.mult), reads=[kr, cm], writes=[kt])
                    for dup in range(2):
                        kb.op("dve", lambda e, dup=dup: e.tensor_tensor(
                            out=kbf[:, dup * 64:dup * 64 + 32], in0=kt[:, 0, :], in1=kt[:, 1, :], op=ALU.subtract),
                            reads=[kt], writes=[kbf])
                        kb.op("dve", lambda e, dup=dup: e.tensor_tensor(
                            out=kbf[:, dup * 64 + 32:dup * 64 + 64], in0=kt[:, 2, :], in1=kt[:, 3, :], op=ALU.add),
                            reads=[kt], writes=[kbf])
                    self.transpose_blocks(kbf, lambda i: kbf[:, :], 1, idb, pst, krTs,
                                          lambda i0, n: krTs[:, t * 128:(t + 1) * 128].rearrange("p (a b) -> p a b", a=1), ctr)
                    qb_ = qrb[nx % 2]
                    for hf in range(2):
                        pp = psC[nc_ % 2]
                        nc_ += 1
                        rhs_w = lambda kc: wqb[:, kc, hf * 8 * 192:(hf + 1) * 8 * 192].rearrange(
                            "p (h d) -> p h d", d=192)[:, :, 128:192]
                        for kc in range(4):
                            kb.op("pe", lambda e, kc=kc, pp=pp, rhs_w=rhs_w: e.matmul(
                                pp[:], cqT[:, kc, t * 128:(t + 1) * 128], rhs_w(kc),
                                start=(kc == 0), stop=(kc == 3)), reads=[cqT, wqb], writes=[pp])
                        qr, qt = qrr[nc_ % 2], qrt[0]
                        kb.op("act", lambda e, pp=pp, qr=qr: e.copy(out=qr[:], in_=pp[:]), reads=[pp], writes=[qr])
                        q3 = qr[:].rearrange("p (h d) -> p h d", d=64)
                        c3 = cm[:, 0, :].rearrange("p (h d) -> p h d", d=32)
                        s3 = cm[:, 1, :].rearrange("p (h d) -> p h d", d=32)
                        for ti, (xa, tb) in enumerate(((q3[:, :, 0:32], c3), (q3[:, :, 32:64], s3),
                                                       (q3[:, :, 0:32], s3), (q3[:, :, 32:64], c3))):
                            kb.op("dve", lambda e, ti=ti, xa=xa, tb=tb, qt=qt: e.tensor_tensor(
                                out=qt[:, ti, :].rearrange("p (h d) -> p h d", d=32), in0=xa, in1=tb, op=ALU.mult),
                                reads=[qr, cm], writes=[qt])
                        qo3 = qb_[:, hf * 512:(hf + 1) * 512].rearrange("p (h d) -> p h d", d=64)
                        kb.op("pool", lambda e, qt=qt, qo3=qo3: e.tensor_tensor(
                            out=qo3[:, :, 0:32], in0=qt[:, 0, :].rearrange("p (h d) -> p h d", d=32),
                            in1=qt[:, 1, :].rearrange("p (h d) -> p h d", d=32), op=ALU.subtract),
                            reads=[qt], writes=[qb_])
                        kb.op("pool", lambda e, qt=qt, qo3=qo3: e.tensor_tensor(
                            out=qo3[:, :, 32:64], in0=qt[:, 2, :].rearrange("p (h d) -> p h d", d=32),
                            in1=qt[:, 3, :].rearrange("p (h d) -> p h d", d=32), op=ALU.add),
                            reads=[qt], writes=[qb_])
                    qT_ = qrTs[nx % 2]
                    self.transpose_blocks(qb_, lambda i: qb_[:, i * 128:(i + 1) * 128], 8, idb, pst, qT_,
                                          lambda i0, n: qT_[:, i0:i0 + n, :], ctr)
                    kb.dma("act", S["qrT"][:, :, r0:r0 + 128].rearrange("k p t -> p k t"), qT_[:], qT_, reads=[qT_])
                    v_ = vo[0]
                    for q4 in range(4):
                        pp = psC[nc_ % 2]
                        nc_ += 1
                        rhs_w = lambda kc: wkvb[:, kc, q4 * 4 * 256:(q4 + 1) * 4 * 256].rearrange(
                            "p (h d) -> p h d", d=256)[:, :, 128:256]
                        for kc in range(4):
                            kb.op("pe", lambda e, kc=kc, pp=pp, rhs_w=rhs_w: e.matmul(
                                pp[:], ckvT[:, kc, t * 128:(t + 1) * 128], rhs_w(kc),
                                start=(kc == 0), stop=(kc == 3)), reads=[ckvT, wkvb], writes=[pp])
                        kb.op("act", lambda e, pp=pp, q4=q4: e.copy(out=v_[:, q4 * 512:(q4 + 1) * 512], in_=pp[:]),
                              reads=[pp], writes=[v_])
                    kb.dma("act", S["vm"][r0:r0 + 128, :], v_[:], v_, reads=[v_])
                kb.dma("act", S["krT"][:, c0:c0 + N], krTs[0:64, 0:N], krTs, reads=[krTs])
                for which in ("q", "k"):
                    for h in range(MH):
                        pp = psB[nb % 2]
                        nb += 1
                        if which == "q":
                            wsl = lambda kc: wqb[:, kc, h * 192:h * 192 + 128]
                            srcT, wb_, dst = cqT, wqb, S["qnT"]
                        else:
                            wsl = lambda kc: wkvb[:, kc, h * 256:h * 256 + 128]
                            srcT, wb_, dst = ckvT, wkvb, S["knT"]
                        for kc in range(4):
                            kb.op("pe", lambda e, kc=kc, pp=pp, wsl=wsl, srcT=srcT: e.matmul(
                                pp[:, 0:N], wsl(kc), srcT[:, kc, 0:N], start=(kc == 0), stop=(kc == 3)),
                                reads=[srcT, wb_], writes=[pp])
                        f_ = fo[nf % 2]
                        if nf % 2 == 0:
                            kb.op("act", lambda e, pp=pp, f_=f_: e.copy(out=f_[:, 0:N], in_=pp[:, 0:N]),
                                  reads=[pp], writes=[f_])
                        else:
                            kb.op("dve", lambda e, pp=pp, f_=f_: e.tensor_copy(out=f_[:, 0:N], in_=pp[:, 0:N]),
                                  reads=[pp], writes=[f_])
                        nf += 1
                        kb.dma("act", dst[h, :, c0:c0 + N], f_[:, 0:N], f_, reads=[f_])

    def phase_M2(self, l):
        kb, I, S = self.kb, self.I, self.S
        TT, TOK, NT = self.TT, self.TOK, self.NT
        scale = float((MNOPE + MROPE) ** -0.5)
        NP = TT // 2
        with kb.scope() as sc:
            kbias = self.load_const(sc, "kbiasc", I["kbias"][:, :], [128, TT * 2])
            ones = sc.sb("ones", [128, 128], F32)
            kb.op("pool", lambda e: e.memset(ones[:], 1.0), writes=[ones])
            accD = sc.ring("accD", 2, [128, 2, 512], F32)
            accP = sc.ring("accP", 2, [128, 2, 512], F32)
            krT = sc.sb("krT2", [128, TOK], BF16, dma=True)
            kb.dma("sp", krT[0:64, :], S["krT"][:, :], krT, writes=[krT])
            kb.dma("sp", krT[64:128, :], S["krT"][:, :], krT, writes=[krT])
            knT = sc.ring("knT", 2, [128, TOK], BF16, dma=True)
            vh = sc.ring("vh", 2, [128, TT, 128], BF16, dma=True)
            qn = sc.ring("qn", 2, [128, 512], BF16, dma=True)
            qrE = sc.ring("qrE", 2, [128, 512], BF16, dma=True)
            qrO = sc.ring("qrO", 2, [128, 512], BF16, dma=True)
            for b_ in qrE + qrO:
                kb.op("pool", lambda e, b_=b_: e.memset(b_[:], 0.0), writes=[b_])
            NPT = 4
            pT = sc.ring("pT", NPT, [128, 2, 512], BF16)
            rinv = sc.ring("rinv", 2, [128, 512], F32)
            oo = sc.ring("oo", 2, [128, 512], BF16, dma=True)
            NPS = 2
            psS = sc.psring("psSm", NPS, [128, 2, 512], F32)
            psO = sc.psring("psOm", 2, [128, 512], F32)
            psL = sc.psring("psLm", 2, [128, 512], F32)
            pend = []
            it = 0
            ng = 0
            for h in range(MH):
                kn, v_ = knT[h % 2], vh[h % 2]
                kb.dma("sp", kn[:], S["knT"][h, :, :], kn, writes=[kn])
                for tb in range(0, TT, 8):
                    te = min(TT, tb + 8)
                    kb.dma("sp", v_[:, tb:te, :],
                           S["vm"][tb * 128:te * 128, h * 128:(h + 1) * 128].rearrange("(t p) c -> p t c", p=128),
                           v_, writes=[v_])
                hp = (h % 2) * 64
                for (t0, nt, seg) in self.qgroups:
                    N = nt * 128
                    c0 = t0 * 128
                    qn_, qr_ = qn[ng % 2], (qrE if h % 2 == 0 else qrO)[ng % 2]
                    aD, aP = accD[ng % 2], accP[ng % 2]
                    po, pl = psO[ng % 2], psL[ng % 2]
                    kb.dma("sp", qn_[:, 0:N], S["qnT"][h, :, c0:c0 + N], qn_, writes=[qn_])
                    kb.dma("sp", qr_[hp:hp + 64, 0:N], S["qrT"][h // 2, hp:hp + 64, c0:c0 + N], qr_, writes=[qr_])

                    def score(j, slot):
                        ps = psS[slot % NPS]
                        for b2 in range(2):
                            kt = 2 * j + b2
                            kb.op("pe", lambda e: e.matmul(ps[:, b2, 0:N], kn[:, kt * 128:(kt + 1) * 128], qn_[:, 0:N],
                                                           start=True, stop=False), reads=[kn, qn_], writes=[ps])
                            kb.op("pe", lambda e: e.matmul(ps[:, b2, 0:N], krT[:, kt * 128:(kt + 1) * 128],
                                                           qr_[:, 0:N], start=False, stop=True),
                                  reads=[krT, qr_], writes=[ps])

                    score(0, it)
                    score(1, it + 1)
                    for j in range(NP):
                        ps, p_ = psS[it % NPS], pT[it % NPT]
                        kt0 = 2 * j
                        if kt0 + 1 < 2 * NT:
                            bcol = kt0 * 2 + seg
                            kb.op("act", lambda e: e.activation(
                                out=p_[:, :, 0:N], in_=ps[:, :, 0:N], func=AF.Exp,
                                bias=kbias[:, bcol:bcol + 1], scale=scale), reads=[ps, kbias], writes=[p_])
                        else:
                            for b2 in range(2):
                                bcol = (kt0 + b2) * 2 + seg
                                kb.op("act", lambda e: e.activation(
                                    out=p_[:, b2, 0:N], in_=ps[:, b2, 0:N], func=AF.Exp,
                                    bias=kbias[:, bcol:bcol + 1], scale=scale), reads=[ps, kbias], writes=[p_])
                        for b2 in range(2):
                            kt = kt0 + b2
                            kb.op("pe", lambda e: e.matmul(po[:, 0:N], v_[:, kt, :], p_[:, b2, 0:N],
                                                           start=(kt == 0), stop=(kt == TT - 1)),
                                  reads=[v_, p_], writes=[po])
                        if j + 2 < NP:
                            score(j + 2, it + 2)
                        aeng, acc = "dve", aD
                        if j < 1:
                            kb.op(aeng, lambda e: e.tensor_copy(out=acc[:, :, 0:N], in_=p_[:, :, 0:N]),
                                  reads=[p_], writes=[acc])
                        else:
                            kb.op(aeng, lambda e: e.tensor_tensor(
                                out=acc[:, :, 0:N], in0=acc[:, :, 0:N], in1=p_[:, :, 0:N], op=ALU.add),
                                reads=[p_, acc], writes=[acc])
                        it += 1
                        if j == 0 and pend:
                            pend.pop(0)()
                    kb.op("dve", lambda e: e.tensor_tensor(out=aD[:, 0, 0:N], in0=aD[:, 0, 0:N],
                                                           in1=aD[:, 1, 0:N], op=ALU.add),
                          reads=[aD], writes=[aD])

                    def epi(N=N, aD=aD, po=po, pl=pl, ri=rinv[ng % 2], o_=oo[ng % 2], h=h, c0=c0):
                        kb.op("pe", lambda e: e.matmul(pl[:, 0:N], ones[:], aD[:, 0, 0:N], start=True, stop=True),
                              reads=[ones, aD], writes=[pl])
                        kb.op("dve", lambda e: e.reciprocal(out=ri[:, 0:N], in_=pl[:, 0:N]), reads=[pl], writes=[ri])
                        kb.op("dve", lambda e: e.tensor_tensor(out=o_[:, 0:N], in0=po[:, 0:N], in1=ri[:, 0:N],
                                                               op=ALU.mult), reads=[po, ri], writes=[o_])
                        kb.dma("pool", S["ogT"][h, :, c0:c0 + N], o_[:, 0:N], o_, reads=[o_])

                    pend.append(epi)
                    ng += 1
            while pend:
                pend.pop(0)()


def _tables(NT, kind):
    TT = 2 * NT + 2
    TOK = TT * 128
    S = NT * 128
    pos = np.zeros(TOK, np.float64)
    valid = np.zeros(TOK, np.float64)
    for seg in range(2):
        base = seg * S
        off = NMETA + (base if kind == "prompt" else 0)
        pos[base:base + S] = off + np.arange(S)
        valid[base:base + S] = 1
    for seg in range(2):
        r = (2 * NT + seg) * 128 + 112
        if seg == 0 or kind == "sample":
            pos[r:r + 16] = np.arange(16)
            valid[r:r + 16] = 1
    f32 = np.float32
    inv_r = 1.0 / (10000.0 ** (np.arange(0, RDK, 2, dtype=np.float64) / RDK))
    ang = (pos[None, :] * inv_r[:, None])
    T = dict(cosRT=np.cos(ang).astype(f32), sinRT=np.sin(ang).astype(f32))
    inv_m = 1.0 / (10000.0 ** (np.arange(0, MROPE, 2, dtype=np.float64) / MROPE))
    angm = pos[:, None] * inv_m[None, :]
    T["cosM"] = np.tile(np.cos(angm), (1, 8)).astype(f32)
    T["sinM"] = np.tile(np.sin(angm), (1, 8)).astype(f32)
    T["ident"] = np.eye(128, dtype=f32)
    hh = np.arange(RH, dtype=np.float64)
    lgf = np.log(1.0 - 2.0 ** (-5.0 - hh))
    lgb = np.log(1.0 - 2.0 ** (-5.5 - hh))
    i = np.arange(128, dtype=np.float64)
    diff = i[:, None] - i[None, :]
    kscale = RDK ** -0.5
    DT = np.zeros((128, RH, 128))
    qdf = np.zeros((128, RH, 2, 128))
    qdb = np.zeros((128, RH, 2, 128))
    for h in range(RH):
        Df = np.where(diff >= 0, np.exp(lgf[h] * np.abs(diff)), 0.0)
        Db = np.where(diff < 0, np.exp(lgb[h] * np.abs(diff)), 0.0)
        DT[:, h, :] = (Df + Db).T * kscale
        qdf[:, h, :, :] = np.exp(lgf[h] * (i + 1.0))[None, None, :]
        qdb[:, h, :, :] = np.exp(lgb[h] * (128 - i))[None, None, :]
    T["DT"] = DT.reshape(128, RH * 128).astype(f32)
    T["qdf"] = qdf.reshape(128, RH * 256).astype(f32)
    T["qdb"] = qdb.reshape(128, RH * 256).astype(f32)
    kdF = np.zeros((128, TT, RH)); kdB = np.zeros((128, TT, RH))
    cdf = np.zeros((128, TT, RH)); cdb = np.zeros((128, TT, RH))
    for h in range(RH):
        kdF[:, :, h] = (np.exp(lgf[h] * (127.0 - i)) * kscale)[:, None]
        kdB[:, :, h] = (np.exp(lgb[h] * i) * kscale)[:, None]
        cdf[:, :, h] = np.exp(lgf[h] * 128)
        cdb[:, :, h] = np.exp(lgb[h] * 128)
    m1 = 2 * NT + 1
    if kind == "prompt":
        kdF[:, m1, :] = 0; kdB[:, m1, :] = 0; cdf[:, m1, :] = 1; cdb[:, m1, :] = 1
    else:
        kdF[:, NT - 1, :] = 0; cdf[:, NT - 1, :] = 0
        kdB[:, m1, :] = 0; cdb[:, m1, :] = 0
    for n, a in (("kdF", kdF), ("kdB", kdB), ("cdf", cdf), ("cdb", cdb)):
        T[n] = a.reshape(128, TT * RH).astype(f32)
    kbias = np.zeros((128, TT, 2))
    v2 = valid.reshape(TT, 128)
    for kt in range(TT):
        sk = 0 if kt < NT else (1 if kt < 2 * NT else kt - 2 * NT)
        for sq in range(2):
            b = np.where(v2[kt] > 0, 0.0, NEG)
            if kind == "sample" and sk != sq:
                b = np.full(128, NEG)
            kbias[:, kt, sq] = b
    T["kbias"] = kbias.reshape(128, TT * 2).astype(f32)
    T["valid"] = np.ascontiguousarray(v2.T).astype(f32)
    return T


def _core_x(NT, kind, seqs, meta):
    TT = 2 * NT + 2
    S = NT * 128
    x = np.zeros((TT * 128, D), np.float32)
    if kind == "prompt":
        x[0:2 * S] = seqs[0]
    else:
        x[0:S] = seqs[0]
        x[S:2 * S] = seqs[1]
    r = 2 * NT * 128 + 112
    x[r:r + 16] = meta
    if kind == "sample":
        r = (2 * NT + 1) * 128 + 112
        x[r:r + 16] = meta
    return x


_CACHE = {}


def run_cores(NT, depth, core_specs, weights, debug=None, trace=False):
    key = (NT, depth, tuple(sorted(debug)) if debug else None)
    if key not in _CACHE:
        _CACHE[key] = Prog(NT, depth, debug=debug).build()
    nc = _CACHE[key]
    NR, NM = (depth + 1) // 2, depth // 2
    common = {}
    f = lambda a: np.ascontiguousarray(np.asarray(a, dtype=np.float32))
    common["norm1_g"] = f(weights["norm1_g"][:depth])
    common["norm2_g"] = f(weights["norm2_g"][:depth])
    common["final_norm"] = f(weights["final_norm"]).reshape(1, D)
    common["mlp_w1"] = f(weights["mlp_w1"][:depth])
    common["mlp_w2"] = f(weights["mlp_w2"][:depth])
    for n in ("ret_wq", "ret_wk", "ret_wv", "ret_wg", "ret_wo"):
        common[n] = f(weights[n][:NR])
    if NM:
        for n in ("mla_wq_a", "mla_wq_b", "mla_wkv_a", "mla_wkv_b", "mla_wo", "mla_q_norm", "mla_kv_norm"):
            common[n] = f(weights[n][:NM])
    tabs = {k: _tables(NT, k) for k in set(s[0] for s in core_specs)}
    meta = f(weights["meta_tokens"])
    in_maps = []
    for kind, seqs in core_specs:
        m = dict(common)
        m.update(tabs[kind])
        m["x_in"] = _core_x(NT, kind, seqs, meta)
        in_maps.append(m)
    res = run_bass_kernel_spmd(nc, in_maps, core_ids=list(range(len(core_specs))), trace=trace)
    return res


def kernel(**inputs):
    NT = 32
    xp = np.asarray(inputs["x_prompt"], dtype=np.float32)
    xs = np.asarray(inputs["x_sample"], dtype=np.float32)
    specs = [("prompt", [xp[b]]) for b in range(4)] + [("sample", [xs[2 * c], xs[2 * c + 1]]) for c in range(4)]
    res = run_cores(NT, 4, specs, inputs)
    outs = [np.asarray(r["out"]) for r in res.results]
    yp = np.stack([outs[b].reshape(8192, D) for b in range(4)], axis=0).astype(np.float32)
    ys = np.concatenate([outs[4 + c].reshape(2, 4096, D) for c in range(4)], axis=0).astype(np.float32)
    return (yp, ys)
```

```python
import numpy as np
import ml_dtypes
from contextlib import ExitStack
import concourse.bass as bass
import concourse.mybir as mybir
from concourse.bass_utils import run_bass_kernel_spmd

F32 = mybir.dt.float32
BF16 = mybir.dt.bfloat16
AF = mybir.ActivationFunctionType
ALU = mybir.AluOpType

D = 2048
DFF = 8192
NMETA = 16
EPS = 1e-6
RH, RDK, RDV = 8, 256, 512
MH, MQL, MKL, MNOPE, MROPE, MV = 16, 512, 512, 128, 64, 128
NEG = -30000.0


class DSem:
    def __init__(self, handle, key):
        self.handle, self.key, self.cnt = handle, key, 0


class Buf:
    def __init__(self, t, name):
        self.t, self.name = t, name
        self.w, self.r = {}, {}
        self.dsem = None

    def __getitem__(self, k):
        return self.t[k]


class KB:
    def __init__(self, nc, es, n_dsem=48):
        self.nc, self.es = nc, es
        self.E = dict(pe=nc.tensor, act=nc.scalar, dve=nc.vector, pool=nc.gpsimd, sp=nc.sync)
        self.sem = {e: es.enter_context(nc.semaphore("s_" + e)) for e in self.E}
        self.semobj = dict(self.sem)
        self.cnt = {e: 0 for e in self.E}
        self.seen = {e: {} for e in self.E}
        self.dsems = []
        for i in range(n_dsem):
            d = DSem(es.enter_context(nc.semaphore("d%d" % i)), "d%d" % i)
            self.dsems.append(d)
            self.semobj[d.key] = d.handle
        self.dfree = list(self.dsems)
        self.dmap = {d.key: d for d in self.dsems}
        self.nwait = 0

    def scope(self):
        return Scope(self)

    def _wait(self, e, key, v):
        d = self.dmap.get(key)
        if d is not None:
            v = max(v, d.cnt)
        if self.seen[e].get(key, 0) >= v:
            return
        self.E[e].wait_ge(self.semobj[key], v)
        self.seen[e][key] = v
        self.nwait += 1

    def _pre(self, e, reads, writes):
        for b in reads:
            for key, v in b.w.items():
                self._wait(e, key, v)
        for b in writes:
            for key, v in b.w.items():
                if key != e:
                    self._wait(e, key, v)
            for key, v in b.r.items():
                if key != e:
                    self._wait(e, key, v)

    def op(self, e, fn, reads=(), writes=()):
        self._pre(e, reads, writes)
        ins = fn(self.E[e])
        self.cnt[e] += 1
        v = self.cnt[e]
        ins.then_inc(self.sem[e], 1)
        for b in reads:
            b.r[e] = v
        for b in writes:
            b.w[e] = v
            b.r = {}
        return ins

    def dma(self, q, out, in_, sbuf, reads=(), writes=()):
        self._pre(q, reads, writes)
        ds = sbuf.dsem
        ins = self.E[q].dma_start(out=out, in_=in_)
        ds.cnt += 16
        ins.then_inc(ds.handle, 16)
        for b in reads:
            b.r[ds.key] = ds.cnt
        for b in writes:
            b.w[ds.key] = ds.cnt
            b.r = {}

    def barrier(self):
        for e in self.E:
            for e2 in self.E:
                if e2 != e and self.cnt[e2]:
                    self._wait(e, e2, self.cnt[e2])
            for d in self.dsems:
                if d.cnt:
                    self._wait(e, d.key, d.cnt)


class Scope:
    _n = 0

    def __init__(self, kb):
        self.kb = kb
        self.es = ExitStack()
        self.held = []
        Scope._n += 1
        self.sfx = "_s%d" % Scope._n

    def __enter__(self):
        self.es.__enter__()
        return self

    def __exit__(self, *a):
        self.kb.barrier()
        for d in self.held:
            self.kb.dfree.append(d)
        return self.es.__exit__(*a)

    def sb(self, name, shape, dt, dma=False):
        b = Buf(self.es.enter_context(self.kb.nc.sbuf_tensor(name + self.sfx, list(shape), dt)), name)
        if dma:
            b.dsem = self.kb.dfree.pop()
            self.held.append(b.dsem)
        return b

    def ps(self, name, shape, dt):
        return Buf(self.es.enter_context(self.kb.nc.psum_tensor(name + self.sfx, list(shape), dt)), name)

    def ring(self, name, n, shape, dt, dma=False):
        return [self.sb("%s%d" % (name, i), shape, dt, dma) for i in range(n)]

    def psring(self, name, n, shape, dt):
        return [self.ps("%s%d" % (name, i), shape, dt) for i in range(n)]


class Prog:
    def __init__(self, NT, depth, debug=False):
        self.NT, self.depth = NT, depth
        self.TT = 2 * NT + 2
        self.TOK = self.TT * 128
        self.debug = debug
        self.groups = [(t, min(4, 2 * NT - t)) for t in range(0, 2 * NT, 4)] + [(2 * NT, 2)]
        self.qgroups = []
        for s in range(2):
            for t in range(s * NT, (s + 1) * NT, 4):
                self.qgroups.append((t, min(4, (s + 1) * NT - t), s))
        self.qgroups += [(2 * NT, 1, 0), (2 * NT + 1, 1, 1)]
        self.fwd = [2 * NT] + list(range(NT)) + [2 * NT + 1] + list(range(NT, 2 * NT))

    def seg_of_tile(self, t):
        NT = self.NT
        if t < NT:
            return 0
        if t < 2 * NT:
            return 1
        return t - 2 * NT

    def build(self):
        nc = bass.Bass("TRN2", target_bir_lowering=False)
        self.nc = nc
        TOK, TT = self.TOK, self.TT
        NR, NM = (self.depth + 1) // 2, self.depth // 2
        self.NR, self.NM = NR, NM

        def inp(name, shape, dt=F32):
            return nc.dram_tensor(name, list(shape), dt, kind="ExternalInput").ap()

        def scr(name, shape, dt):
            kind = "ExternalOutput" if (self.debug and name in self.debug) else "Internal"
            return nc.dram_tensor(name, list(shape), dt, kind=kind).ap()

        dp = self.depth
        I = self.I = {}
        I["x_in"] = inp("x_in", [TOK, D])
        I["norm1_g"] = inp("norm1_g", [dp, D])
        I["norm2_g"] = inp("norm2_g", [dp, D])
        I["final_norm"] = inp("final_norm", [1, D])
        wshapes = dict(
            mlp_w1=[dp, D, DFF], mlp_w2=[dp, DFF, D],
            ret_wq=[NR, D, RH * RDK], ret_wk=[NR, D, RH * RDK], ret_wv=[NR, D, RH * RDV],
            ret_wg=[NR, D, RH * RDV], ret_wo=[NR, RH * RDV, D],
            mla_wq_a=[NM, D, MQL], mla_wq_b=[NM, MQL, MH * (MNOPE + MROPE)],
            mla_wkv_a=[NM, D, MKL + MROPE], mla_wkv_b=[NM, MKL, MH * (MNOPE + MV)],
            mla_wo=[NM, MH * MV, D])
        self.W = {}
        for n, s in wshapes.items():
            if s[0] == 0:
                continue
            I[n] = inp(n, s)
            self.W[n] = scr("wb_" + n, s, BF16)
        if NM:
            I["mla_q_norm"] = inp("mla_q_norm", [NM, MQL])
            I["mla_kv_norm"] = inp("mla_kv_norm", [NM, MKL])
        for n, s in dict(ident=[128, 128], cosRT=[128, TOK], sinRT=[128, TOK],
                         cosM=[TOK, 256], sinM=[TOK, 256], DT=[128, RH * 128],
                         qdf=[128, RH * 256], qdb=[128, RH * 256],
                         kdF=[128, TT * RH], kdB=[128, TT * RH],
                         cdf=[128, TT * RH], cdb=[128, TT * RH],
                         kbias=[128, TT * 2], valid=[128, TT]).items():
            I[n] = inp(n, s)
        self.out = nc.dram_tensor("out", [2 * self.NT * 128, D], F32, kind="ExternalOutput").ap()
        S = self.S = {}
        S["h"] = scr("h", [TOK, D], F32)
        S["qT"] = scr("qT", [RH, 128, 2, TOK], BF16)
        S["kT"] = scr("kT", [RH, 128, 2, TOK], BF16)
        S["v"] = scr("v", [TOK, RH * RDV], BF16)
        S["g"] = scr("g", [TOK, RH * RDV], BF16)
        S["ob"] = scr("ob", [TOK, RH * RDV], F32)
        S["ogT"] = scr("ogT", [32, 128, TOK], BF16)
        S["qnT"] = scr("qnT", [MH, 128, TOK], BF16)
        S["qrT"] = scr("qrT", [MH // 2, 128, TOK], BF16)
        S["knT"] = scr("knT", [MH, 128, TOK], BF16)
        S["krT"] = scr("krT", [64, TOK], BF16)
        S["vm"] = scr("vm", [TOK, MH * MV], BF16)

        with ExitStack() as es:
            kb = self.kb = KB(nc, es)
            self.prologue()
            for l in range(self.depth):
                src = I["x_in"] if l == 0 else S["h"]
                last = (l == self.depth - 1)
                if l % 2 == 0:
                    self.phase_R1(l, src)
                    self.phase_R2(l)
                    self.phase_out(l, src, "ret", last)
                else:
                    self.phase_M1(l, src)
                    self.phase_M2(l)
                    self.phase_out(l, src, "mla", last)
            kb.barrier()
        return nc

    def prologue(self):
        kb, nc = self.kb, self.nc

        def convert(names_layers, tracker):
            for n, l in names_layers:
                dst, src = self.W[n], self.I[n]
                L, R, C = src.shape
                step = max(1, (1 << 21) // C)
                for r0 in range(0, R, step):
                    r1 = min(R, r0 + step)
                    kb.dma("pool", dst[l, r0:r1, :], src[l, r0:r1, :], tracker)

        first = [(n, 0) for n in ("ret_wq", "ret_wk", "ret_wv", "ret_wg")]
        with kb.scope() as sc:
            dummy = sc.sb("cvt_dummy", [128, 8], F32, dma=True)
            convert(first, dummy)
        bg = Buf(None, "cvt_bg")
        bg.dsem = kb.dfree.pop()
        rest = [(n, l) for n in self.W for l in range(self.I[n].shape[0]) if (n, l) not in first]
        order = ["ret_wo", "mlp_w1", "mlp_w2", "mla_wq_a", "mla_wkv_a", "mla_wq_b", "mla_wkv_b", "mla_wo",
                 "ret_wq", "ret_wk", "ret_wv", "ret_wg"]
        rest.sort(key=lambda x: (x[1], order.index(x[0])))
        convert(rest, bg)

    def load_ident(self, sc):
        kb = self.kb
        idf = sc.sb("idf", [128, 128], F32, dma=True)
        idb = sc.sb("idb", [128, 128], BF16)
        kb.dma("sp", idf[:], self.I["ident"][:, :], idf, writes=[idf])
        kb.op("dve", lambda e: e.tensor_copy(out=idb[:], in_=idf[:]), reads=[idf], writes=[idb])
        return idb

    def load_bcast(self, sc, name, row_ap, n):
        kb = self.kb
        b = sc.sb(name, [128, n], F32, dma=True)
        kb.dma("sp", b[:], row_ap.partition_broadcast(128), b, writes=[b])
        return b

    def load_const(self, sc, name, ap, shape, dt=F32):
        kb = self.kb
        b = sc.sb(name, shape, dt, dma=True)
        kb.dma("sp", b[:], ap, b, writes=[b])
        return b

    def rmsnorm_tile(self, sc_bufs, x, xap, gbc, out, outap, n):
        kb = self.kb
        junk, ssq, rstd = sc_bufs
        kb.op("act", lambda e: e.activation(out=junk[:, 0:n], in_=xap, func=AF.Square,
                                             accum_out=ssq[:, 0:1]),
              reads=[x], writes=[junk, ssq])
        kb.op("dve", lambda e: e.tensor_scalar(out=rstd[:, 0:1], in0=ssq[:, 0:1], scalar1=1.0 / n,
                                               scalar2=EPS, op0=ALU.mult, op1=ALU.add),
              reads=[ssq], writes=[rstd])
        kb.op("act", lambda e: e.sqrt(out=rstd[:, 0:1], in_=rstd[:, 0:1]), reads=[rstd], writes=[rstd])
        kb.op("dve", lambda e: e.reciprocal(out=rstd[:, 0:1], in_=rstd[:, 0:1]), reads=[rstd], writes=[rstd])
        kb.op("dve", lambda e: e.scalar_tensor_tensor(out=outap, in0=xap, scalar=rstd[:, 0:1],
                                                      in1=gbc[:, 0:n], op0=ALU.mult, op1=ALU.mult),
              reads=[x, rstd, gbc], writes=[out])

    def transpose_blocks(self, src, src_ap_fn, nblk, idb, pst_ring, dst, dst_ap_fn, ctr, evac=None):
        kb = self.kb
        i = 0
        while i < nblk:
            n = min(8, nblk - i)
            ps = pst_ring[ctr[0] % len(pst_ring)]
            for j in range(n):
                kb.op("pe", lambda e, j=j: e.transpose(ps[:, j * 128:(j + 1) * 128], src_ap_fn(i + j), idb[:]),
                      reads=[src, idb], writes=[ps])
            eng = ("act", "dve")[ctr[0] % 2] if evac is None else evac
            oap = dst_ap_fn(i, n)
            iap = ps[:, 0:n * 128].rearrange("p (a b) -> p a b", b=128)
            if eng == "act":
                kb.op("act", lambda e: e.copy(out=oap, in_=iap), reads=[ps], writes=[dst])
            else:
                kb.op("dve", lambda e: e.tensor_copy(out=oap, in_=iap), reads=[ps], writes=[dst])
            ctr[0] += 1
            i += n

    def phase_R1(self, l, src):
        kb, I, S = self.kb, self.I, self.S
        j = l // 2
        Wq, Wk, Wv, Wg = (self.W[n][j] for n in ("ret_wq", "ret_wk", "ret_wv", "ret_wg"))
        with kb.scope() as sc:
            idb = self.load_ident(sc)
            gbc = self.load_bcast(sc, "gbc", I["norm1_g"][l, :], D)
            xr = sc.ring("xr", 2, [128, D], F32, dma=True)
            atm = sc.ring("atm", 2, [128, D], BF16)
            junk = sc.sb("junk", [128, D], F32)
            ssq = sc.sb("ssq", [128, 1], F32)
            rstd = sc.sb("rstd", [128, 1], F32)
            aT = sc.sb("aT", [128, 16, 512], BF16)
            cs = sc.ring("cs", 2, [128, 2, 512], F32, dma=True)
            wqk = sc.ring("wqk", 2, [128, 16, 256], BF16, dma=True)
            wvg = sc.ring("wvg", 2, [128, 16, 512], BF16, dma=True)
            tmp = sc.ring("rt", 2, [128, 4, 512], F32)
            qko = sc.ring("qko", 2, [128, 2, 512], BF16, dma=True)
            vgo = sc.ring("vgo", 2, [128, 4, 512], BF16, dma=True)
            pst = sc.psring("pst", 2, [128, 1024], BF16)
            psqk = sc.psring("psqk", 4, [128, 512], F32)
            psvg = sc.psring("psvg", 2, [128, 512], F32)
            ctr = [0]
            nx = nqk = nvg = nps = 0
            for gi, (t0, nt) in enumerate(self.groups):
                N = nt * 128
                c0 = t0 * 128
                csb = cs[gi % 2]
                kb.dma("sp", csb[:, 0, 0:N], I["cosRT"][:, c0:c0 + N], csb, writes=[csb])
                kb.dma("sp", csb[:, 1, 0:N], I["sinRT"][:, c0:c0 + N], csb, writes=[csb])
                for t in range(nt):
                    x = xr[nx % 2]
                    a = atm[nx % 2]
                    nx += 1
                    r0 = (t0 + t) * 128
                    kb.dma("sp", x[:], src[r0:r0 + 128, :], x, writes=[x])
                    self.rmsnorm_tile((junk, ssq, rstd), x, x[:], gbc, a, a[:], D)
                    self.transpose_blocks(a, lambda i: a[:, i * 128:(i + 1) * 128], 16, idb, pst, aT,
                                          lambda i0, n: aT[:, i0:i0 + n, t * 128:(t + 1) * 128], ctr)
                for which, Wm, dst in (("q", Wq, S["qT"]), ("k", Wk, S["kT"])):
                    for h in range(RH):
                        w = wqk[nqk % 2]
                        kb.dma("sp", w[:], Wm[:, h * 256:(h + 1) * 256].rearrange("(k p) c -> p k c", p=128),
                               w, writes=[w])
                        p1 = psqk[(nqk % 2) * 2]
                        p2 = psqk[(nqk % 2) * 2 + 1]
                        for c, pp in ((0, p1), (1, p2)):
                            for kc in range(16):
                                kb.op("pe", lambda e, c=c, pp=pp, kc=kc: e.matmul(
                                    pp[:, 0:N], w[:, kc, c * 128:(c + 1) * 128], aT[:, kc, 0:N],
                                    start=(kc == 0), stop=(kc == 15)), reads=[w, aT], writes=[pp])
                        tb = tmp[nqk % 2]
                        ob = qko[nqk % 2]
                        nqk += 1
                        for ti, (pp, ci) in enumerate(((p1, 0), (p2, 1), (p1, 1), (p2, 0))):
                            kb.op("dve", lambda e, ti=ti, pp=pp, ci=ci: e.tensor_tensor(
                                out=tb[:, ti, 0:N], in0=pp[:, 0:N], in1=csb[:, ci, 0:N], op=ALU.mult),
                                reads=[pp, csb], writes=[tb])
                        kb.op("pool", lambda e: e.tensor_tensor(out=ob[:, 0, 0:N], in0=tb[:, 0, 0:N],
                                                                in1=tb[:, 1, 0:N], op=ALU.subtract),
                              reads=[tb], writes=[ob])
                        kb.op("pool", lambda e: e.tensor_tensor(out=ob[:, 1, 0:N], in0=tb[:, 2, 0:N],
                                                                in1=tb[:, 3, 0:N], op=ALU.add),
                              reads=[tb], writes=[ob])
                        kb.dma("act", dst[h, :, :, c0:c0 + N], ob[:, :, 0:N], ob, reads=[ob])
                for which, Wm, dst in (("v", Wv, S["v"]), ("g", Wg, S["g"])):
                    for cg in range(8):
                        w = wvg[nvg % 2]
                        ob = vgo[nvg % 2]
                        nvg += 1
                        kb.dma("sp", w[:], Wm[:, cg * 512:(cg + 1) * 512].rearrange("(k p) c -> p k c", p=128),
                               w, writes=[w])
                        for t in range(nt):
                            pp = psvg[nps % 2]
                            nps += 1
                            for kc in range(16):
                                kb.op("pe", lambda e, pp=pp, kc=kc, t=t: e.matmul(
                                    pp[:], aT[:, kc, t * 128:(t + 1) * 128], w[:, kc, :],
                                    start=(kc == 0), stop=(kc == 15)), reads=[w, aT], writes=[pp])
                            fn = AF.Copy if which == "v" else AF.Silu
                            kb.op("act", lambda e, pp=pp, t=t, fn=fn: e.activation(out=ob[:, t, :], in_=pp[:], func=fn),
                                  reads=[pp], writes=[ob])
                        kb.dma("act", dst[c0:c0 + N, cg * 512:(cg + 1) * 512].rearrange("(t p) c -> p t c", p=128),
                               ob[:, 0:nt, :], ob, reads=[ob])

    def phase_R2(self, l):
        kb, I, S = self.kb, self.I, self.S
        TT = self.TT
        NH = 2
        with kb.scope() as sc:
            idb = self.load_ident(sc)
            DT = self.load_const(sc, "DTc", I["DT"][:, :], [128, RH * 128])
            qdf = self.load_const(sc, "qdfc", I["qdf"][:, :], [128, RH * 256])
            qdb = self.load_const(sc, "qdbc", I["qdb"][:, :], [128, RH * 256])
            kdF = self.load_const(sc, "kdFc", I["kdF"][:, :], [128, TT * RH])
            kdB = self.load_const(sc, "kdBc", I["kdB"][:, :], [128, TT * RH])
            cdf = self.load_const(sc, "cdfc", I["cdf"][:, :], [128, TT * RH])
            cdb = self.load_const(sc, "cdbc", I["cdb"][:, :], [128, TT * RH])
            NRG = 6
            qc = sc.ring("qc", NRG, [128, 2, 128], BF16, dma=True)
            kc_ = sc.ring("kc", NRG, [128, 2, 128], BF16, dma=True)
            vc = sc.ring("vc", NRG, [128, 512], BF16, dma=True)
            gc = sc.ring("gc", NRG, [128, 512], BF16, dma=True)
            obl = sc.ring("obl", NRG, [128, 512], F32, dma=True)
            ks = sc.ring("ks", 4, [128, 256], BF16)
            qs = sc.ring("qs", 4, [128, 2, 128], BF16)
            obs = sc.ring("obs", 4, [128, 512], F32, dma=True)
            Rst = sc.ring("Rst", NH, [128, 2, 512], F32)
            R16 = sc.ring("R16", NH, [128, 2, 512], BF16)
            sm = sc.ring("sm", 4, [128, 128], BF16)
            ot = sc.ring("ot", 4, [128, 512], F32)
            junk = sc.sb("junk2", [128, 512], F32)
            st = sc.ring("st", 4, [128, 8], F32)
            on = sc.ring("on", 4, [128, 512], F32)
            ogm = sc.ring("ogm", 4, [128, 512], BF16)
            ogT = sc.ring("ogTs", 4, [128, 4, 128], BF16, dma=True)
            psT = sc.ps("psT", [128, 1024], BF16)
            psG = Buf(psT.t, "psG")
            psO = sc.psring("psO", 2, [128, 512], F32)
            psKV = sc.psring("psKV", 2 * NH, [128, 512], F32)
            psS = sc.ps("psS", [128, 512], F32)
            cnt = [0]

            def body(direction, h, slot, c):
                kd, cd, qd = (kdB, cdb, qdb) if direction == "b" else (kdF, cdf, qdf)
                Rs, Rb = Rst[slot], R16[slot]
                i = cnt[0]
                cnt[0] += 1
                r0 = c * 128
                q_, k_, v_ = qc[i % NRG], kc_[i % NRG], vc[i % NRG]
                kb.dma("sp", q_[:], S["qT"][h, :, :, r0:r0 + 128], q_, writes=[q_])
                kb.dma("sp", k_[:], S["kT"][h, :, :, r0:r0 + 128], k_, writes=[k_])
                kb.dma("sp", v_[:], S["v"][r0:r0 + 128, h * 512:(h + 1) * 512], v_, writes=[v_])
                col = c * RH + h
                qq = qs[i % 4]
                kb.op("pool", lambda e: e.tensor_tensor(
                    out=qq[:], in0=q_[:], in1=qd[:, h * 256:(h + 1) * 256].rearrange("p (a b) -> p a b", b=128),
                    op=ALU.mult), reads=[q_, qd], writes=[qq])
                for dc in range(2):
                    kb.op("pe", lambda e, dc=dc: e.transpose(psT[:, dc * 128:(dc + 1) * 128], k_[:, dc, :], idb[:]),
                          reads=[k_, idb], writes=[psT])
                ks_ = ks[i % 4]
                kb.op("act", lambda e: e.activation(out=ks_[:], in_=psT[:, 0:256], func=AF.Copy,
                                                     scale=kd[:, col:col + 1]),
                      reads=[psT, kd], writes=[ks_])
                po = psO[i % 2]
                if direction == "b":
                    for dc in range(2):
                        kb.op("pe", lambda e, dc=dc: e.matmul(po[:], qq[:, dc, :], Rb[:, dc, :],
                                                              start=(dc == 0), stop=(dc == 1)),
                              reads=[qq, Rb], writes=[po])
                    o_ = obs[i % 4]
                    kb.op("act", lambda e: e.copy(out=o_[:], in_=po[:]), reads=[po], writes=[o_])
                    kb.dma("act", S["ob"][r0:r0 + 128, h * 512:(h + 1) * 512], o_[:], o_, reads=[o_])
                else:
                    g_, ol = gc[i % NRG], obl[i % NRG]
                    kb.dma("sp", g_[:], S["g"][r0:r0 + 128, h * 512:(h + 1) * 512], g_, writes=[g_])
                    kb.dma("sp", ol[:], S["ob"][r0:r0 + 128, h * 512:(h + 1) * 512], ol, writes=[ol])
                    for dc in range(2):
                        kb.op("pe", lambda e, dc=dc: e.matmul(psS[:, 0:128], k_[:, dc, :], q_[:, dc, :],
                                                              start=(dc == 0), stop=(dc == 1)),
                              reads=[k_, q_], writes=[psS])
                    s_ = sm[i % 4]
                    kb.op("dve", lambda e: e.tensor_tensor(out=s_[:], in0=psS[:, 0:128],
                                                           in1=DT[:, h * 128:(h + 1) * 128], op=ALU.mult),
                          reads=[psS, DT], writes=[s_])
                    kb.op("pe", lambda e: e.matmul(po[:], s_[:], v_[:], start=True, stop=False),
                          reads=[s_, v_], writes=[po])
                    for dc in range(2):
                        kb.op("pe", lambda e, dc=dc: e.matmul(po[:], qq[:, dc, :], Rb[:, dc, :],
                                                              start=False, stop=(dc == 1)),
                              reads=[qq, Rb], writes=[po])
                for dc in range(2):
                    pk = psKV[slot * 2 + dc]
                    kb.op("pe", lambda e, dc=dc, pk=pk: e.matmul(pk[:], ks_[:, dc * 128:(dc + 1) * 128], v_[:],
                                                                 start=True, stop=True),
                          reads=[ks_, v_], writes=[pk])
                    kb.op("dve", lambda e, dc=dc, pk=pk: e.scalar_tensor_tensor(
                        out=Rs[:, dc, :], in0=Rs[:, dc, :], scalar=cd[:, col:col + 1], in1=pk[:],
                        op0=ALU.mult, op1=ALU.add), reads=[Rs, cd, pk], writes=[Rs])
                kb.op("act", lambda e: e.copy(out=Rb[:], in_=Rs[:]), reads=[Rs], writes=[Rb])
                if direction == "f":
                    o_ = ot[i % 4]
                    s4 = st[i % 4]
                    kb.op("dve", lambda e: e.tensor_tensor(out=o_[:], in0=po[:], in1=ol[:], op=ALU.add),
                          reads=[po, ol], writes=[o_])
                    kb.op("act", lambda e: e.activation(out=junk[:], in_=o_[:], func=AF.Copy,
                                                         accum_out=s4[:, 0:1]),
                          reads=[o_], writes=[junk, s4])
                    kb.op("act", lambda e: e.activation(out=junk[:], in_=o_[:], func=AF.Square,
                                                         accum_out=s4[:, 1:2]),
                          reads=[o_], writes=[junk, s4])
                    kb.op("dve", lambda e: e.tensor_scalar(out=s4[:, 2:3], in0=s4[:, 0:1], scalar1=1.0 / RDV,
                                                           scalar2=None, op0=ALU.mult),
                          reads=[s4], writes=[s4])
                    kb.op("dve", lambda e: e.tensor_tensor(out=s4[:, 3:4], in0=s4[:, 2:3], in1=s4[:, 2:3],
                                                           op=ALU.mult), reads=[s4], writes=[s4])
                    kb.op("dve", lambda e: e.scalar_tensor_tensor(out=s4[:, 4:5], in0=s4[:, 1:2],
                                                                  scalar=1.0 / RDV, in1=s4[:, 3:4],
                                                                  op0=ALU.mult, op1=ALU.subtract),
                          reads=[s4], writes=[s4])
                    kb.op("dve", lambda e: e.tensor_scalar(out=s4[:, 5:6], in0=s4[:, 4:5], scalar1=EPS,
                                                           scalar2=None, op0=ALU.add),
                          reads=[s4], writes=[s4])
                    kb.op("act", lambda e: e.sqrt(out=s4[:, 6:7], in_=s4[:, 5:6]), reads=[s4], writes=[s4])
                    kb.op("dve", lambda e: e.reciprocal(out=s4[:, 5:6], in_=s4[:, 6:7]), reads=[s4], writes=[s4])
                    n_ = on[i % 4]
                    kb.op("dve", lambda e: e.tensor_scalar(out=n_[:], in0=o_[:], scalar1=s4[:, 2:3],
                                                           scalar2=s4[:, 5:6], op0=ALU.subtract, op1=ALU.mult),
                          reads=[o_, s4], writes=[n_])
                    m_ = ogm[i % 4]
                    kb.op("pool", lambda e: e.tensor_tensor(out=m_[:], in0=n_[:], in1=g_[:], op=ALU.mult),
                          reads=[n_, g_], writes=[m_])
                    return (m_, ogT[i % 4], h, r0)
                return None

            def epilogue(p):
                m_, gT, h, r0 = p
                for b4 in range(4):
                    kb.op("pe", lambda e, b4=b4: e.transpose(psG[:, 512 + b4 * 128:512 + (b4 + 1) * 128],
                                                             m_[:, b4 * 128:(b4 + 1) * 128], idb[:]),
                          reads=[m_, idb], writes=[psG])
                kb.op("act", lambda e: e.copy(out=gT[:], in_=psG[:, 512:1024].rearrange("p (a b) -> p a b", b=128)),
                      reads=[psG], writes=[gT])
                kb.dma("act", S["ogT"][h * 4:(h + 1) * 4, :, r0:r0 + 128].rearrange("k p t -> p k t"),
                       gT[:], gT, reads=[gT])

            for direction in ("b", "f"):
                if direction == "f":
                    kb.barrier()
                order = self.fwd[::-1] if direction == "b" else self.fwd
                for h0 in range(0, RH, NH):
                    for slot in range(NH):
                        kb.op("pool", lambda e, slot=slot: e.memset(Rst[slot][:], 0.0), writes=[Rst[slot]])
                        kb.op("pool", lambda e, slot=slot: e.memset(R16[slot][:], 0.0), writes=[R16[slot]])
                    pend = []
                    for c in order:
                        for slot in range(NH):
                            p = body(direction, h0 + slot, slot, c)
                            if p is not None:
                                pend.append(p)
                            if len(pend) > 2:
                                epilogue(pend.pop(0))
                    while pend:
                        epilogue(pend.pop(0))

    def phase_out(self, l, src, kind, last):
        kb, I, S = self.kb, self.I, self.S
        j = l // 2
        Wo = self.W["ret_wo" if kind == "ret" else "mla_wo"][j]
        KC = 32 if kind == "ret" else 16
        W1, W2 = self.W["mlp_w1"][l], self.W["mlp_w2"][l]
        NT2 = 2 * self.NT
        with kb.scope() as sc:
            idb = self.load_ident(sc)
            g2 = self.load_bcast(sc, "g2", I["norm2_g"][l, :], D)
            gf = self.load_bcast(sc, "gf", I["final_norm"][0, :], D) if last else None
            valid = self.load_const(sc, "validc", I["valid"][:, :], [128, self.TT])
            hb = sc.sb("hb", [128, 4, D], F32, dma=True)
            big = sc.sb("big", [128, 64, 512], BF16, dma=True)
            aT = sc.sb("aT2", [128, 16, 512], BF16)
            atm = sc.ring("atm2", 2, [128, D], BF16)
            junk = sc.sb("junk3", [128, D], F32)
            ssq = sc.sb("ssq3", [128, 1], F32)
            rtmp = sc.sb("rtmp", [128, 512], F32)
            rstd = sc.sb("rstd3", [128, 1], F32)
            w1s = sc.ring("w1s", 2, [128, 16, 256], BF16, dma=True)
            w2s = sc.ring("w2s", 2, [128, 16, 512], BF16, dma=True)
            yo = sc.ring("yo", 1, [128, D], F32, dma=True) if last else None
            pst = sc.psring("pst2", 2, [128, 1024], BF16)
            ps1 = sc.psring("ps1", 2, [128, 512], F32)
            psA = sc.psring("psA", 4, [128, 512], F32)
            ctr = [0]
            n1 = n2 = ny = 0
            for gi, (t0, nt) in enumerate(self.groups):
                N = nt * 128
                c0 = t0 * 128
                kb.dma("sp", big[:, 0:KC, 0:N], S["ogT"][0:KC, :, c0:c0 + N].rearrange("k p t -> p k t"),
                       big, writes=[big])
                hb_loaded = False
                for cg in range(4):
                    for half in range(KC // 16):
                        w = w2s[n2 % 2]
                        n2 += 1
                        kb.dma("sp", w[:], Wo[half * 2048:(half + 1) * 2048, cg * 512:(cg + 1) * 512]
                               .rearrange("(k p) c -> p k c", p=128), w, writes=[w])
                        if not hb_loaded:
                            kb.dma("sp", hb[:, 0:nt, :], src[c0:c0 + N, :].rearrange("(t p) c -> p t c", p=128),
                                   hb, writes=[hb])
                            hb_loaded = True
                        for t in range(nt):
                            for kc in range(16):
                                kk = half * 16 + kc
                                kb.op("pe", lambda e, t=t, kc=kc, kk=kk: e.matmul(
                                    psA[t][:], big[:, kk, t * 128:(t + 1) * 128], w[:, kc, :],
                                    start=(kk == 0), stop=(kk == KC - 1)), reads=[big, w], writes=[psA[t]])
                    for t in range(nt):
                        kb.op("dve", lambda e, t=t: e.tensor_tensor(
                            out=hb[:, t, cg * 512:(cg + 1) * 512], in0=psA[t][:],
                            in1=hb[:, t, cg * 512:(cg + 1) * 512], op=ALU.add),
                            reads=[psA[t], hb], writes=[hb])
                for t in range(nt):
                    a = atm[(gi * 4 + t) % 2]
                    self.rmsnorm_tile((junk, ssq, rstd), hb, hb[:, t, :], g2, a, a[:], D)
                    self.transpose_blocks(a, lambda i: a[:, i * 128:(i + 1) * 128], 16, idb, pst, aT,
                                          lambda i0, n: aT[:, i0:i0 + n, t * 128:(t + 1) * 128], ctr)
                for fs in range(32):
                    w = w1s[n1 % 2]
                    n1 += 1
                    kb.dma("sp", w[:], W1[:, fs * 256:(fs + 1) * 256].rearrange("(k p) c -> p k c", p=128),
                           w, writes=[w])
                    for f2 in range(2):
                        f = fs * 2 + f2
                        pp = ps1[f % 2]
                        for kc in range(16):
                            kb.op("pe", lambda e, kc=kc, pp=pp, f2=f2: e.matmul(
                                pp[:, 0:N], w[:, kc, f2 * 128:(f2 + 1) * 128], aT[:, kc, 0:N],
                                start=(kc == 0), stop=(kc == 15)), reads=[w, aT], writes=[pp])
                        if f % 2 == 0:
                            kb.op("dve", lambda e, pp=pp, f=f: e.tensor_scalar(
                                out=rtmp[:, 0:N], in0=pp[:, 0:N], scalar1=0.0, scalar2=None,
                                op0=ALU.max), reads=[pp], writes=[rtmp])
                            kb.op("pool", lambda e, f=f: e.tensor_tensor(
                                out=big[:, f, 0:N], in0=rtmp[:, 0:N], in1=rtmp[:, 0:N], op=ALU.mult),
                                reads=[rtmp], writes=[big])
                        else:
                            kb.op("act", lambda e, pp=pp, f=f: e.activation(
                                out=junk[:, 0:N], in_=pp[:, 0:N], func=AF.Relu), reads=[pp], writes=[junk])
                            kb.op("pool", lambda e, f=f: e.tensor_tensor(
                                out=big[:, f, 0:N], in0=junk[:, 0:N], in1=junk[:, 0:N], op=ALU.mult),
                                reads=[junk], writes=[big])
                for cg in range(4):
                    for qk in range(4):
                        w = w2s[n2 % 2]
                        n2 += 1
                        kb.dma("sp", w[:], W2[qk * 2048:(qk + 1) * 2048, cg * 512:(cg + 1) * 512]
                               .rearrange("(k p) c -> p k c", p=128), w, writes=[w])
                        for t in range(nt):
                            for kc in range(16):
                                kk = qk * 16 + kc
                                kb.op("pe", lambda e, t=t, kc=kc, kk=kk: e.matmul(
                                    psA[t][:], big[:, kk, t * 128:(t + 1) * 128], w[:, kc, :],
                                    start=(kk == 0), stop=(kk == 63)), reads=[big, w], writes=[psA[t]])
                    for t in range(nt):
                        kb.op("dve", lambda e, t=t: e.tensor_tensor(
                            out=hb[:, t, cg * 512:(cg + 1) * 512], in0=psA[t][:],
                            in1=hb[:, t, cg * 512:(cg + 1) * 512], op=ALU.add),
                            reads=[psA[t], hb], writes=[hb])
                for t in range(nt):
                    tile = t0 + t
                    r0 = tile * 128
                    if tile >= NT2 and not last:
                        kb.op("dve", lambda e, t=t, tile=tile: e.tensor_scalar(
                            out=hb[:, t, :], in0=hb[:, t, :], scalar1=valid[:, tile:tile + 1], scalar2=None,
                            op0=ALU.mult), reads=[hb, valid], writes=[hb])
                    if not last:
                        kb.dma("act", S["h"][r0:r0 + 128, :], hb[:, t, :], hb, reads=[hb])
                    elif tile < NT2:
                        y = yo[0]
                        ny += 1
                        self.rmsnorm_tile((junk, ssq, rstd), hb, hb[:, t, :], gf, y, y[:], D)
                        kb.dma("act", self.out[r0:r0 + 128, :], y[:], y, reads=[y])

    def phase_M1(self, l, src):
        kb, I, S = self.kb, self.I, self.S
        j = l // 2
        Wqa, Wqb, Wkva, Wkvb = (self.W[n][j] for n in ("mla_wq_a", "mla_wq_b", "mla_wkv_a", "mla_wkv_b"))
        with kb.scope() as sc:
            idb = self.load_ident(sc)
            gbc = self.load_bcast(sc, "gbcm", I["norm1_g"][l, :], D)
            gq = self.load_bcast(sc, "gq", I["mla_q_norm"][j, :], MQL)
            gkv = self.load_bcast(sc, "gkv", I["mla_kv_norm"][j, :], MKL)
            wqa = self.load_const(sc, "wqa", Wqa.rearrange("(k p) c -> p k c", p=128), [128, 16, MQL], BF16)
            wkva = self.load_const(sc, "wkva", Wkva.rearrange("(k p) c -> p k c", p=128), [128, 16, MKL + MROPE], BF16)
            wqb = self.load_const(sc, "wqb", Wqb.rearrange("(k p) c -> p k c", p=128), [128, 4, MH * 192], BF16)
            wkvb = self.load_const(sc, "wkvb", Wkvb.rearrange("(k p) c -> p k c", p=128), [128, 4, MH * 256], BF16)
            xr = sc.ring("xrm", 2, [128, D], F32, dma=True)
            atm = sc.ring("atmm", 1, [128, D], BF16)
            junk = sc.sb("junkm", [128, D], F32)
            ssq = sc.sb("ssqm", [128, 1], F32)
            rstd = sc.sb("rstdm", [128, 1], F32)
            aT = sc.sb("aTm", [128, 16, 512], BF16)
            cqT = sc.sb("cqT", [128, 4, 512], BF16)
            ckvT = sc.sb("ckvT", [128, 4, 512], BF16)
            lat = sc.ring("lat", 1, [128, 512], F32)
            latn = sc.ring("latn", 2, [128, 512], BF16)
            csm = sc.ring("csm", 2, [128, 2, 256], F32, dma=True)
            krr = sc.ring("krr", 2, [128, 64], F32)
            krt = sc.ring("krt", 2, [128, 4, 32], F32)
            krb = sc.ring("krb", 2, [128, 128], BF16)
            krTs = sc.sb("krTs", [128, 512], BF16, dma=True)
            qrr = sc.ring("qrr", 2, [128, 512], F32)
            qrt = sc.ring("qrt", 1, [128, 4, 256], F32)
            qrb = sc.ring("qrb", 2, [128, 1024], BF16)
            qrTs = sc.ring("qrTs", 2, [128, 8, 128], BF16, dma=True)
            fo = sc.ring("fo", 2, [128, 512], BF16, dma=True)
            vo = sc.ring("vo", 1, [128, 2048], BF16, dma=True)
            pst = sc.psring("pstm", 2, [128, 1024], BF16)
            psB = sc.psring("psB", 2, [128, 512], F32)
            psC = sc.psring("psC", 2, [128, 512], F32)
            psD = sc.ps("psD", [128, 512], F32)
            ctr = [0]
            nx = nb = nc_ = nf = 0
            for gi, (t0, nt) in enumerate(self.groups):
                N = nt * 128
                c0 = t0 * 128
                for t in range(nt):
                    x, a = xr[nx % 2], atm[0]
                    r0 = (t0 + t) * 128
                    cm = csm[nx % 2]
                    nx += 1
                    kb.dma("sp", x[:], src[r0:r0 + 128, :], x, writes=[x])
                    kb.dma("sp", cm[:, 0, :], I["cosM"][r0:r0 + 128, :], cm, writes=[cm])
                    kb.dma("sp", cm[:, 1, :], I["sinM"][r0:r0 + 128, :], cm, writes=[cm])
                    self.rmsnorm_tile((junk, ssq, rstd), x, x[:], gbc, a, a[:], D)
                    self.transpose_blocks(a, lambda i: a[:, i * 128:(i + 1) * 128], 16, idb, pst, aT,
                                          lambda i0, n: aT[:, i0:i0 + n, t * 128:(t + 1) * 128], ctr)
                    for which in ("q", "kv"):
                        wa = wqa if which == "q" else wkva
                        pp = psB[nb % 2]
                        nb += 1
                        for kc in range(16):
                            kb.op("pe", lambda e, kc=kc, pp=pp, wa=wa: e.matmul(
                                pp[:], aT[:, kc, t * 128:(t + 1) * 128], wa[:, kc, 0:512],
                                start=(kc == 0), stop=(kc == 15)), reads=[aT, wa], writes=[pp])
                        la, ln_ = lat[0], latn[nb % 2]
                        kb.op("act", lambda e, pp=pp, la=la: e.copy(out=la[:], in_=pp[:]), reads=[pp], writes=[la])
                        self.rmsnorm_tile((junk, ssq, rstd), la, la[:], gq if which == "q" else gkv, ln_, ln_[:], 512)
                        dstT = cqT if which == "q" else ckvT
                        self.transpose_blocks(ln_, lambda i, ln_=ln_: ln_[:, i * 128:(i + 1) * 128], 4, idb, pst, dstT,
                                              lambda i0, n, dstT=dstT: dstT[:, i0:i0 + n, t * 128:(t + 1) * 128], ctr)
                    for kc in range(16):
                        kb.op("pe", lambda e, kc=kc: e.matmul(
                            psD[:, 0:64], aT[:, kc, t * 128:(t + 1) * 128], wkva[:, kc, 512:576],
                            start=(kc == 0), stop=(kc == 15)), reads=[aT, wkva], writes=[psD])
                    kr, kt, kbf = krr[nx % 2], krt[nx % 2], krb[nx % 2]
                    kb.op("act", lambda e: e.copy(out=kr[:], in_=psD[:, 0:64]), reads=[psD], writes=[kr])
                    cosv, sinv = cm[:, 0, 0:32], cm[:, 1, 0:32]
                    for ti, (xa, tb) in enumerate(((kr[:, 0:32], cosv), (kr[:, 32:64], sinv),
                                                   (kr[:, 0:32], sinv), (kr[:, 32:64], cosv))):
                        kb.op("dve", lambda e, ti=ti, xa=xa, tb=tb: e.tensor_tensor(
                            out=kt[:, ti, :], in0=xa, in1=tb, op=ALU.mult), reads=[kr, cm], writes=[kt])
                    for dup in range(2):
                        kb.op("dve", lambda e, dup=dup: e.tensor_tensor(
                            out=kbf[:, dup * 64:dup * 64 + 32], in0=kt[:, 0, :], in1=kt[:, 1, :], op=ALU.subtract),
                            reads=[kt], writes=[kbf])
                        kb.op("dve", lambda e, dup=dup: e.tensor_tensor(
                            out=kbf[:, dup * 64 + 32:dup * 64 + 64], in0=kt[:, 2, :], in1=kt[:, 3, :], op=ALU.add),
                            reads=[kt], writes=[kbf])
                    self.transpose_blocks(kbf, lambda i: kbf[:, :], 1, idb, pst, krTs,
                                          lambda i0, n: krTs[:, t * 128:(t + 1) * 128].rearrange("p (a b) -> p a b", a=1), ctr)
                    qb_ = qrb[nx % 2]
                    for hf in range(2):
                        pp = psC[nc_ % 2]
                        nc_ += 1
                        rhs_w = lambda kc: wqb[:, kc, hf * 8 * 192:(hf + 1) * 8 * 192].rearrange(
                            "p (h d) -> p h d", d=192)[:, :, 128:192]
                        for kc in range(4):
                            kb.op("pe", lambda e, kc=kc, pp=pp, rhs_w=rhs_w: e.matmul(
                                pp[:], cqT[:, kc, t * 128:(t + 1) * 128], rhs_w(kc),
                                start=(kc == 0), stop=(kc == 3)), reads=[cqT, wqb], writes=[pp])
                        qr, qt = qrr[nc_ % 2], qrt[0]
                        kb.op("act", lambda e, pp=pp, qr=qr: e.copy(out=qr[:], in_=pp[:]), reads=[pp], writes=[qr])
                        q3 = qr[:].rearrange("p (h d) -> p h d", d=64)
                        c3 = cm[:, 0, :].rearrange("p (h d) -> p h d", d=32)
                        s3 = cm[:, 1, :].rearrange("p (h d) -> p h d", d=32)
                        for ti, (xa, tb) in enumerate(((q3[:, :, 0:32], c3), (q3[:, :, 32:64], s3),
                                                       (q3[:, :, 0:32], s3), (q3[:, :, 32:64], c3))):
                            kb.op("dve", lambda e, ti=ti, xa=xa, tb=tb, qt=qt: e.tensor_tensor(
                                out=qt[:, ti, :].rearrange("p (h d) -> p h d", d=32), in0=xa, in1=tb, op=ALU.mult),
                                reads=[qr, cm], writes=[qt])
                        qo3 = qb_[:, hf * 512:(hf + 1) * 512].rearrange("p (h d) -> p h d", d=64)
                        kb.op("pool", lambda e, qt=qt, qo3=qo3: e.tensor_tensor(
                            out=qo3[:, :, 0:32], in0=qt[:, 0, :].rearrange("p (h d) -> p h d", d=32),
                            in1=qt[:, 1, :].rearrange("p (h d) -> p h d", d=32), op=ALU.subtract),
                            reads=[qt], writes=[qb_])
                        kb.op("pool", lambda e, qt=qt, qo3=qo3: e.tensor_tensor(
                            out=qo3[:, :, 32:64], in0=qt[:, 2, :].rearrange("p (h d) -> p h d", d=32),
                            in1=qt[:, 3, :].rearrange("p (h d) -> p h d", d=32), op=ALU.add),
                            reads=[qt], writes=[qb_])
                    qT_ = qrTs[nx % 2]
                    self.transpose_blocks(qb_, lambda i: qb_[:, i * 128:(i + 1) * 128], 8, idb, pst, qT_,
                                          lambda i0, n: qT_[:, i0:i0 + n, :], ctr)
                    kb.dma("act", S["qrT"][:, :, r0:r0 + 128].rearrange("k p t -> p k t"), qT_[:], qT_, reads=[qT_])
                    v_ = vo[0]
                    for q4 in range(4):
                        pp = psC[nc_ % 2]
                        nc_ += 1
                        rhs_w = lambda kc: wkvb[:, kc, q4 * 4 * 256:(q4 + 1) * 4 * 256].rearrange(
                            "p (h d) -> p h d", d=256)[:, :, 128:256]
                        for kc in range(4):
                            kb.op("pe", lambda e, kc=kc, pp=pp, rhs_w=rhs_w: e.matmul(
                                pp[:], ckvT[:, kc, t * 128:(t + 1) * 128], rhs_w(kc),
                                start=(kc == 0), stop=(kc == 3)), reads=[ckvT, wkvb], writes=[pp])
                        kb.op("act", lambda e, pp=pp, q4=q4: e.copy(out=v_[:, q4 * 512:(q4 + 1) * 512], in_=pp[:]),
                              reads=[pp], writes=[v_])
                    kb.dma("act", S["vm"][r0:r0 + 128, :], v_[:], v_, reads=[v_])
                kb.dma("act", S["krT"][:, c0:c0 + N], krTs[0:64, 0:N], krTs, reads=[krTs])
                for which in ("q", "k"):
                    for h in range(MH):
                        pp = psB[nb % 2]
                        nb += 1
                        if which == "q":
                            wsl = lambda kc: wqb[:, kc, h * 192:h * 192 + 128]
                            srcT, wb_, dst = cqT, wqb, S["qnT"]
                        else:
                            wsl = lambda kc: wkvb[:, kc, h * 256:h * 256 + 128]
                            srcT, wb_, dst = ckvT, wkvb, S["knT"]
                        for kc in range(4):
                            kb.op("pe", lambda e, kc=kc, pp=pp, wsl=wsl, srcT=srcT: e.matmul(
                                pp[:, 0:N], wsl(kc), srcT[:, kc, 0:N], start=(kc == 0), stop=(kc == 3)),
                                reads=[srcT, wb_], writes=[pp])
                        f_ = fo[nf % 2]
                        if nf % 2 == 0:
                            kb.op("act", lambda e, pp=pp, f_=f_: e.copy(out=f_[:, 0:N], in_=pp[:, 0:N]),
                                  reads=[pp], writes=[f_])
                        else:
                            kb.op("dve", lambda e, pp=pp, f_=f_: e.tensor_copy(out=f_[:, 0:N], in_=pp[:, 0:N]),
                                  reads=[pp], writes=[f_])
                        nf += 1
                        kb.dma("act", dst[h, :, c0:c0 + N], f_[:, 0:N], f_, reads=[f_])

    def phase_M2(self, l):
        kb, I, S = self.kb, self.I, self.S
        TT, TOK, NT = self.TT, self.TOK, self.NT
        scale = float((MNOPE + MROPE) ** -0.5)
        NP = TT // 2
        with kb.scope() as sc:
            kbias = self.load_const(sc, "kbiasc", I["kbias"][:, :], [128, TT * 2])
            ones = sc.sb("ones", [128, 128], F32)
            kb.op("pool", lambda e: e.memset(ones[:], 1.0), writes=[ones])
            accD = sc.ring("accD", 2, [128, 2, 512], F32)
            accP = sc.ring("accP", 2, [128, 2, 512], F32)
            krT = sc.sb("krT2", [128, TOK], BF16, dma=True)
            kb.dma("sp", krT[0:64, :], S["krT"][:, :], krT, writes=[krT])
            kb.dma("sp", krT[64:128, :], S["krT"][:, :], krT, writes=[krT])
            knT = sc.ring("knT", 2, [128, TOK], BF16, dma=True)
            vh = sc.ring("vh", 2, [128, TT, 128], BF16, dma=True)
            qn = sc.ring("qn", 2, [128, 512], BF16, dma=True)
            qrE = sc.ring("qrE", 2, [128, 512], BF16, dma=True)
            qrO = sc.ring("qrO", 2, [128, 512], BF16, dma=True)
            for b_ in qrE + qrO:
                kb.op("pool", lambda e, b_=b_: e.memset(b_[:], 0.0), writes=[b_])
            NPT = 4
            pT = sc.ring("pT", NPT, [128, 2, 512], BF16)
            rinv = sc.ring("rinv", 2, [128, 512], F32)
            oo = sc.ring("oo", 2, [128, 512], BF16, dma=True)
            NPS = 2
            psS = sc.psring("psSm", NPS, [128, 2, 512], F32)
            psO = sc.psring("psOm", 2, [128, 512], F32)
            psL = sc.psring("psLm", 2, [128, 512], F32)
            pend = []
            it = 0
            ng = 0
            for h in range(MH):
                kn, v_ = knT[h % 2], vh[h % 2]
                kb.dma("sp", kn[:], S["knT"][h, :, :], kn, writes=[kn])
                for tb in range(0, TT, 8):
                    te = min(TT, tb + 8)
                    kb.dma("sp", v_[:, tb:te, :],
                           S["vm"][tb * 128:te * 128, h * 128:(h + 1) * 128].rearrange("(t p) c -> p t c", p=128),
                           v_, writes=[v_])
                hp = (h % 2) * 64
                for (t0, nt, seg) in self.qgroups:
                    N = nt * 128
                    c0 = t0 * 128
                    qn_, qr_ = qn[ng % 2], (qrE if h % 2 == 0 else qrO)[ng % 2]
                    aD, aP = accD[ng % 2], accP[ng % 2]
                    po, pl = psO[ng % 2], psL[ng % 2]
                    kb.dma("sp", qn_[:, 0:N], S["qnT"][h, :, c0:c0 + N], qn_, writes=[qn_])
                    kb.dma("sp", qr_[hp:hp + 64, 0:N], S["qrT"][h // 2, hp:hp + 64, c0:c0 + N], qr_, writes=[qr_])

                    def score(j, slot):
                        ps = psS[slot % NPS]
                        for b2 in range(2):
                            kt = 2 * j + b2
                            kb.op("pe", lambda e: e.matmul(ps[:, b2, 0:N], kn[:, kt * 128:(kt + 1) * 128], qn_[:, 0:N],
                                                           start=True, stop=False), reads=[kn, qn_], writes=[ps])
                            kb.op("pe", lambda e: e.matmul(ps[:, b2, 0:N], krT[:, kt * 128:(kt + 1) * 128],
                                                           qr_[:, 0:N], start=False, stop=True),
                                  reads=[krT, qr_], writes=[ps])

                    score(0, it)
                    score(1, it + 1)
                    for j in range(NP):
                        ps, p_ = psS[it % NPS], pT[it % NPT]
                        kt0 = 2 * j
                        if kt0 + 1 < 2 * NT:
                            bcol = kt0 * 2 + seg
                            kb.op("act", lambda e: e.activation(
                                out=p_[:, :, 0:N], in_=ps[:, :, 0:N], func=AF.Exp,
                                bias=kbias[:, bcol:bcol + 1], scale=scale), reads=[ps, kbias], writes=[p_])
                        else:
                            for b2 in range(2):
                                bcol = (kt0 + b2) * 2 + seg
                                kb.op("act", lambda e: e.activation(
                                    out=p_[:, b2, 0:N], in_=ps[:, b2, 0:N], func=AF.Exp,
                                    bias=kbias[:, bcol:bcol + 1], scale=scale), reads=[ps, kbias], writes=[p_])
                        for b2 in range(2):
                            kt = kt0 + b2
                            kb.op("pe", lambda e: e.matmul(po[:, 0:N], v_[:, kt, :], p_[:, b2, 0:N],
                                                           start=(kt == 0), stop=(kt == TT - 1)),
                                  reads=[v_, p_], writes=[po])
                        if j + 2 < NP:
                            score(j + 2, it + 2)
                        aeng, acc = "dve", aD
                        if j < 1:
                            kb.op(aeng, lambda e: e.tensor_copy(out=acc[:, :, 0:N], in_=p_[:, :, 0:N]),
                                  reads=[p_], writes=[acc])
                        else:
                            kb.op(aeng, lambda e: e.tensor_tensor(
                                out=acc[:, :, 0:N], in0=acc[:, :, 0:N], in1=p_[:, :, 0:N], op=ALU.add),
                                reads=[p_, acc], writes=[acc])
                        it += 1
                        if j == 0 and pend:
                            pend.pop(0)()
                    kb.op("dve", lambda e: e.tensor_tensor(out=aD[:, 0, 0:N], in0=aD[:, 0, 0:N],
                                                           in1=aD[:, 1, 0:N], op=ALU.add),
                          reads=[aD], writes=[aD])

                    def epi(N=N, aD=aD, po=po, pl=pl, ri=rinv[ng % 2], o_=oo[ng % 2], h=h, c0=c0):
                        kb.op("pe", lambda e: e.matmul(pl[:, 0:N], ones[:], aD[:, 0, 0:N], start=True, stop=True),
                              reads=[ones, aD], writes=[pl])
                        kb.op("dve", lambda e: e.reciprocal(out=ri[:, 0:N], in_=pl[:, 0:N]), reads=[pl], writes=[ri])
                        kb.op("dve", lambda e: e.tensor_tensor(out=o_[:, 0:N], in0=po[:, 0:N], in1=ri[:, 0:N],
                                                               op=ALU.mult), reads=[po, ri], writes=[o_])
                        kb.dma("pool", S["ogT"][h, :, c0:c0 + N], o_[:, 0:N], o_, reads=[o_])

                    pend.append(epi)
                    ng += 1
            while pend:
                pend.pop(0)()


def _tables(NT, kind):
    TT = 2 * NT + 2
    TOK = TT * 128
    S = NT * 128
    pos = np.zeros(TOK, np.float64)
    valid = np.zeros(TOK, np.float64)
    for seg in range(2):
        base = seg * S
        off = NMETA + (base if kind == "prompt" else 0)
        pos[base:base + S] = off + np.arange(S)
        valid[base:base + S] = 1
    for seg in range(2):
        r = (2 * NT + seg) * 128 + 112
        if seg == 0 or kind == "sample":
            pos[r:r + 16] = np.arange(16)
            valid[r:r + 16] = 1
    f32 = np.float32
    inv_r = 1.0 / (10000.0 ** (np.arange(0, RDK, 2, dtype=np.float64) / RDK))
    ang = (pos[None, :] * inv_r[:, None])
    T = dict(cosRT=np.cos(ang).astype(f32), sinRT=np.sin(ang).astype(f32))
    inv_m = 1.0 / (10000.0 ** (np.arange(0, MROPE, 2, dtype=np.float64) / MROPE))
    angm = pos[:, None] * inv_m[None, :]
    T["cosM"] = np.tile(np.cos(angm), (1, 8)).astype(f32)
    T["sinM"] = np.tile(np.sin(angm), (1, 8)).astype(f32)
    T["ident"] = np.eye(128, dtype=f32)
    hh = np.arange(RH, dtype=np.float64)
    lgf = np.log(1.0 - 2.0 ** (-5.0 - hh))
    lgb = np.log(1.0 - 2.0 ** (-5.5 - hh))
    i = np.arange(128, dtype=np.float64)
    diff = i[:, None] - i[None, :]
    kscale = RDK ** -0.5
    DT = np.zeros((128, RH, 128))
    qdf = np.zeros((128, RH, 2, 128))
    qdb = np.zeros((128, RH, 2, 128))
    for h in range(RH):
        Df = np.where(diff >= 0, np.exp(lgf[h] * np.abs(diff)), 0.0)
        Db = np.where(diff < 0, np.exp(lgb[h] * np.abs(diff)), 0.0)
        DT[:, h, :] = (Df + Db).T * kscale
        qdf[:, h, :, :] = np.exp(lgf[h] * (i + 1.0))[None, None, :]
        qdb[:, h, :, :] = np.exp(lgb[h] * (128 - i))[None, None, :]
    T["DT"] = DT.reshape(128, RH * 128).astype(f32)
    T["qdf"] = qdf.reshape(128, RH * 256).astype(f32)
    T["qdb"] = qdb.reshape(128, RH * 256).astype(f32)
    kdF = np.zeros((128, TT, RH)); kdB = np.zeros((128, TT, RH))
    cdf = np.zeros((128, TT, RH)); cdb = np.zeros((128, TT, RH))
    for h in range(RH):
        kdF[:, :, h] = (np.exp(lgf[h] * (127.0 - i)) * kscale)[:, None]
        kdB[:, :, h] = (np.exp(lgb[h] * i) * kscale)[:, None]
        cdf[:, :, h] = np.exp(lgf[h] * 128)
        cdb[:, :, h] = np.exp(lgb[h] * 128)
    m1 = 2 * NT + 1
    if kind == "prompt":
        kdF[:, m1, :] = 0; kdB[:, m1, :] = 0; cdf[:, m1, :] = 1; cdb[:, m1, :] = 1
    else:
        kdF[:, NT - 1, :] = 0; cdf[:, NT - 1, :] = 0
        kdB[:, m1, :] = 0; cdb[:, m1, :] = 0
    for n, a in (("kdF", kdF), ("kdB", kdB), ("cdf", cdf), ("cdb", cdb)):
        T[n] = a.reshape(128, TT * RH).astype(f32)
    kbias = np.zeros((128, TT, 2))
    v2 = valid.reshape(TT, 128)
    for kt in range(TT):
        sk = 0 if kt < NT else (1 if kt < 2 * NT else kt - 2 * NT)
        for sq in range(2):
            b = np.where(v2[kt] > 0, 0.0, NEG)
            if kind == "sample" and sk != sq:
                b = np.full(128, NEG)
            kbias[:, kt, sq] = b
    T["kbias"] = kbias.reshape(128, TT * 2).astype(f32)
    T["valid"] = np.ascontiguousarray(v2.T).astype(f32)
    return T


def _core_x(NT, kind, seqs, meta):
    TT = 2 * NT + 2
    S = NT * 128
    x = np.zeros((TT * 128, D), np.float32)
    if kind == "prompt":
        x[0:2 * S] = seqs[0]
    else:
        x[0:S] = seqs[0]
        x[S:2 * S] = seqs[1]
    r = 2 * NT * 128 + 112
    x[r:r + 16] = meta
    if kind == "sample":
        r = (2 * NT + 1) * 128 + 112
        x[r:r + 16] = meta
    return x


_CACHE = {}


def run_cores(NT, depth, core_specs, weights, debug=None, trace=False):
    key = (NT, depth, tuple(sorted(debug)) if debug else None)
    if key not in _CACHE:
        _CACHE[key] = Prog(NT, depth, debug=debug).build()
    nc = _CACHE[key]
    NR, NM = (depth + 1) // 2, depth // 2
    common = {}
    f = lambda a: np.ascontiguousarray(np.asarray(a, dtype=np.float32))
    common["norm1_g"] = f(weights["norm1_g"][:depth])
    common["norm2_g"] = f(weights["norm2_g"][:depth])
    common["final_norm"] = f(weights["final_norm"]).reshape(1, D)
    common["mlp_w1"] = f(weights["mlp_w1"][:depth])
    common["mlp_w2"] = f(weights["mlp_w2"][:depth])
    for n in ("ret_wq", "ret_wk", "ret_wv", "ret_wg", "ret_wo"):
        common[n] = f(weights[n][:NR])
    if NM:
        for n in ("mla_wq_a", "mla_wq_b", "mla_wkv_a", "mla_wkv_b", "mla_wo", "mla_q_norm", "mla_kv_norm"):
            common[n] = f(weights[n][:NM])
    tabs = {k: _tables(NT, k) for k in set(s[0] for s in core_specs)}
    meta = f(weights["meta_tokens"])
    in_maps = []
    for kind, seqs in core_specs:
        m = dict(common)
        m.update(tabs[kind])
        m["x_in"] = _core_x(NT, kind, seqs, meta)
        in_maps.append(m)
    res = run_bass_kernel_spmd(nc, in_maps, core_ids=list(range(len(core_specs))), trace=trace)
    return res


def kernel(**inputs):
    NT = 32
    xp = np.asarray(inputs["x_prompt"], dtype=np.float32)
    xs = np.asarray(inputs["x_sample"], dtype=np.float32)
    specs = [("prompt", [xp[b]]) for b in range(4)] + [("sample", [xs[2 * c], xs[2 * c + 1]]) for c in range(4)]
    res = run_cores(NT, 4, specs, inputs)
    outs = [np.asarray(r["out"]) for r in res.results]
    yp = np.stack([outs[b].reshape(8192, D) for b in range(4)], axis=0).astype(np.float32)
    ys = np.concatenate([outs[4 + c].reshape(2, 4096, D) for c in range(4)], axis=0).astype(np.float32)
    return (yp, ys)
```
